# Optimizing a Trainium2 kernel written in Bass

```python
import math
import jax, jax.numpy as jnp
from jax import lax
import numpy as np

D_MODEL = 1024
BATCH = 8
SEQ = 4096
DEPTH = 2

GRID_W = 64
CTX_LEN = 256
NORM_EPS = 1e-6
N_MOD = 9

D_FF = 2816
MACARON_W = 0.5

D_LRU = 256
LRU_BLOCKS = 4
LRU_BS = D_LRU // LRU_BLOCKS
LRU_CONV = 4
LRU_PAD = (2, 1)
LRU_C = 8.0

D_HY = 256
HY_CONV = 3
HY_PAD = (1, 1)
HY_EMB = 33
HY_BANDS = (HY_EMB - 1) // 2
HY_HID = 64
HY_INNER = 2
HY_FAST = 0.3
HY_SLOW = 1.5
HY_TARGET = 1e-2

N_QH = 8
N_KVH = 2
HEAD_DIM = 64
Q_PER_KV = N_QH // N_KVH
D_ATTN = N_QH * HEAD_DIM
WINDOW = 128
BLOCK = 128
ROPE_THETA = 10000.0
ROPE_PAIRS_AXIS = HEAD_DIM // 4
NEG_INF = -1e30

D_IN = 2 * D_LRU + 3 * D_HY + (N_QH + 2 * N_KVH) * HEAD_DIM
D_CAT = D_LRU + D_HY + D_ATTN
SPLITS = np.cumsum([D_LRU, D_LRU, 3 * D_HY, N_QH * HEAD_DIM, N_KVH * HEAD_DIM]).tolist()

kernel_name = "hybrid_lru_hyena_swa_dit_block"

F32 = jnp.float32


def rmsnorm(x, g):
    xf = x.astype(F32)
    y = xf * lax.rsqrt(jnp.mean(xf * xf, axis=-1, keepdims=True) + NORM_EPS)
    return (y * g.astype(F32)).astype(x.dtype)


def ada_norm(x, g, shift, scale):
    return rmsnorm(x, g) * (1 + scale) + shift


def swiglu(h, w1, w2):
    a, b = jnp.split(h @ w1, 2, axis=-1)
    return (jax.nn.silu(a) * b) @ w2


def dwconv(x, w, b, pad):
    L = x.shape[1]
    xp = jnp.pad(x, ((0, 0), pad, (0, 0)))
    out = b
    for k in range(w.shape[0]):
        out = out + xp[:, k:k + L] * w[k]
    return out


def rglru_coeffs(u, wa, ba, wx, bx, lam):
    ub = u.reshape(u.shape[:-1] + (LRU_BLOCKS, LRU_BS))
    r = jax.nn.sigmoid(jnp.einsum('blnd,nde->blne', ub, wa.astype(F32)).reshape(u.shape) + ba.astype(F32))
    i = jax.nn.sigmoid(jnp.einsum('blnd,nde->blne', ub, wx.astype(F32)).reshape(u.shape) + bx.astype(F32))
    log_a = -LRU_C * r * jax.nn.softplus(-lam.astype(F32))
    a = jnp.exp(log_a)
    b = jnp.sqrt(-jnp.expm1(2.0 * log_a)) * (i * u)
    return a, b


def _combine(e1, e2):
    a1, b1 = e1
    a2, b2 = e2
    return a1 * a2, a2 * b1 + b2


def linear_scan(a, b, h0):
    a_cum, b_cum = lax.associative_scan(_combine, (a, b), axis=1)
    return a_cum * h0[:, None] + b_cum


def rglru_mixer(xl, gl, xlc, glc, conv_w, conv_b, wa, ba, wx, bx, lam, need_ctx):
    u = dwconv(xl, conv_w, conv_b, LRU_PAD).astype(F32)
    uc = dwconv(xlc, conv_w, conv_b, LRU_PAD).astype(F32)
    hs, hcs = [], []
    for d in range(2):
        ac, bc = rglru_coeffs(uc, wa[d], ba[d], wx[d], bx[d], lam[d])
        ax, bxl = rglru_coeffs(u, wa[d], ba[d], wx[d], bx[d], lam[d])
        if d == 1:
            ac, bc, ax, bxl = (jnp.flip(ac, 1), jnp.flip(bc, 1), jnp.flip(ax, 1), jnp.flip(bxl, 1))
        hc = linear_scan(ac, bc, jnp.zeros_like(ac[:, 0]))
        hx = linear_scan(ax, bxl, hc[:, -1])
        if d == 1:
            hc, hx = jnp.flip(hc, 1), jnp.flip(hx, 1)
        hs.append(hx)
        hcs.append(hc)
    y = ((hs[0] + hs[1]) * jax.nn.gelu(gl.astype(F32))).astype(xl.dtype)
    if not need_ctx:
        return y, None
    yc = ((hcs[0] + hcs[1]) * jax.nn.gelu(glc.astype(F32))).astype(xlc.dtype)
    return y, yc


def hyena_filter(L, fw0, fb0, fw_in, fb_in, freq, fw_last):
    t = jnp.linspace(0.0, 1.0, L, dtype=F32)[:, None]
    w = 2.0 * math.pi * jnp.arange(L, dtype=F32)[:, None] / L
    f = jnp.linspace(1e-4, HY_BANDS - 1, HY_BANDS, dtype=F32)[None, :]
    z = jnp.concatenate([t, jnp.cos(f * w), -jnp.sin(f * w)], axis=-1)
    fr = freq.astype(F32)
    hdn = jnp.sin(fr * (z @ fw0.astype(F32) + fb0.astype(F32)))
    for j in range(HY_INNER):
        hdn = jnp.sin(fr * (hdn @ fw_in[j].astype(F32) + fb_in[j].astype(F32)))
    k = hdn @ fw_last.astype(F32)
    max_decay = math.log(HY_TARGET) / HY_FAST
    min_decay = math.log(HY_TARGET) / HY_SLOW
    deltas = jnp.abs(jnp.linspace(min_decay, max_decay, D_HY, dtype=F32))
    decay = jnp.exp(-t * deltas)
    k_fwd = k[:, :D_HY] * decay
    k_bwd = k[:, D_HY:] * decay
    return jnp.concatenate([k_fwd, jnp.zeros((1, D_HY), F32), jnp.flip(k_bwd[1:], axis=0)], axis=0)


def hyena_op(z, conv_w, conv_b, fw0, fb0, fw_in, fb_in, freq, fw_last, skip):
    L = z.shape[1]
    zc = dwconv(z, conv_w, conv_b, HY_PAD).astype(F32)
    x0, x1, v = jnp.split(zc, 3, axis=-1)
    k = hyena_filter(L, fw0, fb0, fw_in, fb_in, freq, fw_last)
    u = x1 * v
    n = 2 * L
    y = jnp.fft.irfft(jnp.fft.rfft(u, n=n, axis=1) * jnp.fft.rfft(k, n=n, axis=0)[None], n=n, axis=1)[:, :L]
    y = y + u * skip.astype(F32)
    return (x0 * y).astype(z.dtype)


def rope_tables(rows):
    r = jnp.repeat(jnp.arange(rows, dtype=F32), GRID_W)
    col = jnp.tile(jnp.arange(GRID_W, dtype=F32), rows)
    inv = ROPE_THETA ** (-jnp.arange(ROPE_PAIRS_AXIS, dtype=F32) / ROPE_PAIRS_AXIS)
    ang = jnp.concatenate([r[:, None] * inv, col[:, None] * inv], axis=-1)
    return jnp.cos(ang), jnp.sin(ang)


def apply_rope(x, cos, sin):
    xf = x.astype(F32)
    x1, x2 = jnp.split(xf, 2, axis=-1)
    cc = cos[None, :, None]
    ss = sin[None, :, None]
    return jnp.concatenate([x1 * cc - x2 * ss, x1 * ss + x2 * cc], axis=-1).astype(x.dtype)


def sink_attend(s, vals, sink):
    sk = sink.astype(F32)[None, :, :, None, None]
    m = jnp.maximum(jnp.max(s, axis=-1, keepdims=True), sk)
    p = jnp.exp(s - m)
    denom = jnp.sum(p, axis=-1, keepdims=True) + jnp.exp(sk - m)
    return jnp.einsum('bhgqk,bkhd->bqhgd', p / denom, vals.astype(F32))


def window_attention(q, k, v, kc, vc, sink):
    B, L = q.shape[0], q.shape[1]
    nb = L // BLOCK
    span = BLOCK + 2 * WINDOW
    scale = HEAD_DIM ** -0.5
    qb = q.reshape(B, nb, BLOCK, N_KVH, Q_PER_KV, HEAD_DIM).swapaxes(0, 1)
    kp = jnp.pad(k, ((0, 0), (WINDOW, WINDOW), (0, 0), (0, 0)))
    vp = jnp.pad(v, ((0, 0), (WINDOW, WINDOW), (0, 0), (0, 0)))
    sink_g = sink.reshape(N_KVH, Q_PER_KV)
    n_ctx = kc.shape[1]

    def block_fn(args):
        qi, bi = args
        start = bi * BLOCK
        kw = lax.dynamic_slice_in_dim(kp, start, span, axis=1)
        vw = lax.dynamic_slice_in_dim(vp, start, span, axis=1)
        keys = jnp.concatenate([kw, kc], axis=1)
        vals = jnp.concatenate([vw, vc], axis=1)
        s = jnp.einsum('bqhgd,bkhd->bhgqk', qi, keys).astype(F32) * scale
        qpos = start + jnp.arange(BLOCK)
        kpos = start - WINDOW + jnp.arange(span)
        valid = (jnp.abs(qpos[:, None] - kpos[None, :]) <= WINDOW) & (kpos >= 0)[None] & (kpos < L)[None]
        valid = jnp.concatenate([valid, jnp.ones((BLOCK, n_ctx), dtype=bool)], axis=1)
        s = jnp.where(valid, s, NEG_INF)
        return sink_attend(s, vals, sink_g)

    o = lax.map(block_fn, (qb, jnp.arange(nb)))
    return o.swapaxes(0, 1).reshape(B, L, D_ATTN)


def context_attention(qc, kc, vc, sink):
    B, C = qc.shape[0], qc.shape[1]
    qg = qc.reshape(B, C, N_KVH, Q_PER_KV, HEAD_DIM)
    s = jnp.einsum('bqhgd,bkhd->bhgqk', qg, kc).astype(F32) * (HEAD_DIM ** -0.5)
    return sink_attend(s, vc, sink.reshape(N_KVH, Q_PER_KV)).reshape(B, C, D_ATTN)


def token_mixer(h, hc, cos, sin, w_in, w_out, lru_conv_w, lru_conv_b, lru_wa, lru_ba, lru_wx, lru_bx,
                lru_lam, hy_conv_w, hy_conv_b, hy_fw0, hy_fb0, hy_fw_in, hy_fb_in, hy_freq, hy_fw_last,
                hy_skip, attn_sink, need_ctx):
    B, L = h.shape[0], h.shape[1]
    C = hc.shape[1]
    xl, gl, zh, q, k, v = jnp.split(h @ w_in, SPLITS, axis=-1)
    xlc, glc, zhc, qc, kc, vc = jnp.split(hc @ w_in, SPLITS, axis=-1)

    y_lru, yc_lru = rglru_mixer(xl, gl, xlc, glc, lru_conv_w, lru_conv_b, lru_wa, lru_ba,
                                lru_wx, lru_bx, lru_lam, need_ctx)
    y_hy = hyena_op(zh, hy_conv_w, hy_conv_b, hy_fw0, hy_fb0, hy_fw_in, hy_fb_in, hy_freq, hy_fw_last, hy_skip)

    q = apply_rope(q.reshape(B, L, N_QH, HEAD_DIM), cos, sin)
    k = apply_rope(k.reshape(B, L, N_KVH, HEAD_DIM), cos, sin)
    v = v.reshape(B, L, N_KVH, HEAD_DIM)
    kc = kc.reshape(B, C, N_KVH, HEAD_DIM)
    vc = vc.reshape(B, C, N_KVH, HEAD_DIM)
    y_att = window_attention(q, k, v, kc, vc, attn_sink).astype(h.dtype)

    y = jnp.concatenate([y_lru, y_hy, y_att], axis=-1) @ w_out
    if not need_ctx:
        return y, None
    yc_hy = hyena_op(zhc, hy_conv_w, hy_conv_b, hy_fw0, hy_fb0, hy_fw_in, hy_fb_in, hy_freq, hy_fw_last, hy_skip)
    yc_att = context_attention(qc.reshape(B, C, N_QH, HEAD_DIM), kc, vc, attn_sink).astype(hc.dtype)
    yc = jnp.concatenate([yc_lru, yc_hy, yc_att], axis=-1) @ w_out
    return y, yc


def setup_inputs(seed: int = 0) -> dict:
    key = jax.random.key(seed)
    ks = jax.random.split(key, 32)
    D = D_MODEL

    def nrm(k, shape, s):
        return jax.random.normal(k, shape, F32) * s

    u = jax.random.uniform(ks[17], (DEPTH, 2, D_LRU), F32, 0.9, 0.999)
    a = u ** (1.0 / LRU_C)
    lam = jnp.log(a) - jnp.log1p(-a)
    return {
        "x": nrm(ks[0], (BATCH, SEQ, D), 1.0),
        "c": nrm(ks[1], (BATCH, D), 1.0),
        "ctx": nrm(ks[2], (BATCH, CTX_LEN, D), 1.0),
        "c_ctx": nrm(ks[3], (D,), 1.0),
        "w_mod": nrm(ks[4], (DEPTH, D, N_MOD * D), 0.5 * D ** -0.5),
        "b_mod": nrm(ks[5], (DEPTH, N_MOD * D), 0.02),
        "norm_g": 1.0 + nrm(ks[6], (DEPTH, 3, D), 0.05),
        "ffn_w1": nrm(ks[7], (DEPTH, 2, D, 2 * D_FF), D ** -0.5),
        "ffn_w2": nrm(ks[8], (DEPTH, 2, D_FF, D), D_FF ** -0.5),
        "w_in": nrm(ks[9], (DEPTH, D, D_IN), D ** -0.5),
        "w_out": nrm(ks[10], (DEPTH, D_CAT, D), D_CAT ** -0.5),
        "lru_conv_w": nrm(ks[11], (DEPTH, LRU_CONV, D_LRU), LRU_CONV ** -0.5),
        "lru_conv_b": nrm(ks[12], (DEPTH, D_LRU), 0.02),
        "lru_wa": nrm(ks[13], (DEPTH, 2, LRU_BLOCKS, LRU_BS, LRU_BS), LRU_BS ** -0.5),
        "lru_ba": nrm(ks[14], (DEPTH, 2, D_LRU), 0.02),
        "lru_wx": nrm(ks[15], (DEPTH, 2, LRU_BLOCKS, LRU_BS, LRU_BS), LRU_BS ** -0.5),
        "lru_bx": nrm(ks[16], (DEPTH, 2, D_LRU), 0.02),
        "lru_lam": lam,
        "hy_conv_w": nrm(ks[18], (DEPTH, HY_CONV, 3 * D_HY), HY_CONV ** -0.5),
        "hy_conv_b": nrm(ks[19], (DEPTH, 3 * D_HY), 0.02),
        "hy_fw0": nrm(ks[20], (DEPTH, HY_EMB, HY_HID), HY_EMB ** -0.5),
        "hy_fb0": nrm(ks[21], (DEPTH, HY_HID), 0.1),
        "hy_fw_in": nrm(ks[22], (DEPTH, HY_INNER, HY_HID, HY_HID), HY_HID ** -0.5),
        "hy_fb_in": nrm(ks[23], (DEPTH, HY_INNER, HY_HID), 0.1),
        "hy_freq": 1.0 + nrm(ks[24], (DEPTH, HY_HID), 0.05),
        "hy_fw_last": nrm(ks[25], (DEPTH, HY_HID, 2 * D_HY), 0.1 * HY_HID ** -0.5),
        "hy_skip": nrm(ks[26], (DEPTH, D_HY), 0.5),
        "attn_sink": nrm(ks[27], (DEPTH, N_QH), 0.5),
        "final_g": 1.0 + nrm(ks[28], (D,), 0.05),
    }


def reference(x, c, ctx, c_ctx, w_mod, b_mod, norm_g, ffn_w1, ffn_w2, w_in, w_out, lru_conv_w, lru_conv_b,
              lru_wa, lru_ba, lru_wx, lru_bx, lru_lam, hy_conv_w, hy_conv_b, hy_fw0, hy_fb0, hy_fw_in,
              hy_fb_in, hy_freq, hy_fw_last, hy_skip, attn_sink, final_g):
    B = x.shape[0]
    ROWS = x.shape[1] // GRID_W
    cos, sin = rope_tables(ROWS)
    s_lat = jax.nn.silu(c)
    s_ctx = jax.nn.silu(c_ctx)
    xc = ctx
    for l in range(DEPTH):
        need_ctx = l < DEPTH - 1
        mod = (s_lat @ w_mod[l] + b_mod[l]).reshape(B, N_MOD, 1, D_MODEL)
        modc = (s_ctx @ w_mod[l] + b_mod[l]).reshape(N_MOD, D_MODEL)
        m = [mod[:, i] for i in range(N_MOD)]
        mc = [modc[i] for i in range(N_MOD)]

        x = x + MACARON_W * m[2] * swiglu(ada_norm(x, norm_g[l, 0], m[0], m[1]), ffn_w1[l, 0], ffn_w2[l, 0])
        xc = xc + MACARON_W * mc[2] * swiglu(ada_norm(xc, norm_g[l, 0], mc[0], mc[1]), ffn_w1[l, 0], ffn_w2[l, 0])

        h = ada_norm(x, norm_g[l, 1], m[3], m[4])
        hc = ada_norm(xc, norm_g[l, 1], mc[3], mc[4])
        y, yc = token_mixer(h, hc, cos, sin, w_in[l], w_out[l], lru_conv_w[l], lru_conv_b[l], lru_wa[l],
                            lru_ba[l], lru_wx[l], lru_bx[l], lru_lam[l], hy_conv_w[l], hy_conv_b[l], hy_fw0[l],
                            hy_fb0[l], hy_fw_in[l], hy_fb_in[l], hy_freq[l], hy_fw_last[l], hy_skip[l],
                            attn_sink[l], need_ctx)
        x = x + m[5] * y

        x = x + MACARON_W * m[8] * swiglu(ada_norm(x, norm_g[l, 2], m[6], m[7]), ffn_w1[l, 1], ffn_w2[l, 1])
        if need_ctx:
            xc = xc + mc[5] * yc
            xc = xc + MACARON_W * mc[8] * swiglu(ada_norm(xc, norm_g[l, 2], mc[6], mc[7]), ffn_w1[l, 1], ffn_w2[l, 1])
    return rmsnorm(x, final_g)
```

```python
import math
import numpy as np
from contextlib import ExitStack
import concourse.bass as bass
import concourse.mybir as mybir
from concourse.bass_utils import run_bass_kernel_spmd

F32 = mybir.dt.float32
BF16 = mybir.dt.bfloat16
AF = mybir.ActivationFunctionType
ALU = mybir.AluOpType
AX = mybir.AxisListType

D = 1024
L = 4096
C = 256
T = L + C
NT = T // 128
DFF = 2816
NF = DFF // 128
DEPTH = 2
NMOD = 9
DIN = 2048
EPS = 1e-6
NFFT = 8192


class Res:
    __slots__ = ("w", "r")

    def __init__(self):
        self.w = None
        self.r = {}


class Lane:
    __slots__ = ("sem", "n")

    def __init__(self, sem):
        self.sem = sem
        self.n = 0


class Sched:
    def __init__(self, nc, es, lanes_sp=12, lanes_pool=6, lanes_act=2):
        self.nc = nc
        self.eng = {"pe": nc.tensor, "act": nc.scalar, "dve": nc.vector,
                    "pool": nc.gpsimd, "sp": nc.sync}
        self.sem = {}
        self.cnt = {}
        for k in ("pe", "act", "dve", "pool"):
            self.sem[k] = es.enter_context(nc.semaphore("s_" + k))
            self.cnt[k] = 0
        self.lanes = {}
        self.lane_rr = {}
        for q, n in (("sp", lanes_sp), ("pool", lanes_pool), ("act", lanes_act)):
            self.lanes[q] = [Lane(es.enter_context(nc.semaphore("d_%s%d" % (q, i)))) for i in range(n)]
            self.lane_rr[q] = 0
        self.seen = {k: {} for k in self.eng}
        self.nwaits = 0
        self.nins = 0

    def _wait(self, E, deps):
        need = {}
        for tok in deps:
            kind, key, val = tok
            if kind == "e" and key == E:
                if E == "pe":
                    continue
                if self.cnt[E] - val >= 2:
                    continue
            k = (kind, key)
            if need.get(k, 0) < val:
                need[k] = val
        seen = self.seen[E]
        for k, val in need.items():
            if seen.get(k, 0) >= val:
                continue
            seen[k] = val
            sem = self.sem[k[1]] if k[0] == "e" else k[1].sem
            self.eng[E].wait_ge(sem, val)
            self.nwaits += 1

    @staticmethod
    def _deps(reads, writes):
        deps = set()
        for r in reads:
            if r.w is not None:
                deps.add(r.w)
        for w in writes:
            if w.w is not None:
                deps.add(w.w)
            deps.update(w.r.values())
        return deps

    @staticmethod
    def _commit(tok, skey, reads, writes):
        for r in reads:
            r.r[skey] = tok
        for w in writes:
            w.w = tok
            w.r = {}

    def op(self, E, fn, reads=(), writes=()):
        self._wait(E, self._deps(reads, writes))
        ins = fn(self.eng[E])
        self.cnt[E] += 1
        ins.then_inc(self.sem[E], 1)
        self._commit(("e", E, self.cnt[E]), E, reads, writes)
        self.nins += 1
        return ins

    def dma(self, out, in_, reads=(), writes=(), q="sp", **kw):
        lanes = self.lanes[q]
        lane = lanes[self.lane_rr[q] % len(lanes)]
        self.lane_rr[q] += 1
        deps = self._deps(reads, writes)
        if lane.n:
            deps.add(("d", lane, 16 * lane.n))
        self._wait(q, deps)
        ins = self.eng[q].dma_start(out=out, in_=in_, **kw)
        ins.then_inc(lane.sem, 16)
        lane.n += 1
        self._commit(("d", lane, 16 * lane.n), lane, reads, writes)
        self.nins += 1
        return ins

    def barrier(self):
        deps = set()
        for k in self.cnt:
            if self.cnt[k]:
                deps.add(("e", k, self.cnt[k]))
        for q in self.lanes:
            for ln in self.lanes[q]:
                if ln.n:
                    deps.add(("d", ln, 16 * ln.n))
        for E in ("pe", "act", "dve", "pool", "sp"):
            self._wait(E, deps)

    def finish(self):
        deps = set()
        for k in self.cnt:
            if self.cnt[k]:
                deps.add(("e", k, self.cnt[k]))
        for q in self.lanes:
            for ln in self.lanes[q]:
                if ln.n:
                    deps.add(("d", ln, 16 * ln.n))
        self._wait("sp", deps)


class Tl:
    def __init__(self, t, nslots=1):
        self.t = t
        self.rs = [Res() for _ in range(nslots)]

    @property
    def r(self):
        return self.rs[0]

    def __getitem__(self, k):
        return self.t[k]


class Dr:
    def __init__(self, h):
        self.h = h
        self.res = {}

    def r(self, key=0):
        if key not in self.res:
            self.res[key] = Res()
        return self.res[key]

    def __getitem__(self, k):
        return self.h[k]


def _consts():
    cst = {}
    cst["ident"] = np.eye(128, dtype=np.float32)
    inv = 10000.0 ** (-np.arange(16, dtype=np.float64) / 16)
    t = np.arange(L)
    ang = np.concatenate([(t // 64)[:, None] * inv, (t % 64)[:, None] * inv], axis=1)
    cosf = np.ones((128, T), np.float64)
    sinf = np.zeros((128, T), np.float64)
    for p in range(128):
        pp = (p % 64) % 32
        cosf[p, :L] = np.cos(ang[:, pp])
        sinf[p, :L] = np.sin(ang[:, pp])
    cst["ropeC"] = cosf.astype(np.float32)
    cst["ropeS"] = sinf.astype(np.float32)
    j = np.arange(128)[:, None]
    q = np.arange(128)[None, :]
    cst["mprev"] = np.tile((q <= j).astype(np.float32), (1, 4))
    cst["mnext"] = np.tile((j <= q).astype(np.float32), (1, 4))

    def zfeat(Lh):
        tt = np.linspace(0.0, 1.0, Lh)[:, None]
        w = 2.0 * math.pi * np.arange(Lh)[:, None] / Lh
        f = np.linspace(1e-4, 15, 16)[None, :]
        z = np.concatenate([tt, np.cos(f * w), -np.sin(f * w)], axis=-1)
        max_decay = math.log(1e-2) / 0.3
        min_decay = math.log(1e-2) / 1.5
        deltas = np.abs(np.linspace(min_decay, max_decay, 256))
        dec = np.exp(-tt * deltas[None, :])
        return z, dec
    z, dec = zfeat(L)
    zf = np.zeros((NFFT, 33)); df = np.zeros((NFFT, 256))
    zf[:L] = z; df[:L] = dec
    zf[L + 1:] = z[1:][::-1]; df[L + 1:] = dec[1:][::-1]
    cst["hy_z"] = np.ascontiguousarray(zf.T).astype(np.float32)
    cst["hy_dec"] = np.ascontiguousarray(df.T).astype(np.float32)
    zc, decc = zfeat(C)
    ze = np.zeros((512, 33)); de = np.zeros((512, 256))
    for m in range(511):
        lag = abs(m - 255)
        ze[m] = zc[lag]; de[m] = decc[lag]
    cst["hy_zc"] = np.ascontiguousarray(ze.T).astype(np.float32)
    cst["hy_decc"] = np.ascontiguousarray(de.T).astype(np.float32)
    n1 = np.arange(128)[:, None]; f1 = np.arange(65)[None, :]
    a = 2 * math.pi * n1 * f1 / 128
    cst["f_WA"] = np.concatenate([np.cos(a), -np.sin(a)], axis=1).astype(np.float32)
    n2 = np.arange(64)[:, None]
    a = 2 * math.pi * n2 * f1 / NFFT
    cst["f_TWr"] = np.cos(a).astype(np.float32)
    cst["f_TWi"] = (-np.sin(a)).astype(np.float32)
    f2 = np.arange(64)[None, :]
    a = 2 * math.pi * n2 * f2 / 64
    cst["f_C64"] = np.cos(a).astype(np.float32)
    cst["f_S64"] = np.sin(a).astype(np.float32)
    cst["f_nS64"] = (-np.sin(a)).astype(np.float32)
    cst["f_CS1"] = np.concatenate([np.cos(a), np.sin(a)], axis=1).astype(np.float32)
    cst["f_CS2"] = np.concatenate([-np.sin(a), np.cos(a)], axis=1).astype(np.float32)
    f1c = np.arange(65)[:, None]; n2r = np.arange(64)[None, :]
    a = 2 * math.pi * f1c * n2r / NFFT
    cst["f_TWir"] = np.cos(a).astype(np.float32)
    cst["f_TWii"] = np.sin(a).astype(np.float32)
    g = np.full((65, 1), 2.0); g[0] = 1.0; g[64] = 1.0
    n1r = np.arange(64)[None, :]
    a = 2 * math.pi * f1c * n1r / 128
    cst["f_Gr"] = (g * np.cos(a) / NFFT).astype(np.float32)
    cst["f_nGi"] = (-g * np.sin(a) / NFFT).astype(np.float32)
    return cst


_CONST_SHAPES = None


class Builder:
    def __init__(self, debug=None, nlayers=DEPTH, upto=None):
        self.debug = debug or []
        self.nlayers = nlayers
        self.upto = upto
        self.nc = bass.Bass("TRN2", target_bir_lowering=False)
        self.es = ExitStack()
        self.S = None
        self.n_sb = 0

    def din(self, name, shape, dt=F32):
        return Dr(self.nc.dram_tensor(name, list(shape), dt, kind="ExternalInput"))

    def dscr(self, name, shape, dt=F32):
        kind = "ExternalOutput" if name in self.debug else "Internal"
        return Dr(self.nc.dram_tensor(name, list(shape), dt, kind=kind))

    def sb(self, st, shape, dt=F32, nslots=1, name=None):
        self.n_sb += 1
        t = st.enter_context(self.nc.sbuf_tensor("%s_%d" % (name or "t", self.n_sb), list(shape), dt))
        return Tl(t, nslots)

    def ps(self, st, shape, dt=F32, name=None):
        self.n_sb += 1
        t = st.enter_context(self.nc.psum_tensor("%s_%d" % (name or "p", self.n_sb), list(shape), dt))
        return Tl(t, 1)

    def dump(self, name, ap, shape, reads, dt=F32):
        if name not in self.debug:
            return
        d = self.nc.dram_tensor(name, list(shape), dt, kind="ExternalOutput")
        self.S.dma(d.ap(), ap, reads=reads)

    def declare(self, consts):
        w = {}
        w["x"] = self.din("x", [L, D])
        w["ctx"] = self.din("ctx", [C, D])
        w["c"] = self.din("c", [D])
        w["c_ctx"] = self.din("c_ctx", [D])
        shp = dict(w_mod=[DEPTH, D, NMOD * D], b_mod=[DEPTH, NMOD * D], norm_g=[DEPTH, 3, D],
                   ffn_w1=[DEPTH, 2, D, 2 * DFF], ffn_w2=[DEPTH, 2, DFF, D], w_in=[DEPTH, D, DIN],
                   w_out=[DEPTH, D, D], lru_conv_w=[DEPTH, 4, 256], lru_conv_b=[DEPTH, 256],
                   lru_wa=[DEPTH, 2, 4, 64, 64], lru_ba=[DEPTH, 2, 256], lru_wx=[DEPTH, 2, 4, 64, 64],
                   lru_bx=[DEPTH, 2, 256], lru_lam=[DEPTH, 2, 256], hy_conv_w=[DEPTH, 3, 768],
                   hy_conv_b=[DEPTH, 768], hy_fw0=[DEPTH, 33, 64], hy_fb0=[DEPTH, 64],
                   hy_fw_in=[DEPTH, 2, 64, 64], hy_fb_in=[DEPTH, 2, 64], hy_freq=[DEPTH, 64],
                   hy_fw_last=[DEPTH, 64, 512], hy_skip=[DEPTH, 256], attn_sink=[DEPTH, 8], final_g=[D])
        for k, s in shp.items():
            w[k] = self.din(k, s)
        for k, v in consts.items():
            w[k] = self.din("k_" + k, v.shape)
        self.w = w
        self.out = Dr(self.nc.dram_tensor("out", [L, D], F32, kind="ExternalOutput"))
        self.xs = self.dscr("xs", [T, D])
        self.modD = self.dscr("modD", [DEPTH, 2, NMOD * D])
        self.projT = self.dscr("projT", [1280, T])
        self.qT = self.dscr("qT", [4, 128, T], BF16)
        self.kT = self.dscr("kT", [128, T], BF16)
        self.vS = self.dscr("vS", [T, 128], BF16)
        self.yT = self.dscr("yT", [D, T], BF16)
        self.uT = self.dscr("uT", [256, T])
        self.x0T = self.dscr("x0T", [256, T])
        self.ycT = self.dscr("ycT", [256, L])
        self.kfT = self.dscr("kfT", [256, NFFT])
        self.KfD = self.dscr("KfD", [2, 64, 256, 65])

    def col_load(self, st, dst, src_ap_1d, n, q="sp"):
        S = self.S
        S.dma(dst.t[:, 0:n], src_ap_1d.rearrange("(k p) -> p k", p=128), writes=[dst.r], q=q,
              allow_slow_non_contiguous=True)

    def phase_mod(self):
        S, nc, w = self.S, self.nc, self.w
        with ExitStack() as st:
            sT = self.sb(st, [128, 8, 2])
            cin = self.sb(st, [128, 16])
            S.dma(cin.t[:, 0:8], w["c"].h.ap().rearrange("(k p) -> p k", p=128), writes=[cin.r],
                  allow_slow_non_contiguous=True)
            S.dma(cin.t[:, 8:16], w["c_ctx"].h.ap().rearrange("(k p) -> p k", p=128), writes=[cin.r],
                  allow_slow_non_contiguous=True)
            S.op("act", lambda e: e.activation(sT.t[:, :, 0], cin.t[:, 0:8], AF.Silu), reads=[cin.r], writes=[sT.r])
            S.op("act", lambda e: e.activation(sT.t[:, :, 1], cin.t[:, 8:16], AF.Silu), reads=[cin.r], writes=[sT.r])
            wt = [self.sb(st, [128, 8, 512]) for _ in range(2)]
            pm = [self.ps(st, [128, 512]) for _ in range(2)]
            bt = self.sb(st, [2, NMOD * D])
            mo = self.sb(st, [2, NMOD * D])
            it = 0
            for l in range(self.nlayers):
                S.dma(bt.t[:], w["b_mod"][l:l + 1, :].broadcast_to([2, NMOD * D]), writes=[bt.r])
                wsrc = w["w_mod"][l].rearrange("(k p) n -> p k n", p=128)
                for cg in range(NMOD * D // 512):
                    wb = wt[it % 2]; pp = pm[it % 2]; it += 1
                    S.dma(wb.t[:], wsrc[:, :, cg * 512:(cg + 1) * 512], writes=[wb.r],
                          q=("sp" if cg % 2 == 0 else "pool"))
                    for k in range(8):
                        S.op("pe", lambda e: e.matmul(pp.t[0:2, :], lhsT=sT.t[:, k, :], rhs=wb.t[:, k, :],
                                                      start=(k == 0), stop=(k == 7)),
                             reads=[sT.r, wb.r], writes=[pp.r])
                    S.op("dve", lambda e: e.tensor_tensor(mo.t[:, cg * 512:(cg + 1) * 512], pp.t[0:2, :],
                                                          bt.t[:, cg * 512:(cg + 1) * 512], ALU.add),
                         reads=[pp.r, bt.r], writes=[mo.r])
                S.dma(self.modD[l], mo.t[:], reads=[mo.r], writes=[self.modD.r(l)])

    def load_mod_tiles(self, l, which, i_shift, i_scale, i_gate, i_norm, gate_mul, G, Sh, Ga, tmp):
        S, w = self.S, self.w
        md = self.modD
        def bc(i):
            return md[l, which:which + 1, i * D:(i + 1) * D].broadcast_to([128, D])
        S.dma(Sh.t[:], bc(i_shift), reads=[md.r(l)], writes=[Sh.r])
        S.dma(tmp.t[:], bc(i_scale), reads=[md.r(l)], writes=[tmp.r])
        S.dma(G.t[:], w["norm_g"][l, i_norm:i_norm + 1, :].broadcast_to([128, D]), writes=[G.r])
        S.op("dve", lambda e: e.scalar_tensor_tensor(G.t[:], tmp.t[:], 1.0, G.t[:], ALU.add, ALU.mult),
             reads=[tmp.r, G.r], writes=[G.r])
        if Ga is not None:
            S.dma(Ga.t[:], bc(i_gate), reads=[md.r(l)], writes=[Ga.r])
            if gate_mul != 1.0:
                S.op("pool", lambda e: e.tensor_scalar_mul(Ga.t[:], Ga.t[:], gate_mul), reads=[Ga.r], writes=[Ga.r])

    def norm_group(self, xt, ntile, G, Sh, hb, ss, rstd, junk, epst):
        S = self.S
        for i in range(ntile):
            S.op("act", lambda e: e.activation(junk.t[:], xt.t[:, i, :], AF.Square, accum_out=ss.t[:, i:i + 1]),
                 reads=[xt.rs[i]], writes=[junk.r, ss.r])
        S.op("act", lambda e: e.activation(rstd.t[:, 0:ntile], ss.t[:, 0:ntile], AF.Sqrt, bias=epst.t[:], scale=1.0 / D),
             reads=[ss.r, epst.r], writes=[rstd.r])
        S.op("dve", lambda e: e.reciprocal(rstd.t[:, 0:ntile], rstd.t[:, 0:ntile]), reads=[rstd.r], writes=[rstd.r])
        for i in range(ntile):
            S.op("dve", lambda e: e.scalar_tensor_tensor(junk.t[:], xt.t[:, i, :], rstd.t[:, i:i + 1], G.t[:],
                                                         ALU.mult, ALU.mult),
                 reads=[xt.rs[i], rstd.r, G.r], writes=[junk.r])
            S.op("pool", lambda e: e.tensor_tensor(hb.t[:, i, :], junk.t[:], Sh.t[:], ALU.add),
                 reads=[junk.r, Sh.r], writes=[hb.rs[i]])

    def transpose_group(self, hb, ntile, hT, ptr, ident):
        S = self.S
        for i in range(ntile):
            p = ptr[i % len(ptr)]
            for k in range(8):
                S.op("pe", lambda e: e.transpose(p.t[:, k, :], hb.t[:, i, k * 128:(k + 1) * 128], ident.t[:]),
                     reads=[hb.rs[i], ident.r], writes=[p.r])
            eng = "act" if i % 2 == 0 else "dve"
            if eng == "act":
                S.op("act", lambda e: e.copy(hT.t[:, :, i * 128:(i + 1) * 128], p.t[:]), reads=[p.r], writes=[hT.r])
            else:
                S.op("dve", lambda e: e.tensor_copy(hT.t[:, :, i * 128:(i + 1) * 128], p.t[:]), reads=[p.r], writes=[hT.r])

    def load_ident(self, st):
        S = self.S
        idf = self.sb(st, [128, 128])
        ident = self.sb(st, [128, 128], BF16)
        S.dma(idf.t[:], self.w["ident"].h.ap(), writes=[idf.r])
        S.op("dve", lambda e: e.tensor_copy(ident.t[:], idf.t[:]), reads=[idf.r], writes=[ident.r])
        return ident

    def cast_weight(self, dst_ap, dst_res, src_ap, stg, idx, ncols, scale=None):
        S = self.S
        sl = stg[idx % len(stg)]
        S.dma(sl.t[:, 0:ncols], src_ap, writes=[sl.r], q=("sp" if idx % 2 == 0 else "pool"))
        eng = ("dve", "act", "pool")[idx % 3]
        if eng == "act":
            S.op("act", lambda e: e.copy(dst_ap, sl.t[:, 0:ncols]), reads=[sl.r], writes=[dst_res])
        else:
            S.op(eng, lambda e: e.tensor_copy(dst_ap, sl.t[:, 0:ncols]), reads=[sl.r], writes=[dst_res])

    def phase_ffn(self, l, j, first, last_layer_ctx_skip=False):
        S, nc, w = self.S, self.nc, self.w
        GT = 256
        i_shift, i_scale, i_gate, i_norm = (0, 1, 2, 0) if j == 0 else (6, 7, 8, 2)
        with ExitStack() as st:
            w1b = self.sb(st, [128, 8, 2 * DFF], BF16, name="w1b")
            w2b = self.sb(st, [128, NF, D], BF16, name="w2b")
            stg = [self.sb(st, [128, 1024], name="stg") for _ in range(2)]
            ident = self.load_ident(st)
            ci = 0
            w1src = w["ffn_w1"][l, j].rearrange("(k p) n -> p k n", p=128)
            for k in range(8):
                for c0 in range(0, 2 * DFF, 1024):
                    nco = min(1024, 2 * DFF - c0)
                    self.cast_weight(w1b.t[:, k, c0:c0 + nco], w1b.r, w1src[:, k, c0:c0 + nco], stg, ci, nco)
                    ci += 1
            w2src = w["ffn_w2"][l, j].rearrange("(f p) n -> p f n", p=128)
            for f in range(NF):
                self.cast_weight(w2b.t[:, f, :], w2b.r, w2src[:, f, :], stg, ci, D)
                ci += 1
            G = self.sb(st, [128, D]); Sh = self.sb(st, [128, D]); Ga = self.sb(st, [128, D])
            xt = self.sb(st, [128, 2, D], nslots=2, name="xt")
            hb = self.sb(st, [128, 2, D], BF16, nslots=2, name="hb")
            junk = self.sb(st, [128, D], name="junk")
            hT = self.sb(st, [128, 8, GT], BF16, name="hT")
            gT = self.sb(st, [128, NF, GT], BF16, nslots=NF, name="gT")
            sil = [self.sb(st, [128, GT], name="sil") for _ in range(2)]
            ss = self.sb(st, [128, 2]); rstd = self.sb(st, [128, 2])
            epst = self.sb(st, [128, 1])
            S.op("pool", lambda e: e.memset(epst.t[:], EPS), writes=[epst.r])
            ptr = [self.ps(st, [128, 8, 128], BF16, name="ptr") for _ in range(1)]
            pab = [self.ps(st, [128, 2, GT], name="pab") for _ in range(3)]
            po = [self.ps(st, [128, 512], name="po") for _ in range(4)]
            ngroups = T // GT
            if last_layer_ctx_skip:
                ngroups = L // GT
            cur_which = None
            for g in range(ngroups):
                t0 = g * GT
                which = 0 if t0 < L else 1
                if which != cur_which:
                    self.load_mod_tiles(l, which, i_shift, i_scale, i_gate, i_norm, 0.5, G, Sh, Ga, junk)
                    cur_which = which
                for i in range(2):
                    r0 = t0 + i * 128
                    if first:
                        src = w["x"][r0:r0 + 128, :] if r0 < L else w["ctx"][r0 - L:r0 - L + 128, :]
                        S.dma(xt.t[:, i, :], src, writes=[xt.rs[i]])
                    else:
                        S.dma(xt.t[:, i, :], self.xs[r0:r0 + 128, :], reads=[self.xs.r(r0 // 128)], writes=[xt.rs[i]])
                self.norm_group(xt, 2, G, Sh, hb, ss, rstd, junk, epst)
                self.transpose_group(hb, 2, hT, ptr, ident)
                for f in range(NF):
                    pb = pab[f % 3]
                    for half in range(2):
                        col = half * DFF + f * 128
                        for k in range(8):
                            S.op("pe", lambda e: e.matmul(pb.t[:, half, :], lhsT=w1b.t[:, k, col:col + 128],
                                                          rhs=hT.t[:, k, :], start=(k == 0), stop=(k == 7)),
                                 reads=[w1b.r, hT.r], writes=[pb.r])
                    sl = sil[f % 2]
                    S.op("act", lambda e: e.activation(sl.t[:], pb.t[:, 0, :], AF.Silu), reads=[pb.r], writes=[sl.r])
                    S.op("dve", lambda e: e.tensor_tensor(gT.t[:, f, :], sl.t[:], pb.t[:, 1, :], ALU.mult),
                         reads=[sl.r, pb.r], writes=[gT.rs[f]])
                for i in range(2):
                    for dh in range(2):
                        pp = po[(i * 2 + dh) % 4]
                        for f in range(NF):
                            S.op("pe", lambda e: e.matmul(pp.t[:], lhsT=gT.t[:, f, i * 128:(i + 1) * 128],
                                                          rhs=w2b.t[:, f, dh * 512:(dh + 1) * 512],
                                                          start=(f == 0), stop=(f == NF - 1)),
                                 reads=[gT.rs[f], w2b.r], writes=[pp.r])
                        S.op("dve", lambda e: e.tensor_tensor(junk.t[:, dh * 512:(dh + 1) * 512], pp.t[:],
                                                              Ga.t[:, dh * 512:(dh + 1) * 512], ALU.mult),
                             reads=[pp.r, Ga.r], writes=[junk.r])
                        S.op("pool", lambda e: e.tensor_tensor(xt.t[:, i, dh * 512:(dh + 1) * 512],
                                                               xt.t[:, i, dh * 512:(dh + 1) * 512],
                                                               junk.t[:, dh * 512:(dh + 1) * 512], ALU.add),
                             reads=[junk.r, xt.rs[i]], writes=[xt.rs[i]])
                    r0 = t0 + i * 128
                    S.dma(self.xs[r0:r0 + 128, :], xt.t[:, i, :], reads=[xt.rs[i]], writes=[self.xs.r(r0 // 128)], q="pool")

    def phase_inproj(self, l):
        S, nc, w = self.S, self.nc, self.w
        GT = 512
        NCOL = 2688
        with ExitStack() as st:
            wb = self.sb(st, [128, 8, NCOL], BF16, name="winb")
            stg = [self.sb(st, [128, DIN], name="stgi") for _ in range(2)]
            ident = self.load_ident(st)
            wsrc = w["w_in"][l].rearrange("(k p) n -> p k n", p=128)
            engs = ("dve", "pool")
            ne = 0
            for k in range(8):
                sl = stg[k % 2]
                S.dma(sl.t[:], wsrc[:, k, :], writes=[sl.r], q=("sp" if k % 2 == 0 else "pool"))
                def cp(dst, src, neg=False):
                    nonlocal ne
                    E = engs[ne % 2]; ne += 1
                    if neg:
                        S.op(E, lambda e: e.tensor_scalar_mul(dst, src, -1.0), reads=[sl.r], writes=[wb.r])
                    else:
                        S.op(E, lambda e: e.tensor_copy(dst, src), reads=[sl.r], writes=[wb.r])
                S.op("act", lambda e: e.copy(wb.t[:, k, 0:1280], sl.t[:, 0:1280]), reads=[sl.r], writes=[wb.r])
                cp(wb.t[:, k, 1792:2048], sl.t[:, 1792:2048])
                qs = sl.t[:, 1280:1792].rearrange("p (half ch d) -> p ch half d", half=2, ch=4)
                cp(wb.t[:, k, 1280:1792].rearrange("p (ch half d) -> p ch half d", ch=4, half=2), qs)
                qs2 = sl.t[:, 1280:1792].rearrange("p (half ch two d) -> p ch half two d", half=2, ch=4, two=2)
                qd2 = wb.t[:, k, 2048:2560].rearrange("p (ch half two d) -> p ch half two d", ch=4, half=2, two=2)
                cp(qd2[:, :, :, 0, :], qs2[:, :, :, 1, :], neg=True)
                cp(qd2[:, :, :, 1, :], qs2[:, :, :, 0, :])
                ks2 = sl.t[:, 1792:1920].rearrange("p (g two d) -> p g two d", g=2, two=2)
                kd2 = wb.t[:, k, 2560:2688].rearrange("p (g two d) -> p g two d", g=2, two=2)
                cp(kd2[:, :, 0, :], ks2[:, :, 1, :], neg=True)
                cp(kd2[:, :, 1, :], ks2[:, :, 0, :])
            G = self.sb(st, [128, D]); Sh = self.sb(st, [128, D])
            xt = self.sb(st, [128, 4, D], nslots=4, name="xti")
            hb = self.sb(st, [128, 4, D], BF16, nslots=4, name="hbi")
            junk = self.sb(st, [128, D], name="junki")
            hT = self.sb(st, [128, 8, GT], BF16, name="hTi")
            ss = self.sb(st, [128, 4]); rstd = self.sb(st, [128, 4])
            epst = self.sb(st, [128, 1])
            S.op("pool", lambda e: e.memset(epst.t[:], EPS), writes=[epst.r])
            rc = self.sb(st, [128, GT]); rs_ = self.sb(st, [128, GT])
            ost = [self.sb(st, [128, GT], name="ost") for _ in range(3)]
            obf = [self.sb(st, [128, GT], BF16, name="obf") for _ in range(2)]
            t1 = self.sb(st, [128, GT]); t2 = self.sb(st, [128, GT])
            vbf = self.sb(st, [128, 4, 128], BF16)
            ptr = [self.ps(st, [128, 8, 128], BF16, name="ptri") for _ in range(2)]
            pp = [self.ps(st, [128, GT], name="ppi") for _ in range(4)]
            pv = self.ps(st, [128, 4, 128], name="pvi")
            groups = [(g * GT, 4) for g in range(L // GT)] + [(L, 2)]
            cur_which = None
            ip = 0
            for (t0, ntile) in groups:
                n = ntile * 128
                which = 0 if t0 < L else 1
                if which != cur_which:
                    self.load_mod_tiles(l, which, 3, 4, None, 1, 1.0, G, Sh, None, junk)
                    cur_which = which
                for i in range(ntile):
                    r0 = t0 + i * 128
                    S.dma(xt.t[:, i, :], self.xs[r0:r0 + 128, :], reads=[self.xs.r(r0 // 128)], writes=[xt.rs[i]])
                S.dma(rc.t[:, 0:n], w["ropeC"][:, t0:t0 + n], writes=[rc.r], q="pool")
                S.dma(rs_.t[:, 0:n], w["ropeS"][:, t0:t0 + n], writes=[rs_.r], q="pool")
                self.norm_group(xt, ntile, G, Sh, hb, ss, rstd, junk, epst)
                self.transpose_group(hb, ntile, hT, ptr, ident)
                if which == 1:
                    self.dump("d_G", G.t[:], [128, D], [G.r])
                    self.dump("d_Sh", Sh.t[:], [128, D], [Sh.r])
                    self.dump("d_rstd", rstd.t[:], [128, 4], [rstd.r])
                    self.dump("d_ss", ss.t[:], [128, 4], [ss.r])
                    self.dump("d_xt", xt.t[:, 0, :], [128, D], [xt.rs[0]])
                    self.dump("d_hb", hb.t[:, 0, :], [128, D], [hb.rs[0]], BF16)
                def mm(pt, c0):
                    for k in range(8):
                        S.op("pe", lambda e: e.matmul(pt.t[:, 0:n], lhsT=wb.t[:, k, c0:c0 + 128], rhs=hT.t[:, k, 0:n],
                                                      start=(k == 0), stop=(k == 7)),
                             reads=[wb.r, hT.r], writes=[pt.r])
                for oc in range(10):
                    pt = pp[ip % 4]; ip += 1
                    mm(pt, oc * 128)
                    o = ost[oc % 3]
                    S.op("act", lambda e: e.copy(o.t[:, 0:n], pt.t[:, 0:n]), reads=[pt.r], writes=[o.r])
                    S.dma(self.projT[oc * 128:(oc + 1) * 128, t0:t0 + n], o.t[:, 0:n], reads=[o.r],
                          writes=[self.projT.r((oc, t0))], q="pool")
                for ch in range(5):
                    pa = pp[ip % 4]; ip += 1
                    pb_ = pp[ip % 4]; ip += 1
                    mm(pa, 1280 + ch * 128)
                    mm(pb_, 2048 + ch * 128)
                    S.op("dve", lambda e: e.tensor_tensor(t1.t[:, 0:n], pa.t[:, 0:n], rc.t[:, 0:n], ALU.mult),
                         reads=[pa.r, rc.r], writes=[t1.r])
                    S.op("dve", lambda e: e.tensor_tensor(t2.t[:, 0:n], pb_.t[:, 0:n], rs_.t[:, 0:n], ALU.mult),
                         reads=[pb_.r, rs_.r], writes=[t2.r])
                    ob = obf[ch % 2]
                    S.op("pool", lambda e: e.tensor_tensor(ob.t[:, 0:n], t1.t[:, 0:n], t2.t[:, 0:n], ALU.add),
                         reads=[t1.r, t2.r], writes=[ob.r])
                    if ch < 4:
                        S.dma(self.qT[ch, :, t0:t0 + n], ob.t[:, 0:n], reads=[ob.r], writes=[self.qT.r((ch, t0))], q="pool")
                    else:
                        S.dma(self.kT[:, t0:t0 + n], ob.t[:, 0:n], reads=[ob.r], writes=[self.kT.r(t0)], q="pool")
                for i in range(ntile):
                    for k in range(8):
                        S.op("pe", lambda e: e.matmul(pv.t[:, i, :], lhsT=hT.t[:, k, i * 128:(i + 1) * 128],
                                                      rhs=wb.t[:, k, 1920:2048], start=(k == 0), stop=(k == 7)),
                             reads=[wb.r, hT.r], writes=[pv.r])
                S.op("act", lambda e: e.copy(vbf.t[:, 0:ntile, :], pv.t[:, 0:ntile, :]), reads=[pv.r], writes=[vbf.r])
                S.dma(self.vS[t0:t0 + n, :].rearrange("(i p) d -> p i d", p=128), vbf.t[:, 0:ntile, :], reads=[vbf.r],
                      writes=[self.vS.r(t0)], q="pool")

    def vec_col(self, dst_ap, src_1d, res):
        self.S.dma(dst_ap, src_1d.rearrange("(p o) -> p o", o=1), writes=[res])

    def phase_lru(self, l):
        S, nc, w = self.S, self.nc, self.w
        with ExitStack() as st:
            big = lambda nm: self.sb(st, [128, T], name=nm)
            xl, u, gl, A, B, Cc, Df, Eb = [big(n) for n in ("xl", "u", "gl", "A", "B", "Cc", "Df", "Eb")]
            ybf = self.sb(st, [128, T], BF16, name="ybf")
            one = self.sb(st, [128, 1])
            S.op("pool", lambda e: e.memset(one.t[:], 1.0), writes=[one.r])
            pg = [self.ps(st, [128, 512], name="pg") for _ in range(4)]
            ip = 0
            for cc in range(2):
                c0 = cc * 128
                S.dma(xl.t[:], self.projT[c0:c0 + 128, :], reads=[self.projT.r(k) for k in self.projT.res], writes=[xl.r])
                S.dma(gl.t[:], self.projT[256 + c0:256 + c0 + 128, :], reads=[self.projT.r(k) for k in self.projT.res], writes=[gl.r], q="pool")
                cw = self.sb(st, [128, 4]); cb = self.sb(st, [128, 1])
                S.dma(cw.t[:], w["lru_conv_w"][l, :, c0:c0 + 128].rearrange("k p -> p k"), writes=[cw.r], allow_slow_non_contiguous=True)
                self.vec_col(cb.t[:], w["lru_conv_b"][l, c0:c0 + 128], cb.r)
                for (s0, s1) in ((0, L), (L, T)):
                    S.op("dve", lambda e: e.tensor_scalar(u.t[:, s0:s1], xl.t[:, s0:s1], cw.t[:, 2:3], cb.t[:, 0:1], ALU.mult, ALU.add),
                         reads=[xl.r, cw.r, cb.r], writes=[u.r])
                    for k, off in ((0, -2), (1, -1), (3, 1)):
                        if off < 0:
                            o_ = u.t[:, s0 - off:s1]; i_ = xl.t[:, s0:s1 + off]
                        else:
                            o_ = u.t[:, s0:s1 - off]; i_ = xl.t[:, s0 + off:s1]
                        S.op("dve", lambda e: e.scalar_tensor_tensor(o_, i_, cw.t[:, k:k + 1], o_, ALU.mult, ALU.add),
                             reads=[xl.r, cw.r, u.r], writes=[u.r])
                for d in range(2):
                    wa = self.sb(st, [128, 128]); wx = self.sb(st, [128, 128])
                    for (wt_, nm) in ((wa, "lru_wa"), (wx, "lru_wx")):
                        S.op("pool", lambda e: e.memset(wt_.t[:], 0.0), writes=[wt_.r])
                        S.dma(wt_.t[0:64, 0:64], w[nm][l, d, 2 * cc], writes=[wt_.r])
                        S.dma(wt_.t[64:128, 64:128], w[nm][l, d, 2 * cc + 1], writes=[wt_.r])
                    ba = self.sb(st, [128, 1]); bx = self.sb(st, [128, 1]); lam = self.sb(st, [128, 1])
                    self.vec_col(ba.t[:], w["lru_ba"][l, d, c0:c0 + 128], ba.r)
                    self.vec_col(bx.t[:], w["lru_bx"][l, d, c0:c0 + 128], bx.r)
                    self.vec_col(lam.t[:], w["lru_lam"][l, d, c0:c0 + 128], lam.r)
                    sp = self.sb(st, [128, 2])
                    S.op("act", lambda e: e.activation(lam.t[:], lam.t[:], AF.Exp, scale=-1.0), reads=[lam.r], writes=[lam.r])
                    S.op("act", lambda e: e.activation(lam.t[:], lam.t[:], AF.Ln, bias=one.t[:], scale=1.0), reads=[lam.r, one.r], writes=[lam.r])
                    S.op("dve", lambda e: e.tensor_scalar_mul(sp.t[:, 0:1], lam.t[:], -8.0), reads=[lam.r], writes=[sp.r])
                    S.op("dve", lambda e: e.tensor_scalar_mul(sp.t[:, 1:2], lam.t[:], -16.0), reads=[lam.r], writes=[sp.r])
                    for g0 in range(0, T, 512):
                        n = min(512, T - g0)
                        for (wt_, bt_, dst) in ((wa, ba, A), (wx, bx, B)):
                            pt = pg[ip % 4]; ip += 1
                            S.op("pe", lambda e: e.matmul(pt.t[:, 0:n], lhsT=wt_.t[:], rhs=u.t[:, g0:g0 + n], start=True, stop=True),
                                 reads=[wt_.r, u.r], writes=[pt.r])
                            S.op("act", lambda e: e.activation(dst.t[:, g0:g0 + n], pt.t[:, 0:n], AF.Sigmoid, bias=bt_.t[:], scale=1.0),
                                 reads=[pt.r, bt_.r], writes=[dst.r])
                    S.op("act", lambda e: e.activation(Cc.t[:], A.t[:], AF.Exp, scale=sp.t[:, 0:1]), reads=[A.r, sp.r], writes=[Cc.r])
                    S.op("act", lambda e: e.activation(A.t[:], A.t[:], AF.Exp, scale=sp.t[:, 1:2]), reads=[A.r, sp.r], writes=[A.r])
                    S.op("act", lambda e: e.activation(A.t[:], A.t[:], AF.Sqrt, bias=one.t[:], scale=-1.0), reads=[A.r, one.r], writes=[A.r])
                    S.op("pool", lambda e: e.tensor_tensor(B.t[:], B.t[:], u.t[:], ALU.mult), reads=[B.r, u.r], writes=[B.r])
                    S.op("pool", lambda e: e.tensor_tensor(A.t[:], A.t[:], B.t[:], ALU.mult), reads=[A.r, B.r], writes=[A.r])
                    if d == 0:
                        S.op("dve", lambda e: e.tensor_tensor_scan(Df.t[:, L:T], Cc.t[:, L:T], A.t[:, L:T], 0.0, ALU.mult, ALU.add),
                             reads=[Cc.r, A.r], writes=[Df.r])
                        S.op("dve", lambda e: e.tensor_tensor_scan(Df.t[:, 0:L], Cc.t[:, 0:L], A.t[:, 0:L], Df.t[:, T - 1:T], ALU.mult, ALU.add),
                             reads=[Cc.r, A.r, Df.r], writes=[Df.r])
                    else:
                        rv = lambda tl, a, b: tl.t[:, a:b][:, ::-1]
                        S.op("dve", lambda e: e.tensor_tensor_scan(rv(Eb, L, T), rv(Cc, L, T), rv(A, L, T), 0.0, ALU.mult, ALU.add),
                             reads=[Cc.r, A.r], writes=[Eb.r])
                        S.op("dve", lambda e: e.tensor_tensor_scan(rv(Eb, 0, L), rv(Cc, 0, L), rv(A, 0, L), Eb.t[:, L:L + 1], ALU.mult, ALU.add),
                             reads=[Cc.r, A.r, Eb.r], writes=[Eb.r])
                S.op("act", lambda e: e.activation(gl.t[:], gl.t[:], AF.Gelu), reads=[gl.r], writes=[gl.r])
                S.op("pool", lambda e: e.tensor_tensor(Df.t[:], Df.t[:], Eb.t[:], ALU.add), reads=[Df.r, Eb.r], writes=[Df.r])
                S.op("dve", lambda e: e.tensor_tensor(ybf.t[:], Df.t[:], gl.t[:], ALU.mult), reads=[Df.r, gl.r], writes=[ybf.r])
                S.dma(self.yT[c0:c0 + 128, :], ybf.t[:], reads=[ybf.r], writes=[self.yT.r(("lru", cc))])

    def phase_attn(self, l, need_ctx):
        S, nc, w = self.S, self.nc, self.w
        allk = lambda d: [d.r(k) for k in d.res]
        with ExitStack() as st:
            kTs = self.sb(st, [128, T], BF16, name="kTs")
            qTs = self.sb(st, [128, 4, T], BF16, name="qTs")
            vtmp = self.sb(st, [128, NT, 128], BF16, name="vtmp")
            vA = self.sb(st, [128, NT, 2, 65], BF16, name="vA")
            S.dma(kTs.t[:], self.kT.h.ap(), reads=allk(self.kT), writes=[kTs.r])
            for ch in range(4):
                S.dma(qTs.t[:, ch, :], self.qT[ch], reads=allk(self.qT), writes=[qTs.r], q=("sp" if ch % 2 == 0 else "pool"))
            S.dma(vtmp.t[:], self.vS.h.ap().rearrange("(i p) d -> p i d", p=128), reads=allk(self.vS), writes=[vtmp.r])
            S.op("pool", lambda e: e.memset(vA.t[:], 1.0), writes=[vA.r])
            S.op("dve", lambda e: e.tensor_copy(vA.t[:, :, :, 0:64], vtmp.t[:].rearrange("p i (g d) -> p i g d", g=2)),
                 reads=[vtmp.r, vA.r], writes=[vA.r])
            ident = self.load_ident(st)
            mf = self.sb(st, [128, 512])
            mprev = self.sb(st, [128, 512], BF16); mnext = self.sb(st, [128, 512], BF16)
            for (mt, nm) in ((mprev, "mprev"), (mnext, "mnext")):
                S.dma(mf.t[:], w[nm].h.ap(), writes=[mf.r])
                S.op("dve", lambda e: e.tensor_copy(mt.t[:], mf.t[:]), reads=[mf.r], writes=[mt.r])
            sk = self.sb(st, [128, 8])
            S.dma(sk.t[:], w["attn_sink"][l:l + 1, :].broadcast_to([128, 8]), writes=[sk.r])
            S.op("act", lambda e: e.activation(sk.t[:], sk.t[:], AF.Exp), reads=[sk.r], writes=[sk.r])
            ps_s = [self.ps(st, [128, 512], name="ps_s") for _ in range(3)]
            ps_o = [self.ps(st, [128, 512], name="ps_o") for _ in range(4)]
            ptr = self.ps(st, [128, 4, 128], BF16, name="ptra")
            pbuf = [self.sb(st, [128, 512], BF16, name="pbuf") for _ in range(3)]
            yt = [self.sb(st, [128, 512], BF16, name="yt") for _ in range(2)]
            ytT = [self.sb(st, [128, 4, 128], BF16, name="ytT") for _ in range(2)]
            den = [self.sb(st, [128, 4], name="den") for _ in range(2)]
            blocks = []
            for i in range(L // 128):
                ch = []
                if i > 0:
                    ch.append((i - 1, mprev))
                ch.append((i, None))
                if i < L // 128 - 1:
                    ch.append((i + 1, mnext))
                ch += [(32, None), (33, None)]
                blocks.append((i, ch))
            if need_ctx:
                for i in (32, 33):
                    blocks.append((i, [(32, None), (33, None)]))
            rot = 0; io = 0
            for bi, (qi, chunks) in enumerate(blocks):
                y = yt[bi % 2]
                for g in range(2):
                    pof = ps_o[io % 4]; io += 1
                    po = Tl(pof.t[:, 0:260].rearrange("p (h d) -> p h d", h=4))
                    po.rs = pof.rs
                    pr = slice(g * 64, (g + 1) * 64)
                    for ci, (kc, mk) in enumerate(chunks):
                        ps = ps_s[rot % 3]; pT = pbuf[rot % 3]; rot += 1
                        S.op("pe", lambda e: e.matmul(ps.t[:].rearrange("p (c q) -> p c q", c=4),
                                                      lhsT=kTs.t[pr, kc * 128:(kc + 1) * 128],
                                                      rhs=qTs.t[pr, :, qi * 128:(qi + 1) * 128], start=True, stop=True),
                             reads=[kTs.r, qTs.r], writes=[ps.r])
                        S.op("act", lambda e: e.activation(pT.t[:], ps.t[:], AF.Exp, scale=0.125), reads=[ps.r], writes=[pT.r])
                        if mk is not None:
                            E = "dve" if rot % 2 == 0 else "pool"
                            S.op(E, lambda e: e.tensor_tensor(pT.t[:], pT.t[:], mk.t[:], ALU.mult), reads=[pT.r, mk.r], writes=[pT.r])
                        for hh in range(4):
                            S.op("pe", lambda e: e.matmul(po.t[:, hh, :], lhsT=pT.t[:, hh * 128:(hh + 1) * 128],
                                                          rhs=vA.t[:, kc, g, :], start=(ci == 0 and hh == 0),
                                                          stop=(ci == len(chunks) - 1 and hh == 3)),
                                 reads=[pT.r, vA.r], writes=[po.r])
                    dn = den[g]
                    S.op("dve", lambda e: e.tensor_tensor(dn.t[:], po.t[:, :, 64], sk.t[:, g * 4:(g + 1) * 4], ALU.add),
                         reads=[po.r, sk.r], writes=[dn.r])
                    S.op("dve", lambda e: e.reciprocal(dn.t[:], dn.t[:]), reads=[dn.r], writes=[dn.r])
                    S.op("dve", lambda e: e.tensor_tensor(y.t[:, g * 256:(g + 1) * 256].rearrange("p (h d) -> p h d", h=4),
                                                          po.t[:, :, 0:64],
                                                          dn.t[:, :].unsqueeze(2).broadcast_to([128, 4, 64]), ALU.mult),
                         reads=[po.r, dn.r], writes=[y.r])
                for pair in range(4):
                    S.op("pe", lambda e: e.transpose(ptr.t[:, pair, :], y.t[:, pair * 128:(pair + 1) * 128], ident.t[:]),
                         reads=[y.r, ident.r], writes=[ptr.r])
                yo = ytT[bi % 2]
                S.op("act", lambda e: e.copy(yo.t[:], ptr.t[:]), reads=[ptr.r], writes=[yo.r])
                S.dma(self.yT[512:1024, qi * 128:(qi + 1) * 128].rearrange("(c p) t -> p c t", p=128), yo.t[:],
                      reads=[yo.r], writes=[self.yT.r(("att", qi))], q="pool")

    def phase_outproj(self, l, need_ctx):
        S, nc, w = self.S, self.nc, self.w
        allk = lambda d: [d.r(k) for k in d.res]
        GT = 512
        with ExitStack() as st:
            wob = self.sb(st, [128, 8, D], BF16, name="wob")
            stg = [self.sb(st, [128, 1024], name="stgo") for _ in range(2)]
            wsrc = w["w_out"][l].rearrange("(k p) n -> p k n", p=128)
            for k in range(8):
                self.cast_weight(wob.t[:, k, :], wob.r, wsrc[:, k, :], stg, k, D)
            Ga = self.sb(st, [128, D])
            yt = [self.sb(st, [128, 8, GT], BF16, name="yto") for _ in range(2)]
            xt = [self.sb(st, [128, D], name="xto") for _ in range(3)]
            junk = self.sb(st, [128, D], name="junko")
            po = [self.ps(st, [128, 512], name="poo") for _ in range(4)]
            groups = [(g * GT, 4) for g in range(L // GT)]
            if need_ctx:
                groups.append((L, 2))
            cur_which = None
            ix = 0; ipo = 0
            yres = allk(self.yT)
            for gi, (t0, ntile) in enumerate(groups):
                n = ntile * 128
                which = 0 if t0 < L else 1
                if which != cur_which:
                    S.dma(Ga.t[:], self.modD[l, which:which + 1, 5 * D:6 * D].broadcast_to([128, D]),
                          reads=[self.modD.r(l)], writes=[Ga.r])
                    cur_which = which
                y = yt[gi % 2]
                S.dma(y.t[:, :, 0:n], self.yT[:, t0:t0 + n].rearrange("(k p) t -> p k t", p=128), reads=yres, writes=[y.r])
                for i in range(ntile):
                    r0 = t0 + i * 128
                    x = xt[ix % 3]; ix += 1
                    S.dma(x.t[:], self.xs[r0:r0 + 128, :], reads=[self.xs.r(r0 // 128)], writes=[x.r], q="pool")
                    for dh in range(2):
                        pp = po[ipo % 4]; ipo += 1
                        for k in range(8):
                            S.op("pe", lambda e: e.matmul(pp.t[:], lhsT=y.t[:, k, i * 128:(i + 1) * 128],
                                                          rhs=wob.t[:, k, dh * 512:(dh + 1) * 512], start=(k == 0), stop=(k == 7)),
                                 reads=[y.r, wob.r], writes=[pp.r])
                        S.op("dve", lambda e: e.tensor_tensor(junk.t[:, dh * 512:(dh + 1) * 512], pp.t[:],
                                                              Ga.t[:, dh * 512:(dh + 1) * 512], ALU.mult),
                             reads=[pp.r, Ga.r], writes=[junk.r])
                        S.op("pool", lambda e: e.tensor_tensor(x.t[:, dh * 512:(dh + 1) * 512], x.t[:, dh * 512:(dh + 1) * 512],
                                                               junk.t[:, dh * 512:(dh + 1) * 512], ALU.add),
                             reads=[junk.r, x.r], writes=[x.r])
                    S.dma(self.xs[r0:r0 + 128, :], x.t[:], reads=[x.r], writes=[self.xs.r(r0 // 128)], q="pool")

    def phase_hyena(self, l, need_ctx):
        S, nc, w = self.S, self.nc, self.w
        allk = lambda d: [d.r(k) for k in d.res]
        PI = math.pi
        NB = 12
        with ExitStack() as st:
            banks = [self.ps(st, [128, 512], name="hb") for _ in range(8)]
            bk = [0]
            def bank():
                b = banks[bk[0] % 8]; bk[0] += 1
                return b
            def ld(name, shape):
                t = self.sb(st, shape, name=name)
                S.dma(t.t[:], w[name].h.ap(), writes=[t.r])
                return t
            WA = ld("f_WA", [128, 130]); TWr = ld("f_TWr", [64, 65]); TWi = ld("f_TWi", [64, 65])
            C64 = ld("f_C64", [64, 64]); S64 = ld("f_S64", [64, 64]); nS64 = ld("f_nS64", [64, 64])
            CS1 = ld("f_CS1", [64, 128]); CS2 = ld("f_CS2", [64, 128])
            TWir = ld("f_TWir", [65, 64]); TWii = ld("f_TWii", [65, 64])
            Gr = ld("f_Gr", [65, 64]); nGi = ld("f_nGi", [65, 64])
            fw0 = self.sb(st, [33, 64]); S.dma(fw0.t[:], w["hy_fw0"][l], writes=[fw0.r])
            fwi = [self.sb(st, [64, 64]) for _ in range(2)]
            for j in range(2):
                S.dma(fwi[j].t[:], w["hy_fw_in"][l, j], writes=[fwi[j].r])
            fwl = self.sb(st, [64, 512]); S.dma(fwl.t[:], w["hy_fw_last"][l], writes=[fwl.r])
            freq = self.sb(st, [64, 1]); self.vec_col(freq.t[:], w["hy_freq"][l], freq.r)
            fb = self.sb(st, [64, 3])
            self.vec_col(fb.t[:, 0:1], w["hy_fb0"][l], fb.r)
            self.vec_col(fb.t[:, 1:2], w["hy_fb_in"][l, 0], fb.r)
            self.vec_col(fb.t[:, 2:3], w["hy_fb_in"][l, 1], fb.r)
            S.op("dve", lambda e: e.tensor_scalar_mul(fb.t[:], fb.t[:], freq.t[:, 0:1]),
                 reads=[fb.r, freq.r], writes=[fb.r])
            kq = self.sb(st, [64, 512], mybir.dt.int32, name="kq")
            zt = [self.sb(st, [33, 512], name="zt") for _ in range(2)]
            hh_ = [self.sb(st, [64, 512], name="hmlp") for _ in range(3)]
            dect = [self.sb(st, [128, 512], name="dect") for _ in range(2)]
            kout = [self.sb(st, [128, 512], name="kout") for _ in range(2)]
            kext = [self.sb(st, [128, 512], name="kext") for _ in range(2)]
            cnt = [0]

            def mlp(zsrc, ncol):
                z = zt[cnt[0] % 2]
                S.dma(z.t[:, 0:ncol], zsrc, writes=[z.r])
                srcs = [(fw0, z, 33)] + [(fwi[0], None, 64), (fwi[1], None, 64)]
                h = None
                for j, (wt_, _, kk) in enumerate(srcs):
                    pm = bank()
                    rhs = z.t[0:33, 0:ncol] if j == 0 else h.t[:, 0:ncol]
                    rres = z.r if j == 0 else h.r
                    S.op("pe", lambda e: e.matmul(pm.t[0:64, 0:ncol], lhsT=wt_.t[:], rhs=rhs, start=True, stop=True),
                         reads=[wt_.r, rres], writes=[pm.r])
                    hn = hh_[(cnt[0] * 3 + j) % 3]
                    S.op("dve", lambda e: e.tensor_scalar(hn.t[:, 0:ncol], pm.t[0:64, 0:ncol], freq.t[:, 0:1], fb.t[:, j:j + 1], ALU.mult, ALU.add),
                         reads=[pm.r, freq.r, fb.r], writes=[hn.r])
                    S.op("dve", lambda e: e.tensor_scalar_mul(kq.t[:, 0:ncol], hn.t[:, 0:ncol], 1.0 / (2.0 * PI)),
                         reads=[hn.r], writes=[kq.r])
                    S.op("dve", lambda e: e.scalar_tensor_tensor(hn.t[:, 0:ncol], kq.t[:, 0:ncol], -2.0 * PI, hn.t[:, 0:ncol], ALU.mult, ALU.add),
                         reads=[hn.r, kq.r], writes=[hn.r])
                    S.op("dve", lambda e: e.tensor_scalar(hn.t[:, 0:ncol], hn.t[:, 0:ncol], -PI, PI, ALU.max, ALU.min),
                         reads=[hn.r], writes=[hn.r])
                    S.op("act", lambda e: e.activation(hn.t[:, 0:ncol], hn.t[:, 0:ncol], AF.Sin),
                         reads=[hn.r], writes=[hn.r])
                    h = hn
                cnt[0] += 1
                return h

            for g in range(NFFT // 512):
                h = mlp(w["hy_z"][:, g * 512:(g + 1) * 512], 512)
                wsel = 0 if g < 8 else 1
                for cch in range(2):
                    pk = bank()
                    S.op("pe", lambda e: e.matmul(pk.t[:], lhsT=fwl.t[:, wsel * 256 + cch * 128: wsel * 256 + (cch + 1) * 128],
                                                  rhs=h.t[:], start=True, stop=True), reads=[fwl.r, h.r], writes=[pk.r])
                    dt_ = dect[cch]; ko = kout[cch]
                    S.dma(dt_.t[:], w["hy_dec"][cch * 128:(cch + 1) * 128, g * 512:(g + 1) * 512], writes=[dt_.r], q="pool")
                    S.op("dve", lambda e: e.tensor_tensor(ko.t[:], pk.t[:], dt_.t[:], ALU.mult), reads=[pk.r, dt_.r], writes=[ko.r])
                    S.dma(self.kfT[cch * 128:(cch + 1) * 128, g * 512:(g + 1) * 512], ko.t[:], reads=[ko.r],
                          writes=[self.kfT.r((cch, g))], q="pool")
            if need_ctx:
                h = mlp(w["hy_zc"].h.ap(), 512)
                for cch in range(2):
                    dt_ = dect[cch]
                    S.dma(dt_.t[:], w["hy_decc"][cch * 128:(cch + 1) * 128, :], writes=[dt_.r], q="pool")
                    for wsel, (a, b) in ((1, (0, 255)), (0, (255, 511))):
                        pk = bank()
                        S.op("pe", lambda e: e.matmul(pk.t[:], lhsT=fwl.t[:, wsel * 256 + cch * 128: wsel * 256 + (cch + 1) * 128],
                                                      rhs=h.t[:], start=True, stop=True), reads=[fwl.r, h.r], writes=[pk.r])
                        S.op("dve", lambda e: e.tensor_tensor(kext[cch].t[:, a:b], pk.t[:, a:b], dt_.t[:, a:b], ALU.mult),
                             reads=[pk.r, dt_.r], writes=[kext[cch].r])

            usb = [self.sb(st, [128, T], name="usb") for _ in range(2)]
            x0sb = [self.sb(st, [128, T], name="x0sb") for _ in range(2)]
            skipc = self.sb(st, [128, 2])
            for cc in range(2):
                self.vec_col(skipc.t[:, cc:cc + 1], w["hy_skip"][l, cc * 128:(cc + 1) * 128], skipc.r)
            pres = allk(self.projT)
            with ExitStack() as st2:
                raw = self.sb(st2, [128, T], name="raw")
                x1c = self.sb(st2, [128, T], name="x1c")
                vc = self.sb(st2, [128, T], name="vc")
                cw = self.sb(st2, [128, 3]); cb = self.sb(st2, [128, 1])
                for cc in range(2):
                    for part, dst in ((0, x0sb[cc]), (1, x1c), (2, vc)):
                        ch0 = part * 256 + cc * 128
                        S.dma(raw.t[:], self.projT[512 + ch0:512 + ch0 + 128, :], reads=pres, writes=[raw.r])
                        S.dma(cw.t[:], w["hy_conv_w"][l, :, ch0:ch0 + 128].rearrange("k p -> p k"), writes=[cw.r],
                              allow_slow_non_contiguous=True)
                        self.vec_col(cb.t[:], w["hy_conv_b"][l, ch0:ch0 + 128], cb.r)
                        for (s0, s1) in ((0, L), (L, T)):
                            S.op("act", lambda e: e.activation(dst.t[:, s0:s1], raw.t[:, s0:s1], AF.Identity, bias=cb.t[:], scale=cw.t[:, 1:2]),
                                 reads=[raw.r, cw.r, cb.r], writes=[dst.r])
                            o_ = dst.t[:, s0 + 1:s1]; i_ = raw.t[:, s0:s1 - 1]
                            S.op("dve", lambda e: e.scalar_tensor_tensor(o_, i_, cw.t[:, 0:1], o_, ALU.mult, ALU.add),
                                 reads=[raw.r, cw.r, dst.r], writes=[dst.r])
                            o_ = dst.t[:, s0:s1 - 1]; i_ = raw.t[:, s0 + 1:s1]
                            S.op("dve", lambda e: e.scalar_tensor_tensor(o_, i_, cw.t[:, 2:3], o_, ALU.mult, ALU.add),
                                 reads=[raw.r, cw.r, dst.r], writes=[dst.r])
                    S.op("pool", lambda e: e.tensor_tensor(usb[cc].t[:], x1c.t[:], vc.t[:], ALU.mult), reads=[x1c.r, vc.r], writes=[usb[cc].r])
                    S.dma(self.uT[cc * 128:(cc + 1) * 128, :], usb[cc].t[:], reads=[usb[cc].r], writes=[self.uT.r(cc)])
                S.barrier()

            Bre = self.sb(st, [64, NB, 65], name="Bre"); Bim = self.sb(st, [64, NB, 65], name="Bim")
            tA = [self.sb(st, [65, 512], name="tA") for _ in range(4)]
            Ut = [self.sb(st, [128, NB, 64], name="Ut") for _ in range(2)]
            Kre = self.sb(st, [64, NB, 65], name="Kre"); Kim = self.sb(st, [64, NB, 65], name="Kim")
            Yre = self.sb(st, [64, NB, 65], name="Yre"); Yim = self.sb(st, [64, NB, 65], name="Yim")
            Bpre = self.sb(st, [65, NB, 64], name="Bpre"); Bpim = self.sb(st, [65, NB, 64], name="Bpim")
            ycs = [self.sb(st, [64, NB, 64], name="ycs") for _ in range(2)]
            xo = [self.sb(st, [64, 512], name="xo") for _ in range(2)]

            def cmul(shape_view, ar, ai, br, bi, outr, outi, rres, wres_r, wres_i, conj_sign=1.0):
                t1, t2, t3, t4 = [shape_view(t) for t in tA]
                S.op("dve", lambda e: e.tensor_tensor(t1, ar, br, ALU.mult), reads=rres, writes=[tA[0].r])
                S.op("dve", lambda e: e.tensor_tensor(t2, ai, bi, ALU.mult), reads=rres, writes=[tA[1].r])
                S.op("pool", lambda e: e.tensor_tensor(outr, t1, t2, ALU.subtract), reads=[tA[0].r, tA[1].r], writes=[wres_r])
                S.op("dve", lambda e: e.tensor_tensor(t3, ar, bi, ALU.mult), reads=rres, writes=[tA[2].r])
                S.op("dve", lambda e: e.tensor_tensor(t4, ai, br, ALU.mult), reads=rres, writes=[tA[3].r])
                S.op("pool", lambda e: e.tensor_tensor(outi, t3, t4, ALU.add), reads=[tA[2].r, tA[3].r], writes=[wres_i])

            def fwd(U, K, nb):
                for sub in range(0, nb, 3):
                    ns = min(3, nb - sub)
                    pa = bank()
                    for j in range(ns):
                        S.op("pe", lambda e: e.matmul(pa.t[0:64, j * 130:(j + 1) * 130], lhsT=U.t[0:K, sub + j, :], rhs=WA.t[0:K, :],
                                                      start=True, stop=True), reads=[U.r, WA.r], writes=[pa.r])
                    pv_ = pa.t[0:64, 0:ns * 130].rearrange("p (c f) -> p c f", c=ns)
                    tw = lambda t: t.t[:, :].unsqueeze(1).broadcast_to([64, ns, 65])
                    sv = lambda t: t.t[0:64, 0:ns * 65].rearrange("p (c f) -> p c f", c=ns)
                    cmul(sv, pv_[:, :, 0:65], pv_[:, :, 65:130], tw(TWr), tw(TWi),
                         Bre.t[:, sub:sub + ns, :], Bim.t[:, sub:sub + ns, :], [pa.r, TWr.r, TWi.r], Bre.r, Bim.r)
                for grp in range(0, nb, 6):
                    ng = min(6, nb - grp)
                    ncol = ng * 65
                    pxr = bank(); pxi = bank()
                    br_ = Bre.t[:, grp:grp + ng, :].rearrange("p c f -> p (c f)")
                    bi_ = Bim.t[:, grp:grp + ng, :].rearrange("p c f -> p (c f)")
                    S.op("pe", lambda e: e.matmul(pxr.t[0:64, 0:ncol], lhsT=C64.t[:], rhs=br_, start=True, stop=False), reads=[C64.r, Bre.r], writes=[pxr.r])
                    S.op("pe", lambda e: e.matmul(pxr.t[0:64, 0:ncol], lhsT=S64.t[:], rhs=bi_, start=False, stop=True), reads=[S64.r, Bim.r], writes=[pxr.r])
                    S.op("pe", lambda e: e.matmul(pxi.t[0:64, 0:ncol], lhsT=C64.t[:], rhs=bi_, start=True, stop=False), reads=[C64.r, Bim.r], writes=[pxi.r])
                    S.op("pe", lambda e: e.matmul(pxi.t[0:64, 0:ncol], lhsT=nS64.t[:], rhs=br_, start=False, stop=True), reads=[nS64.r, Bre.r], writes=[pxi.r])
                    yield grp, ng, pxr, pxi

            batches = [(c0, min(NB, 256 - c0)) for c0 in range(0, 256, NB)]
            kres = allk(self.kfT)
            for bi_, (c0, nb) in enumerate(batches):
                U = Ut[bi_ % 2]
                S.dma(U.t[:, 0:nb, :], self.kfT[c0:c0 + nb, :].rearrange("c (a b) -> a c b", b=64), reads=kres, writes=[U.r])
                for grp, ng, pxr, pxi in fwd(U, 128, nb):
                    ncol = ng * 65
                    for which, px in ((0, pxr), (1, pxi)):
                        o = xo[which]
                        if which == 0:
                            S.op("act", lambda e: e.copy(o.t[:, 0:ncol], px.t[0:64, 0:ncol]), reads=[px.r], writes=[o.r])
                        else:
                            S.op("dve", lambda e: e.tensor_copy(o.t[:, 0:ncol], px.t[0:64, 0:ncol]), reads=[px.r], writes=[o.r])
                        S.dma(self.KfD[which, :, c0 + grp:c0 + grp + ng, :], o.t[:, 0:ncol].rearrange("p (c f) -> p c f", c=ng),
                              reads=[o.r], writes=[self.KfD.r((which, c0 + grp))], q="pool")
            S.barrier()
            ures = allk(self.uT)
            kfres = allk(self.KfD)
            for bi_, (c0, nb) in enumerate(batches):
                U = Ut[bi_ % 2]
                S.dma(U.t[0:64, 0:nb, :], self.uT[c0:c0 + nb, 0:L].rearrange("c (a b) -> a c b", b=64), reads=ures, writes=[U.r])
                S.dma(Kre.t[:, 0:nb, :], self.KfD[0, :, c0:c0 + nb, :], reads=kfres, writes=[Kre.r], q="pool")
                S.dma(Kim.t[:, 0:nb, :], self.KfD[1, :, c0:c0 + nb, :], reads=kfres, writes=[Kim.r], q="pool")
                for grp, ng, pxr, pxi in fwd(U, 64, nb):
                    ncol = ng * 65
                    sv = lambda t: t.t[0:64, 0:ncol]
                    fl = lambda t: t.t[:, grp:grp + ng, :].rearrange("p c f -> p (c f)")
                    cmul(sv, pxr.t[0:64, 0:ncol], pxi.t[0:64, 0:ncol], fl(Kre), fl(Kim), fl(Yre), fl(Yim),
                         [pxr.r, pxi.r, Kre.r, Kim.r], Yre.r, Yim.r)
                for sub in range(0, nb, 4):
                    ns = min(4, nb - sub)
                    pd = bank()
                    for j in range(ns):
                        S.op("pe", lambda e: e.matmul(pd.t[0:65, j * 128:(j + 1) * 128], lhsT=Yre.t[:, sub + j, :], rhs=CS1.t[:],
                                                      start=True, stop=False), reads=[Yre.r, CS1.r], writes=[pd.r])
                        S.op("pe", lambda e: e.matmul(pd.t[0:65, j * 128:(j + 1) * 128], lhsT=Yim.t[:, sub + j, :], rhs=CS2.t[:],
                                                      start=False, stop=True), reads=[Yim.r, CS2.r], writes=[pd.r])
                    pv_ = pd.t[0:65, 0:ns * 128].rearrange("p (c r n) -> p c r n", c=ns, r=2)
                    tw = lambda t: t.t[:, :].unsqueeze(1).broadcast_to([65, ns, 64])
                    sv = lambda t: t.t[0:65, 0:ns * 64].rearrange("p (c n) -> p c n", c=ns)
                    cmul(sv, pv_[:, :, 0, :], pv_[:, :, 1, :], tw(TWir), tw(TWii),
                         Bpre.t[:, sub:sub + ns, :], Bpim.t[:, sub:sub + ns, :], [pd.r, TWir.r, TWii.r], Bpre.r, Bpim.r)
                yo = ycs[bi_ % 2]
                for grp in range(0, nb, 8):
                    ng = min(8, nb - grp)
                    ncol = ng * 64
                    py = bank()
                    S.op("pe", lambda e: e.matmul(py.t[0:64, 0:ncol], lhsT=Gr.t[:], rhs=Bpre.t[:, grp:grp + ng, :].rearrange("p c n -> p (c n)"),
                                                  start=True, stop=False), reads=[Gr.r, Bpre.r], writes=[py.r])
                    S.op("pe", lambda e: e.matmul(py.t[0:64, 0:ncol], lhsT=nGi.t[:], rhs=Bpim.t[:, grp:grp + ng, :].rearrange("p c n -> p (c n)"),
                                                  start=False, stop=True), reads=[nGi.r, Bpim.r], writes=[py.r])
                    S.op("act", lambda e: e.copy(yo.t[:, grp:grp + ng, :].rearrange("p c n -> p (c n)"), py.t[0:64, 0:ncol]),
                         reads=[py.r], writes=[yo.r])
                S.dma(self.ycT[c0:c0 + nb, :].rearrange("c (a b) -> a c b", b=64), yo.t[:, 0:nb, :], reads=[yo.r],
                      writes=[self.ycT.r(c0)], q="pool")
            S.barrier()
            ycres = allk(self.ycT)
            ych = self.sb(st, [128, T], name="ych")
            ybf = self.sb(st, [128, T], BF16, name="ybfh")
            for cc in range(2):
                S.dma(ych.t[:, 0:L], self.ycT[cc * 128:(cc + 1) * 128, :], reads=ycres, writes=[ych.r])
                if need_ctx:
                    acc = [self.sb(st, [128, C], name="acc") for _ in range(2)]
                    for s_ in range(C):
                        a = acc[s_ % 2]
                        ks = kext[cc].t[:, 255 - s_:511 - s_]
                        us = usb[cc].t[:, L + s_:L + s_ + 1]
                        if s_ < 2:
                            S.op("dve", lambda e: e.tensor_scalar_mul(a.t[:], ks, us), reads=[kext[cc].r, usb[cc].r], writes=[a.r])
                        else:
                            S.op("dve", lambda e: e.scalar_tensor_tensor(a.t[:], ks, us, a.t[:], ALU.mult, ALU.add),
                                 reads=[kext[cc].r, usb[cc].r, a.r], writes=[a.r])
                    S.op("pool", lambda e: e.tensor_tensor(ych.t[:, L:T], acc[0].t[:], acc[1].t[:], ALU.add),
                         reads=[acc[0].r, acc[1].r], writes=[ych.r])
                else:
                    S.op("pool", lambda e: e.memset(ych.t[:, L:T], 0.0), writes=[ych.r])
                S.op("dve", lambda e: e.scalar_tensor_tensor(ych.t[:], usb[cc].t[:], skipc.t[:, cc:cc + 1], ych.t[:], ALU.mult, ALU.add),
                     reads=[usb[cc].r, skipc.r, ych.r], writes=[ych.r])
                S.op("pool", lambda e: e.tensor_tensor(ybf.t[:], ych.t[:], x0sb[cc].t[:], ALU.mult), reads=[ych.r, x0sb[cc].r], writes=[ybf.r])
                S.dma(self.yT[256 + cc * 128:256 + (cc + 1) * 128, :], ybf.t[:], reads=[ybf.r], writes=[self.yT.r(("hy", cc))])

    def phase_final(self):
        S, w = self.S, self.w
        with ExitStack() as st:
            G = self.sb(st, [128, D])
            S.dma(G.t[:], w["final_g"].h.ap().rearrange("(o n) -> o n", o=1).broadcast_to([128, D]), writes=[G.r])
            xt = [self.sb(st, [128, D]) for _ in range(3)]
            junk = self.sb(st, [128, D])
            ss = [self.sb(st, [128, 1]) for _ in range(3)]
            epst = self.sb(st, [128, 1])
            S.op("pool", lambda e: e.memset(epst.t[:], EPS), writes=[epst.r])
            for i in range(L // 128):
                x = xt[i % 3]; s = ss[i % 3]
                S.dma(x.t[:], self.xs[i * 128:(i + 1) * 128, :], reads=[self.xs.r(i)], writes=[x.r])
                S.op("act", lambda e: e.activation(junk.t[:], x.t[:], AF.Square, accum_out=s.t[:]), reads=[x.r], writes=[junk.r, s.r])
                S.op("act", lambda e: e.activation(s.t[:], s.t[:], AF.Sqrt, bias=epst.t[:], scale=1.0 / D), reads=[s.r, epst.r], writes=[s.r])
                S.op("dve", lambda e: e.reciprocal(s.t[:], s.t[:]), reads=[s.r], writes=[s.r])
                S.op("dve", lambda e: e.scalar_tensor_tensor(x.t[:], x.t[:], s.t[:, 0:1], G.t[:], ALU.mult, ALU.mult),
                     reads=[x.r, s.r, G.r], writes=[x.r])
                S.dma(self.out[i * 128:(i + 1) * 128, :], x.t[:], reads=[x.r], writes=[self.out.r(i)], q="pool")

    def build(self, consts):
        self.declare(consts)
        with self.es as es:
            self.S = Sched(self.nc, es)
            upto = self.upto
            self.phase_mod()
            self.S.barrier()
            done = False
            for l in range(self.nlayers):
                need_ctx = l < DEPTH - 1
                steps = [("ffn1", lambda: self.phase_ffn(l, 0, first=(l == 0))),
                         ("inproj", lambda: self.phase_inproj(l)),
                         ("lru", lambda: self.phase_lru(l)),
                         ("attn", lambda: self.phase_attn(l, need_ctx)),
                         ("hyena", lambda: self.phase_hyena(l, need_ctx)),
                         ("outproj", lambda: self.phase_outproj(l, need_ctx)),
                         ("ffn2", lambda: self.phase_ffn(l, 1, first=False, last_layer_ctx_skip=not need_ctx))]
                for nm, fn in steps:
                    if l == self.nlayers - 1 and upto in ("attn", "hyena") and nm in ("lru", "attn", "hyena") and nm != upto:
                        continue
                    fn()
                    self.S.barrier()
                    if l == self.nlayers - 1 and upto == nm:
                        done = True
                        break
                if done:
                    break
            if not done:
                self.phase_final()
            self.S.finish()
        return self.nc


_CST = None


def _get_consts():
    global _CST
    if _CST is None:
        _CST = _consts()
    return _CST


def make_in_maps(inputs, ncores=8):
    cst = _get_consts()
    maps = []
    shared = {k: np.ascontiguousarray(np.asarray(v, dtype=np.float32)) for k, v in inputs.items()
              if k not in ("x", "c", "ctx")}
    for b in range(ncores):
        m = dict(shared)
        m["x"] = np.ascontiguousarray(inputs["x"][b], dtype=np.float32)
        m["ctx"] = np.ascontiguousarray(inputs["ctx"][b], dtype=np.float32)
        m["c"] = np.ascontiguousarray(inputs["c"][b], dtype=np.float32)
        for k, v in cst.items():
            m["k_" + k] = v
        maps.append(m)
    return maps


def kernel(**inputs):
    cst = _get_consts()
    bld = Builder()
    nc = bld.build(cst)
    maps = make_in_maps(inputs, 8)
    res = run_bass_kernel_spmd(nc, maps, core_ids=list(range(8)))
    return np.stack([np.asarray(r["out"], dtype=np.float32) for r in res.results], axis=0)
```

```python
import math
import numpy as np
from contextlib import ExitStack
import concourse.bass as bass
import concourse.mybir as mybir
from concourse.bass_utils import run_bass_kernel_spmd

F32 = mybir.dt.float32
BF16 = mybir.dt.bfloat16
AF = mybir.ActivationFunctionType
ALU = mybir.AluOpType
AX = mybir.AxisListType

D = 1024
L = 4096
C = 256
T = L + C
NT = T // 128
DFF = 2816
NF = DFF // 128
DEPTH = 2
NMOD = 9
DIN = 2048
EPS = 1e-6
NFFT = 8192


class Res:
    __slots__ = ("w", "r")

    def __init__(self):
        self.w = None
        self.r = {}


class Lane:
    __slots__ = ("sem", "n")

    def __init__(self, sem):
        self.sem = sem
        self.n = 0


class Sched:
    def __init__(self, nc, es, lanes_sp=12, lanes_pool=6, lanes_act=2):
        self.nc = nc
        self.eng = {"pe": nc.tensor, "act": nc.scalar, "dve": nc.vector,
                    "pool": nc.gpsimd, "sp": nc.sync}
        self.sem = {}
        self.cnt = {}
        for k in ("pe", "act", "dve", "pool"):
            self.sem[k] = es.enter_context(nc.semaphore("s_" + k))
            self.cnt[k] = 0
        self.lanes = {}
        self.lane_rr = {}
        for q, n in (("sp", lanes_sp), ("pool", lanes_pool), ("act", lanes_act)):
            self.lanes[q] = [Lane(es.enter_context(nc.semaphore("d_%s%d" % (q, i)))) for i in range(n)]
            self.lane_rr[q] = 0
        self.seen = {k: {} for k in self.eng}
        self.nwaits = 0
        self.nins = 0

    def _wait(self, E, deps):
        need = {}
        for tok in deps:
            kind, key, val = tok
            if kind == "e" and key == E:
                if E == "pe":
                    continue
                if self.cnt[E] - val >= 2:
                    continue
            k = (kind, key)
            if need.get(k, 0) < val:
                need[k] = val
        seen = self.seen[E]
        for k, val in need.items():
            if seen.get(k, 0) >= val:
                continue
            seen[k] = val
            sem = self.sem[k[1]] if k[0] == "e" else k[1].sem
            self.eng[E].wait_ge(sem, val)
            self.nwaits += 1

    @staticmethod
    def _deps(reads, writes):
        deps = set()
        for r in reads:
            if r.w is not None:
                deps.add(r.w)
        for w in writes:
            if w.w is not None:
                deps.add(w.w)
            deps.update(w.r.values())
        return deps

    @staticmethod
    def _commit(tok, skey, reads, writes):
        for r in reads:
            r.r[skey] = tok
        for w in writes:
            w.w = tok
            w.r = {}

    def op(self, E, fn, reads=(), writes=()):
        self._wait(E, self._deps(reads, writes))
        ins = fn(self.eng[E])
        self.cnt[E] += 1
        ins.then_inc(self.sem[E], 1)
        self._commit(("e", E, self.cnt[E]), E, reads, writes)
        self.nins += 1
        return ins

    def dma(self, out, in_, reads=(), writes=(), q="sp", **kw):
        lanes = self.lanes[q]
        lane = lanes[self.lane_rr[q] % len(lanes)]
        self.lane_rr[q] += 1
        deps = self._deps(reads, writes)
        if lane.n:
            deps.add(("d", lane, 16 * lane.n))
        self._wait(q, deps)
        ins = self.eng[q].dma_start(out=out, in_=in_, **kw)
        ins.then_inc(lane.sem, 16)
        lane.n += 1
        self._commit(("d", lane, 16 * lane.n), lane, reads, writes)
        self.nins += 1
        return ins

    def barrier(self):
        deps = set()
        for k in self.cnt:
            if self.cnt[k]:
                deps.add(("e", k, self.cnt[k]))
        for q in self.lanes:
            for ln in self.lanes[q]:
                if ln.n:
                    deps.add(("d", ln, 16 * ln.n))
        for E in ("pe", "act", "dve", "pool", "sp"):
            self._wait(E, deps)

    def finish(self):
        deps = set()
        for k in self.cnt:
            if self.cnt[k]:
                deps.add(("e", k, self.cnt[k]))
        for q in self.lanes:
            for ln in self.lanes[q]:
                if ln.n:
                    deps.add(("d", ln, 16 * ln.n))
        self._wait("sp", deps)


class Tl:
    def __init__(self, t, nslots=1):
        self.t = t
        self.rs = [Res() for _ in range(nslots)]

    @property
    def r(self):
        return self.rs[0]

    def __getitem__(self, k):
        return self.t[k]


class Dr:
    def __init__(self, h):
        self.h = h
        self.res = {}

    def r(self, key=0):
        if key not in self.res:
            self.res[key] = Res()
        return self.res[key]

    def __getitem__(self, k):
        return self.h[k]


def _consts():
    cst = {}
    cst["ident"] = np.eye(128, dtype=np.float32)
    inv = 10000.0 ** (-np.arange(16, dtype=np.float64) / 16)
    t = np.arange(L)
    ang = np.concatenate([(t // 64)[:, None] * inv, (t % 64)[:, None] * inv], axis=1)
    cosf = np.ones((128, T), np.float64)
    sinf = np.zeros((128, T), np.float64)
    for p in range(128):
        pp = (p % 64) % 32
        cosf[p, :L] = np.cos(ang[:, pp])
        sinf[p, :L] = np.sin(ang[:, pp])
    cst["ropeC"] = cosf.astype(np.float32)
    cst["ropeS"] = sinf.astype(np.float32)
    j = np.arange(128)[:, None]
    q = np.arange(128)[None, :]
    cst["mprev"] = np.tile((q <= j).astype(np.float32), (1, 4))
    cst["mnext"] = np.tile((j <= q).astype(np.float32), (1, 4))

    def zfeat(Lh):
        tt = np.linspace(0.0, 1.0, Lh)[:, None]
        w = 2.0 * math.pi * np.arange(Lh)[:, None] / Lh
        f = np.linspace(1e-4, 15, 16)[None, :]
        z = np.concatenate([tt, np.cos(f * w), -np.sin(f * w)], axis=-1)
        max_decay = math.log(1e-2) / 0.3
        min_decay = math.log(1e-2) / 1.5
        deltas = np.abs(np.linspace(min_decay, max_decay, 256))
        dec = np.exp(-tt * deltas[None, :])
        return z, dec
    z, dec = zfeat(L)
    zf = np.zeros((NFFT, 33)); df = np.zeros((NFFT, 256))
    zf[:L] = z; df[:L] = dec
    zf[L + 1:] = z[1:][::-1]; df[L + 1:] = dec[1:][::-1]
    cst["hy_z"] = np.ascontiguousarray(zf.T).astype(np.float32)
    cst["hy_dec"] = np.ascontiguousarray(df.T).astype(np.float32)
    zc, decc = zfeat(C)
    ze = np.zeros((512, 33)); de = np.zeros((512, 256))
    for m in range(511):
        lag = abs(m - 255)
        ze[m] = zc[lag]; de[m] = decc[lag]
    cst["hy_zc"] = np.ascontiguousarray(ze.T).astype(np.float32)
    cst["hy_decc"] = np.ascontiguousarray(de.T).astype(np.float32)
    n1 = np.arange(128)[:, None]; f1 = np.arange(65)[None, :]
    a = 2 * math.pi * n1 * f1 / 128
    cst["f_WA"] = np.concatenate([np.cos(a), -np.sin(a)], axis=1).astype(np.float32)
    n2 = np.arange(64)[:, None]
    a = 2 * math.pi * n2 * f1 / NFFT
    cst["f_TWr"] = np.cos(a).astype(np.float32)
    cst["f_TWi"] = (-np.sin(a)).astype(np.float32)
    f2 = np.arange(64)[None, :]
    a = 2 * math.pi * n2 * f2 / 64
    cst["f_C64"] = np.cos(a).astype(np.float32)
    cst["f_S64"] = np.sin(a).astype(np.float32)
    cst["f_nS64"] = (-np.sin(a)).astype(np.float32)
    cst["f_CS1"] = np.concatenate([np.cos(a), np.sin(a)], axis=1).astype(np.float32)
    cst["f_CS2"] = np.concatenate([-np.sin(a), np.cos(a)], axis=1).astype(np.float32)
    f1c = np.arange(65)[:, None]; n2r = np.arange(64)[None, :]
    a = 2 * math.pi * f1c * n2r / NFFT
    cst["f_TWir"] = np.cos(a).astype(np.float32)
    cst["f_TWii"] = np.sin(a).astype(np.float32)
    g = np.full((65, 1), 2.0); g[0] = 1.0; g[64] = 1.0
    n1r = np.arange(64)[None, :]
    a = 2 * math.pi * f1c * n1r / 128
    cst["f_Gr"] = (g * np.cos(a) / NFFT).astype(np.float32)
    cst["f_nGi"] = (-g * np.sin(a) / NFFT).astype(np.float32)
    return cst


_CONST_SHAPES = None


class Builder:
    def __init__(self, debug=None, nlayers=DEPTH, upto=None):
        self.debug = debug or []
        self.nlayers = nlayers
        self.upto = upto
        self.nc = bass.Bass("TRN2", target_bir_lowering=False)
        self.es = ExitStack()
        self.S = None
        self.n_sb = 0

    def din(self, name, shape, dt=F32):
        return Dr(self.nc.dram_tensor(name, list(shape), dt, kind="ExternalInput"))

    def dscr(self, name, shape, dt=F32):
        kind = "ExternalOutput" if name in self.debug else "Internal"
        return Dr(self.nc.dram_tensor(name, list(shape), dt, kind=kind))

    def sb(self, st, shape, dt=F32, nslots=1, name=None):
        self.n_sb += 1
        t = st.enter_context(self.nc.sbuf_tensor("%s_%d" % (name or "t", self.n_sb), list(shape), dt))
        return Tl(t, nslots)

    def ps(self, st, shape, dt=F32, name=None):
        self.n_sb += 1
        t = st.enter_context(self.nc.psum_tensor("%s_%d" % (name or "p", self.n_sb), list(shape), dt))
        return Tl(t, 1)

    def dump(self, name, ap, shape, reads, dt=F32):
        if name not in self.debug:
            return
        d = self.nc.dram_tensor(name, list(shape), dt, kind="ExternalOutput")
        self.S.dma(d.ap(), ap, reads=reads)

    def declare(self, consts):
        w = {}
        w["x"] = self.din("x", [L, D])
        w["ctx"] = self.din("ctx", [C, D])
        w["c"] = self.din("c", [D])
        w["c_ctx"] = self.din("c_ctx", [D])
        shp = dict(w_mod=[DEPTH, D, NMOD * D], b_mod=[DEPTH, NMOD * D], norm_g=[DEPTH, 3, D],
                   ffn_w1=[DEPTH, 2, D, 2 * DFF], ffn_w2=[DEPTH, 2, DFF, D], w_in=[DEPTH, D, DIN],
                   w_out=[DEPTH, D, D], lru_conv_w=[DEPTH, 4, 256], lru_conv_b=[DEPTH, 256],
                   lru_wa=[DEPTH, 2, 4, 64, 64], lru_ba=[DEPTH, 2, 256], lru_wx=[DEPTH, 2, 4, 64, 64],
                   lru_bx=[DEPTH, 2, 256], lru_lam=[DEPTH, 2, 256], hy_conv_w=[DEPTH, 3, 768],
                   hy_conv_b=[DEPTH, 768], hy_fw0=[DEPTH, 33, 64], hy_fb0=[DEPTH, 64],
                   hy_fw_in=[DEPTH, 2, 64, 64], hy_fb_in=[DEPTH, 2, 64], hy_freq=[DEPTH, 64],
                   hy_fw_last=[DEPTH, 64, 512], hy_skip=[DEPTH, 256], attn_sink=[DEPTH, 8], final_g=[D])
        for k, s in shp.items():
            w[k] = self.din(k, s)
        for k, v in consts.items():
            w[k] = self.din("k_" + k, v.shape)
        self.w = w
        self.out = Dr(self.nc.dram_tensor("out", [L, D], F32, kind="ExternalOutput"))
        self.xs = self.dscr("xs", [T, D])
        self.modD = self.dscr("modD", [DEPTH, 2, NMOD * D])
        self.projT = self.dscr("projT", [1280, T])
        self.qT = self.dscr("qT", [4, 128, T], BF16)
        self.kT = self.dscr("kT", [128, T], BF16)
        self.vS = self.dscr("vS", [T, 128], BF16)
        self.yT = self.dscr("yT", [D, T], BF16)
        self.uT = self.dscr("uT", [256, T])
        self.x0T = self.dscr("x0T", [256, T])
        self.ycT = self.dscr("ycT", [256, L])
        self.kfT = self.dscr("kfT", [256, NFFT])
        self.KfD = self.dscr("KfD", [2, 64, 256, 65])

    def col_load(self, st, dst, src_ap_1d, n, q="sp"):
        S = self.S
        S.dma(dst.t[:, 0:n], src_ap_1d.rearrange("(k p) -> p k", p=128), writes=[dst.r], q=q,
              allow_slow_non_contiguous=True)

    def phase_mod(self):
        S, nc, w = self.S, self.nc, self.w
        with ExitStack() as st:
            sT = self.sb(st, [128, 8, 2])
            cin = self.sb(st, [128, 16])
            S.dma(cin.t[:, 0:8], w["c"].h.ap().rearrange("(k p) -> p k", p=128), writes=[cin.r],
                  allow_slow_non_contiguous=True)
            S.dma(cin.t[:, 8:16], w["c_ctx"].h.ap().rearrange("(k p) -> p k", p=128), writes=[cin.r],
                  allow_slow_non_contiguous=True)
            S.op("act", lambda e: e.activation(sT.t[:, :, 0], cin.t[:, 0:8], AF.Silu), reads=[cin.r], writes=[sT.r])
            S.op("act", lambda e: e.activation(sT.t[:, :, 1], cin.t[:, 8:16], AF.Silu), reads=[cin.r], writes=[sT.r])
            wt = [self.sb(st, [128, 8, 512]) for _ in range(2)]
            pm = [self.ps(st, [128, 512]) for _ in range(2)]
            bt = self.sb(st, [2, NMOD * D])
            mo = self.sb(st, [2, NMOD * D])
            it = 0
            for l in range(self.nlayers):
                S.dma(bt.t[:], w["b_mod"][l:l + 1, :].broadcast_to([2, NMOD * D]), writes=[bt.r])
                wsrc = w["w_mod"][l].rearrange("(k p) n -> p k n", p=128)
                for cg in range(NMOD * D // 512):
                    wb = wt[it % 2]; pp = pm[it % 2]; it += 1
                    S.dma(wb.t[:], wsrc[:, :, cg * 512:(cg + 1) * 512], writes=[wb.r],
                          q=("sp" if cg % 2 == 0 else "pool"))
                    for k in range(8):
                        S.op("pe", lambda e: e.matmul(pp.t[0:2, :], lhsT=sT.t[:, k, :], rhs=wb.t[:, k, :],
                                                      start=(k == 0), stop=(k == 7)),
                             reads=[sT.r, wb.r], writes=[pp.r])
                    S.op("dve", lambda e: e.tensor_tensor(mo.t[:, cg * 512:(cg + 1) * 512], pp.t[0:2, :],
                                                          bt.t[:, cg * 512:(cg + 1) * 512], ALU.add),
                         reads=[pp.r, bt.r], writes=[mo.r])
                S.dma(self.modD[l], mo.t[:], reads=[mo.r], writes=[self.modD.r(l)])

    def load_mod_tiles(self, l, which, i_shift, i_scale, i_gate, i_norm, gate_mul, G, Sh, Ga, tmp):
        S, w = self.S, self.w
        md = self.modD
        def bc(i):
            return md[l, which:which + 1, i * D:(i + 1) * D].broadcast_to([128, D])
        S.dma(Sh.t[:], bc(i_shift), reads=[md.r(l)], writes=[Sh.r])
        S.dma(tmp.t[:], bc(i_scale), reads=[md.r(l)], writes=[tmp.r])
        S.dma(G.t[:], w["norm_g"][l, i_norm:i_norm + 1, :].broadcast_to([128, D]), writes=[G.r])
        S.op("dve", lambda e: e.scalar_tensor_tensor(G.t[:], tmp.t[:], 1.0, G.t[:], ALU.add, ALU.mult),
             reads=[tmp.r, G.r], writes=[G.r])
        if Ga is not None:
            S.dma(Ga.t[:], bc(i_gate), reads=[md.r(l)], writes=[Ga.r])
            if gate_mul != 1.0:
                S.op("pool", lambda e: e.tensor_scalar_mul(Ga.t[:], Ga.t[:], gate_mul), reads=[Ga.r], writes=[Ga.r])

    def norm_group(self, xt, ntile, G, Sh, hb, ss, rstd, junk, epst):
        S = self.S
        for i in range(ntile):
            S.op("act", lambda e: e.activation(junk.t[:], xt.t[:, i, :], AF.Square, accum_out=ss.t[:, i:i + 1]),
                 reads=[xt.rs[i]], writes=[junk.r, ss.r])
        S.op("act", lambda e: e.activation(rstd.t[:, 0:ntile], ss.t[:, 0:ntile], AF.Sqrt, bias=epst.t[:], scale=1.0 / D),
             reads=[ss.r, epst.r], writes=[rstd.r])
        S.op("dve", lambda e: e.reciprocal(rstd.t[:, 0:ntile], rstd.t[:, 0:ntile]), reads=[rstd.r], writes=[rstd.r])
        for i in range(ntile):
            S.op("dve", lambda e: e.scalar_tensor_tensor(junk.t[:], xt.t[:, i, :], rstd.t[:, i:i + 1], G.t[:],
                                                         ALU.mult, ALU.mult),
                 reads=[xt.rs[i], rstd.r, G.r], writes=[junk.r])
            S.op("pool", lambda e: e.tensor_tensor(hb.t[:, i, :], junk.t[:], Sh.t[:], ALU.add),
                 reads=[junk.r, Sh.r], writes=[hb.rs[i]])

    def transpose_group(self, hb, ntile, hT, ptr, ident):
        S = self.S
        for i in range(ntile):
            p = ptr[i % len(ptr)]
            for k in range(8):
                S.op("pe", lambda e: e.transpose(p.t[:, k, :], hb.t[:, i, k * 128:(k + 1) * 128], ident.t[:]),
                     reads=[hb.rs[i], ident.r], writes=[p.r])
            eng = "act" if i % 2 == 0 else "dve"
            if eng == "act":
                S.op("act", lambda e: e.copy(hT.t[:, :, i * 128:(i + 1) * 128], p.t[:]), reads=[p.r], writes=[hT.r])
            else:
                S.op("dve", lambda e: e.tensor_copy(hT.t[:, :, i * 128:(i + 1) * 128], p.t[:]), reads=[p.r], writes=[hT.r])

    def load_ident(self, st):
        S = self.S
        idf = self.sb(st, [128, 128])
        ident = self.sb(st, [128, 128], BF16)
        S.dma(idf.t[:], self.w["ident"].h.ap(), writes=[idf.r])
        S.op("dve", lambda e: e.tensor_copy(ident.t[:], idf.t[:]), reads=[idf.r], writes=[ident.r])
        return ident

    def cast_weight(self, dst_ap, dst_res, src_ap, stg, idx, ncols, scale=None):
        S = self.S
        sl = stg[idx % len(stg)]
        S.dma(sl.t[:, 0:ncols], src_ap, writes=[sl.r], q=("sp" if idx % 2 == 0 else "pool"))
        eng = ("dve", "act", "pool")[idx % 3]
        if eng == "act":
            S.op("act", lambda e: e.copy(dst_ap, sl.t[:, 0:ncols]), reads=[sl.r], writes=[dst_res])
        else:
            S.op(eng, lambda e: e.tensor_copy(dst_ap, sl.t[:, 0:ncols]), reads=[sl.r], writes=[dst_res])

    def phase_ffn(self, l, j, first, last_layer_ctx_skip=False):
        S, nc, w = self.S, self.nc, self.w
        GT = 256
        i_shift, i_scale, i_gate, i_norm = (0, 1, 2, 0) if j == 0 else (6, 7, 8, 2)
        with ExitStack() as st:
            w1b = self.sb(st, [128, 8, 2 * DFF], BF16, name="w1b")
            w2b = self.sb(st, [128, NF, D], BF16, name="w2b")
            stg = [self.sb(st, [128, 1024], name="stg") for _ in range(2)]
            ident = self.load_ident(st)
            ci = 0
            w1src = w["ffn_w1"][l, j].rearrange("(k p) n -> p k n", p=128)
            for k in range(8):
                for c0 in range(0, 2 * DFF, 1024):
                    nco = min(1024, 2 * DFF - c0)
                    self.cast_weight(w1b.t[:, k, c0:c0 + nco], w1b.r, w1src[:, k, c0:c0 + nco], stg, ci, nco)
                    ci += 1
            w2src = w["ffn_w2"][l, j].rearrange("(f p) n -> p f n", p=128)
            for f in range(NF):
                self.cast_weight(w2b.t[:, f, :], w2b.r, w2src[:, f, :], stg, ci, D)
                ci += 1
            G = self.sb(st, [128, D]); Sh = self.sb(st, [128, D]); Ga = self.sb(st, [128, D])
            xts = [self.sb(st, [128, 2, D], nslots=2, name="xt") for _ in range(2)]
            hbs = [self.sb(st, [128, 2, D], BF16, nslots=2, name="hb") for _ in range(2)]
            junk = self.sb(st, [128, D], name="junk")
            hTs = [self.sb(st, [128, 8, GT], BF16, name="hT") for _ in range(2)]
            gT = self.sb(st, [128, NF, GT], BF16, nslots=NF, name="gT")
            sil = [self.sb(st, [128, GT], name="sil") for _ in range(2)]
            sss = [self.sb(st, [128, 2]) for _ in range(2)]; rstds = [self.sb(st, [128, 2]) for _ in range(2)]
            epst = self.sb(st, [128, 1])
            S.op("pool", lambda e: e.memset(epst.t[:], EPS), writes=[epst.r])
            ptr = [self.ps(st, [128, 8, 128], BF16, name="ptr") for _ in range(1)]
            pab = [self.ps(st, [128, 2, GT], name="pab") for _ in range(3)]
            po = [self.ps(st, [128, 512], name="po") for _ in range(4)]
            ngroups = T // GT
            if last_layer_ctx_skip:
                ngroups = L // GT
            cur_which = [None]

            def load_x(g):
                xt = xts[g % 2]; t0 = g * GT
                for i in range(2):
                    r0 = t0 + i * 128
                    if first:
                        src = w["x"][r0:r0 + 128, :] if r0 < L else w["ctx"][r0 - L:r0 - L + 128, :]
                        S.dma(xt.t[:, i, :], src, writes=[xt.rs[i]])
                    else:
                        S.dma(xt.t[:, i, :], self.xs[r0:r0 + 128, :], reads=[self.xs.r(r0 // 128)], writes=[xt.rs[i]])

            def norm(g):
                which = 0 if g * GT < L else 1
                if which != cur_which[0]:
                    self.load_mod_tiles(l, which, i_shift, i_scale, i_gate, i_norm, 0.5, G, Sh, Ga, junk)
                    cur_which[0] = which
                self.norm_group(xts[g % 2], 2, G, Sh, hbs[g % 2], sss[g % 2], rstds[g % 2], junk, epst)

            def transp(g):
                self.transpose_group(hbs[g % 2], 2, hTs[g % 2], ptr, ident)

            def stage1(g):
                hT = hTs[g % 2]
                for f in range(NF):
                    pb = pab[f % 3]
                    for half in range(2):
                        col = half * DFF + f * 128
                        for k in range(8):
                            S.op("pe", lambda e: e.matmul(pb.t[:, half, :], lhsT=w1b.t[:, k, col:col + 128],
                                                          rhs=hT.t[:, k, :], start=(k == 0), stop=(k == 7)),
                                 reads=[w1b.r, hT.r], writes=[pb.r])
                    sl = sil[f % 2]
                    S.op("act", lambda e: e.activation(sl.t[:], pb.t[:, 0, :], AF.Silu), reads=[pb.r], writes=[sl.r])
                    S.op("dve", lambda e: e.tensor_tensor(gT.t[:, f, :], sl.t[:], pb.t[:, 1, :], ALU.mult),
                         reads=[sl.r, pb.r], writes=[gT.rs[f]])

            def stage2(g):
                xt = xts[g % 2]; t0 = g * GT
                for i in range(2):
                    for dh in range(2):
                        pp = po[(i * 2 + dh) % 4]
                        for f in range(NF):
                            S.op("pe", lambda e: e.matmul(pp.t[:], lhsT=gT.t[:, f, i * 128:(i + 1) * 128],
                                                          rhs=w2b.t[:, f, dh * 512:(dh + 1) * 512],
                                                          start=(f == 0), stop=(f == NF - 1)),
                                 reads=[gT.rs[f], w2b.r], writes=[pp.r])
                        S.op("dve", lambda e: e.tensor_tensor(junk.t[:, dh * 512:(dh + 1) * 512], pp.t[:],
                                                              Ga.t[:, dh * 512:(dh + 1) * 512], ALU.mult),
                             reads=[pp.r, Ga.r], writes=[junk.r])
                        S.op("pool", lambda e: e.tensor_tensor(xt.t[:, i, dh * 512:(dh + 1) * 512],
                                                               xt.t[:, i, dh * 512:(dh + 1) * 512],
                                                               junk.t[:, dh * 512:(dh + 1) * 512], ALU.add),
                             reads=[junk.r, xt.rs[i]], writes=[xt.rs[i]])
                    r0 = t0 + i * 128
                    S.dma(self.xs[r0:r0 + 128, :], xt.t[:, i, :], reads=[xt.rs[i]], writes=[self.xs.r(r0 // 128)], q="pool")

            def is_switch(g):
                return g < ngroups and (0 if g * GT < L else 1) != cur_which[0] and cur_which[0] is not None
            load_x(0); norm(0); transp(0)
            for g in range(ngroups):
                nxt = g + 1 < ngroups
                sw = nxt and is_switch(g + 1)
                if nxt:
                    load_x(g + 1)
                    if not sw:
                        norm(g + 1)
                stage1(g)
                if nxt and not sw:
                    transp(g + 1)
                stage2(g)
                if sw:
                    norm(g + 1); transp(g + 1)

    def phase_inproj(self, l):
        S, nc, w = self.S, self.nc, self.w
        GT = 512
        NCOL = 2688
        with ExitStack() as st:
            wb = self.sb(st, [128, 8, NCOL], BF16, name="winb")
            stg = [self.sb(st, [128, DIN], name="stgi") for _ in range(2)]
            ident = self.load_ident(st)
            wsrc = w["w_in"][l].rearrange("(k p) n -> p k n", p=128)
            engs = ("dve", "pool")
            ne = 0
            for k in range(8):
                sl = stg[k % 2]
                S.dma(sl.t[:], wsrc[:, k, :], writes=[sl.r], q=("sp" if k % 2 == 0 else "pool"))
                def cp(dst, src, neg=False):
                    nonlocal ne
                    E = engs[ne % 2]; ne += 1
                    if neg:
                        S.op(E, lambda e: e.tensor_scalar_mul(dst, src, -1.0), reads=[sl.r], writes=[wb.r])
                    else:
                        S.op(E, lambda e: e.tensor_copy(dst, src), reads=[sl.r], writes=[wb.r])
                S.op("act", lambda e: e.copy(wb.t[:, k, 0:1280], sl.t[:, 0:1280]), reads=[sl.r], writes=[wb.r])
                cp(wb.t[:, k, 1792:2048], sl.t[:, 1792:2048])
                qs = sl.t[:, 1280:1792].rearrange("p (half ch d) -> p ch half d", half=2, ch=4)
                cp(wb.t[:, k, 1280:1792].rearrange("p (ch half d) -> p ch half d", ch=4, half=2), qs)
                qs2 = sl.t[:, 1280:1792].rearrange("p (half ch two d) -> p ch half two d", half=2, ch=4, two=2)
                qd2 = wb.t[:, k, 2048:2560].rearrange("p (ch half two d) -> p ch half two d", ch=4, half=2, two=2)
                cp(qd2[:, :, :, 0, :], qs2[:, :, :, 1, :], neg=True)
                cp(qd2[:, :, :, 1, :], qs2[:, :, :, 0, :])
                ks2 = sl.t[:, 1792:1920].rearrange("p (g two d) -> p g two d", g=2, two=2)
                kd2 = wb.t[:, k, 2560:2688].rearrange("p (g two d) -> p g two d", g=2, two=2)
                cp(kd2[:, :, 0, :], ks2[:, :, 1, :], neg=True)
                cp(kd2[:, :, 1, :], ks2[:, :, 0, :])
            G = self.sb(st, [128, D]); Sh = self.sb(st, [128, D])
            xts = [self.sb(st, [128, 4, D], nslots=4, name="xti") for _ in range(2)]
            hbs = [self.sb(st, [128, 4, D], BF16, nslots=4, name="hbi") for _ in range(2)]
            junk = self.sb(st, [128, D], name="junki")
            hTs = [self.sb(st, [128, 8, GT], BF16, name="hTi") for _ in range(2)]
            sss = [self.sb(st, [128, 4]) for _ in range(2)]; rstds = [self.sb(st, [128, 4]) for _ in range(2)]
            epst = self.sb(st, [128, 1])
            S.op("pool", lambda e: e.memset(epst.t[:], EPS), writes=[epst.r])
            rcs = [self.sb(st, [128, GT]) for _ in range(2)]; rss = [self.sb(st, [128, GT]) for _ in range(2)]
            ost = [self.sb(st, [128, GT], name="ost") for _ in range(3)]
            obf = [self.sb(st, [128, GT], BF16, name="obf") for _ in range(2)]
            t1s = [self.sb(st, [128, GT]) for _ in range(2)]; t2s = [self.sb(st, [128, GT]) for _ in range(2)]
            vbf = self.sb(st, [128, 4, 128], BF16)
            ptr = [self.ps(st, [128, 8, 128], BF16, name="ptri") for _ in range(2)]
            pp = [self.ps(st, [128, GT], name="ppi") for _ in range(5)]
            pv = self.ps(st, [128, 4, 128], name="pvi")
            groups = [(g * GT, 4) for g in range(L // GT)] + [(L, 2)]
            cur_which = [None]
            ipc = [0]

            def load_x(gi):
                t0, ntile = groups[gi]
                n = ntile * 128
                xt = xts[gi % 2]
                for i in range(ntile):
                    r0 = t0 + i * 128
                    S.dma(xt.t[:, i, :], self.xs[r0:r0 + 128, :], reads=[self.xs.r(r0 // 128)], writes=[xt.rs[i]])
                S.dma(rcs[gi % 2].t[:, 0:n], w["ropeC"][:, t0:t0 + n], writes=[rcs[gi % 2].r], q="pool")
                S.dma(rss[gi % 2].t[:, 0:n], w["ropeS"][:, t0:t0 + n], writes=[rss[gi % 2].r], q="pool")

            def norm(gi):
                t0, ntile = groups[gi]
                which = 0 if t0 < L else 1
                if which != cur_which[0]:
                    self.load_mod_tiles(l, which, 3, 4, None, 1, 1.0, G, Sh, None, junk)
                    cur_which[0] = which
                self.norm_group(xts[gi % 2], ntile, G, Sh, hbs[gi % 2], sss[gi % 2], rstds[gi % 2], junk, epst)

            def transp(gi):
                t0, ntile = groups[gi]
                self.transpose_group(hbs[gi % 2], ntile, hTs[gi % 2], ptr, ident)

            def mms(gi):
                t0, ntile = groups[gi]
                n = ntile * 128
                hT = hTs[gi % 2]; rc = rcs[gi % 2]; rs_ = rss[gi % 2]
                def mm(pt, c0):
                    for k in range(8):
                        S.op("pe", lambda e: e.matmul(pt.t[:, 0:n], lhsT=wb.t[:, k, c0:c0 + 128], rhs=hT.t[:, k, 0:n],
                                                      start=(k == 0), stop=(k == 7)),
                             reads=[wb.r, hT.r], writes=[pt.r])
                for oc in range(10):
                    pt = pp[ipc[0] % 5]; ipc[0] += 1
                    mm(pt, oc * 128)
                    o = ost[oc % 3]
                    S.op("act", lambda e: e.copy(o.t[:, 0:n], pt.t[:, 0:n]), reads=[pt.r], writes=[o.r])
                    S.dma(self.projT[oc * 128:(oc + 1) * 128, t0:t0 + n], o.t[:, 0:n], reads=[o.r],
                          writes=[self.projT.r((oc, t0))], q="pool")
                for ch in range(5):
                    pa = pp[ipc[0] % 5]; ipc[0] += 1
                    pb_ = pp[ipc[0] % 5]; ipc[0] += 1
                    mm(pa, 1280 + ch * 128)
                    mm(pb_, 2048 + ch * 128)
                    t1 = t1s[ch % 2]; t2 = t2s[ch % 2]
                    S.op("dve", lambda e: e.tensor_tensor(t1.t[:, 0:n], pa.t[:, 0:n], rc.t[:, 0:n], ALU.mult),
                         reads=[pa.r, rc.r], writes=[t1.r])
                    S.op("dve", lambda e: e.tensor_tensor(t2.t[:, 0:n], pb_.t[:, 0:n], rs_.t[:, 0:n], ALU.mult),
                         reads=[pb_.r, rs_.r], writes=[t2.r])
                    ob = obf[ch % 2]
                    S.op("pool", lambda e: e.tensor_tensor(ob.t[:, 0:n], t1.t[:, 0:n], t2.t[:, 0:n], ALU.add),
                         reads=[t1.r, t2.r], writes=[ob.r])
                    if ch < 4:
                        S.dma(self.qT[ch, :, t0:t0 + n], ob.t[:, 0:n], reads=[ob.r], writes=[self.qT.r((ch, t0))], q="pool")
                    else:
                        S.dma(self.kT[:, t0:t0 + n], ob.t[:, 0:n], reads=[ob.r], writes=[self.kT.r(t0)], q="pool")
                for i in range(ntile):
                    for k in range(8):
                        S.op("pe", lambda e: e.matmul(pv.t[:, i, :], lhsT=hT.t[:, k, i * 128:(i + 1) * 128],
                                                      rhs=wb.t[:, k, 1920:2048], start=(k == 0), stop=(k == 7)),
                             reads=[wb.r, hT.r], writes=[pv.r])
                S.op("act", lambda e: e.copy(vbf.t[:, 0:ntile, :], pv.t[:, 0:ntile, :]), reads=[pv.r], writes=[vbf.r])
                S.dma(self.vS[t0:t0 + n, :].rearrange("(i p) d -> p i d", p=128), vbf.t[:, 0:ntile, :], reads=[vbf.r],
                      writes=[self.vS.r(t0)], q="pool")

            load_x(0); norm(0); transp(0)
            for gi in range(len(groups)):
                if gi + 1 < len(groups):
                    load_x(gi + 1); norm(gi + 1)
                mms(gi)
                if gi + 1 < len(groups):
                    transp(gi + 1)

    def vec_col(self, dst_ap, src_1d, res):
        self.S.dma(dst_ap, src_1d.rearrange("(p o) -> p o", o=1), writes=[res])

    def phase_lru(self, l):
        S, nc, w = self.S, self.nc, self.w
        with ExitStack() as st:
            big = lambda nm: self.sb(st, [128, T], name=nm)
            xl, u, gl, A, B, Cc, Df, Eb = [big(n) for n in ("xl", "u", "gl", "A", "B", "Cc", "Df", "Eb")]
            ybf = self.sb(st, [128, T], BF16, name="ybf")
            one = self.sb(st, [128, 1])
            S.op("pool", lambda e: e.memset(one.t[:], 1.0), writes=[one.r])
            pg = [self.ps(st, [128, 512], name="pg") for _ in range(4)]
            ip = 0
            for cc in range(2):
                c0 = cc * 128
                S.dma(xl.t[:], self.projT[c0:c0 + 128, :], reads=[self.projT.r(k) for k in self.projT.res], writes=[xl.r])
                S.dma(gl.t[:], self.projT[256 + c0:256 + c0 + 128, :], reads=[self.projT.r(k) for k in self.projT.res], writes=[gl.r], q="pool")
                cw = self.sb(st, [128, 4]); cb = self.sb(st, [128, 1])
                S.dma(cw.t[:], w["lru_conv_w"][l, :, c0:c0 + 128].rearrange("k p -> p k"), writes=[cw.r], allow_slow_non_contiguous=True)
                self.vec_col(cb.t[:], w["lru_conv_b"][l, c0:c0 + 128], cb.r)
                for (s0, s1) in ((0, L), (L, T)):
                    S.op("dve", lambda e: e.tensor_scalar(u.t[:, s0:s1], xl.t[:, s0:s1], cw.t[:, 2:3], cb.t[:, 0:1], ALU.mult, ALU.add),
                         reads=[xl.r, cw.r, cb.r], writes=[u.r])
                    for k, off in ((0, -2), (1, -1), (3, 1)):
                        if off < 0:
                            o_ = u.t[:, s0 - off:s1]; i_ = xl.t[:, s0:s1 + off]
                        else:
                            o_ = u.t[:, s0:s1 - off]; i_ = xl.t[:, s0 + off:s1]
                        S.op("dve", lambda e: e.scalar_tensor_tensor(o_, i_, cw.t[:, k:k + 1], o_, ALU.mult, ALU.add),
                             reads=[xl.r, cw.r, u.r], writes=[u.r])
                for d in range(2):
                    wa = self.sb(st, [128, 128]); wx = self.sb(st, [128, 128])
                    for (wt_, nm) in ((wa, "lru_wa"), (wx, "lru_wx")):
                        S.op("pool", lambda e: e.memset(wt_.t[:], 0.0), writes=[wt_.r])
                        S.dma(wt_.t[0:64, 0:64], w[nm][l, d, 2 * cc], writes=[wt_.r])
                        S.dma(wt_.t[64:128, 64:128], w[nm][l, d, 2 * cc + 1], writes=[wt_.r])
                    ba = self.sb(st, [128, 1]); bx = self.sb(st, [128, 1]); lam = self.sb(st, [128, 1])
                    self.vec_col(ba.t[:], w["lru_ba"][l, d, c0:c0 + 128], ba.r)
                    self.vec_col(bx.t[:], w["lru_bx"][l, d, c0:c0 + 128], bx.r)
                    self.vec_col(lam.t[:], w["lru_lam"][l, d, c0:c0 + 128], lam.r)
                    sp = self.sb(st, [128, 2])
                    S.op("act", lambda e: e.activation(lam.t[:], lam.t[:], AF.Exp, scale=-1.0), reads=[lam.r], writes=[lam.r])
                    S.op("act", lambda e: e.activation(lam.t[:], lam.t[:], AF.Ln, bias=one.t[:], scale=1.0), reads=[lam.r, one.r], writes=[lam.r])
                    S.op("dve", lambda e: e.tensor_scalar_mul(sp.t[:, 0:1], lam.t[:], -8.0), reads=[lam.r], writes=[sp.r])
                    S.op("dve", lambda e: e.tensor_scalar_mul(sp.t[:, 1:2], lam.t[:], -16.0), reads=[lam.r], writes=[sp.r])
                    for g0 in range(0, T, 512):
                        n = min(512, T - g0)
                        for (wt_, bt_, dst) in ((wa, ba, A), (wx, bx, B)):
                            pt = pg[ip % 4]; ip += 1
                            S.op("pe", lambda e: e.matmul(pt.t[:, 0:n], lhsT=wt_.t[:], rhs=u.t[:, g0:g0 + n], start=True, stop=True),
                                 reads=[wt_.r, u.r], writes=[pt.r])
                            S.op("act", lambda e: e.activation(dst.t[:, g0:g0 + n], pt.t[:, 0:n], AF.Sigmoid, bias=bt_.t[:], scale=1.0),
                                 reads=[pt.r, bt_.r], writes=[dst.r])
                    S.op("act", lambda e: e.activation(Cc.t[:], A.t[:], AF.Exp, scale=sp.t[:, 0:1]), reads=[A.r, sp.r], writes=[Cc.r])
                    S.op("act", lambda e: e.activation(A.t[:], A.t[:], AF.Exp, scale=sp.t[:, 1:2]), reads=[A.r, sp.r], writes=[A.r])
                    S.op("act", lambda e: e.activation(A.t[:], A.t[:], AF.Sqrt, bias=one.t[:], scale=-1.0), reads=[A.r, one.r], writes=[A.r])
                    S.op("pool", lambda e: e.tensor_tensor(B.t[:], B.t[:], u.t[:], ALU.mult), reads=[B.r, u.r], writes=[B.r])
                    S.op("pool", lambda e: e.tensor_tensor(A.t[:], A.t[:], B.t[:], ALU.mult), reads=[A.r, B.r], writes=[A.r])
                    if d == 0:
                        S.op("dve", lambda e: e.tensor_tensor_scan(Df.t[:, L:T], Cc.t[:, L:T], A.t[:, L:T], 0.0, ALU.mult, ALU.add),
                             reads=[Cc.r, A.r], writes=[Df.r])
                        S.op("dve", lambda e: e.tensor_tensor_scan(Df.t[:, 0:L], Cc.t[:, 0:L], A.t[:, 0:L], Df.t[:, T - 1:T], ALU.mult, ALU.add),
                             reads=[Cc.r, A.r, Df.r], writes=[Df.r])
                    else:
                        rv = lambda tl, a, b: tl.t[:, a:b][:, ::-1]
                        S.op("dve", lambda e: e.tensor_tensor_scan(rv(Eb, L, T), rv(Cc, L, T), rv(A, L, T), 0.0, ALU.mult, ALU.add),
                             reads=[Cc.r, A.r], writes=[Eb.r])
                        S.op("dve", lambda e: e.tensor_tensor_scan(rv(Eb, 0, L), rv(Cc, 0, L), rv(A, 0, L), Eb.t[:, L:L + 1], ALU.mult, ALU.add),
                             reads=[Cc.r, A.r, Eb.r], writes=[Eb.r])
                S.op("act", lambda e: e.activation(gl.t[:], gl.t[:], AF.Gelu), reads=[gl.r], writes=[gl.r])
                S.op("pool", lambda e: e.tensor_tensor(Df.t[:], Df.t[:], Eb.t[:], ALU.add), reads=[Df.r, Eb.r], writes=[Df.r])
                S.op("dve", lambda e: e.tensor_tensor(ybf.t[:], Df.t[:], gl.t[:], ALU.mult), reads=[Df.r, gl.r], writes=[ybf.r])
                S.dma(self.yT[c0:c0 + 128, :], ybf.t[:], reads=[ybf.r], writes=[self.yT.r(("lru", cc))])

    def phase_attn(self, l, need_ctx):
        S, nc, w = self.S, self.nc, self.w
        allk = lambda d: [d.r(k) for k in d.res]
        with ExitStack() as st:
            kTs = self.sb(st, [128, T], BF16, name="kTs")
            qTs = self.sb(st, [128, 4, T], BF16, name="qTs")
            vtmp = self.sb(st, [128, NT, 128], BF16, name="vtmp")
            vA = self.sb(st, [128, NT, 2, 65], BF16, name="vA")
            S.dma(kTs.t[:], self.kT.h.ap(), reads=allk(self.kT), writes=[kTs.r])
            for ch in range(4):
                S.dma(qTs.t[:, ch, :], self.qT[ch], reads=allk(self.qT), writes=[qTs.r], q=("sp" if ch % 2 == 0 else "pool"))
            S.dma(vtmp.t[:], self.vS.h.ap().rearrange("(i p) d -> p i d", p=128), reads=allk(self.vS), writes=[vtmp.r])
            S.op("pool", lambda e: e.memset(vA.t[:], 1.0), writes=[vA.r])
            S.op("dve", lambda e: e.tensor_copy(vA.t[:, :, :, 0:64], vtmp.t[:].rearrange("p i (g d) -> p i g d", g=2)),
                 reads=[vtmp.r, vA.r], writes=[vA.r])
            ident = self.load_ident(st)
            mf = self.sb(st, [128, 512])
            mprev = self.sb(st, [128, 512], BF16); mnext = self.sb(st, [128, 512], BF16)
            for (mt, nm) in ((mprev, "mprev"), (mnext, "mnext")):
                S.dma(mf.t[:], w[nm].h.ap(), writes=[mf.r])
                S.op("dve", lambda e: e.tensor_copy(mt.t[:], mf.t[:]), reads=[mf.r], writes=[mt.r])
            sk = self.sb(st, [128, 8])
            S.dma(sk.t[:], w["attn_sink"][l:l + 1, :].broadcast_to([128, 8]), writes=[sk.r])
            S.op("act", lambda e: e.activation(sk.t[:], sk.t[:], AF.Exp), reads=[sk.r], writes=[sk.r])
            ps_s = [self.ps(st, [128, 512], name="ps_s") for _ in range(3)]
            ps_o = [self.ps(st, [128, 512], name="ps_o") for _ in range(4)]
            ptr = self.ps(st, [128, 4, 128], BF16, name="ptra")
            pbuf = [self.sb(st, [128, 512], BF16, name="pbuf") for _ in range(3)]
            yt = [self.sb(st, [128, 512], BF16, name="yt") for _ in range(2)]
            ytT = [self.sb(st, [128, 4, 128], BF16, name="ytT") for _ in range(2)]
            den = [self.sb(st, [128, 4], name="den") for _ in range(2)]
            blocks = []
            for i in range(L // 128):
                ch = []
                if i > 0:
                    ch.append((i - 1, mprev))
                ch.append((i, None))
                if i < L // 128 - 1:
                    ch.append((i + 1, mnext))
                ch += [(32, None), (33, None)]
                blocks.append((i, ch))
            if need_ctx:
                for i in (32, 33):
                    blocks.append((i, [(32, None), (33, None)]))
            rot = 0; io = 0
            steps = []
            for bi, (qi, chunks) in enumerate(blocks):
                for g in range(2):
                    for ci, (kc, mk) in enumerate(chunks):
                        steps.append((bi, qi, g, ci, kc, mk, len(chunks)))
            state = {}

            def emit_S(stp):
                nonlocal rot
                bi, qi, g, ci, kc, mk, nch = stp
                ps = ps_s[rot % 3]; pT = pbuf[rot % 3]; rot += 1
                pr = slice(g * 64, (g + 1) * 64)
                S.op("pe", lambda e: e.matmul(ps.t[:].rearrange("p (c q) -> p c q", c=4),
                                              lhsT=kTs.t[pr, kc * 128:(kc + 1) * 128],
                                              rhs=qTs.t[pr, :, qi * 128:(qi + 1) * 128], start=True, stop=True),
                     reads=[kTs.r, qTs.r], writes=[ps.r])
                S.op("act", lambda e: e.activation(pT.t[:], ps.t[:], AF.Exp, scale=0.125), reads=[ps.r], writes=[pT.r])
                if mk is not None:
                    S.op("dve", lambda e: e.tensor_tensor(pT.t[:], pT.t[:], mk.t[:], ALU.mult), reads=[pT.r, mk.r], writes=[pT.r])
                return pT

            def emit_PV(stp, pT):
                nonlocal io
                bi, qi, g, ci, kc, mk, nch = stp
                if ci == 0:
                    pof = ps_o[io % 4]; io += 1
                    po = Tl(pof.t[:, 0:260].rearrange("p (h d) -> p h d", h=4))
                    po.rs = pof.rs
                    state["po"] = po
                po = state["po"]
                for hh in range(4):
                    S.op("pe", lambda e: e.matmul(po.t[:, hh, :], lhsT=pT.t[:, hh * 128:(hh + 1) * 128],
                                                  rhs=vA.t[:, kc, g, :], start=(ci == 0 and hh == 0),
                                                  stop=(ci == nch - 1 and hh == 3)),
                         reads=[pT.r, vA.r], writes=[po.r])
                if ci == nch - 1:
                    y = yt[bi % 2]
                    dn = den[g]
                    S.op("dve", lambda e: e.tensor_tensor(dn.t[:], po.t[:, :, 64], sk.t[:, g * 4:(g + 1) * 4], ALU.add),
                         reads=[po.r, sk.r], writes=[dn.r])
                    S.op("dve", lambda e: e.reciprocal(dn.t[:], dn.t[:]), reads=[dn.r], writes=[dn.r])
                    S.op("dve", lambda e: e.tensor_tensor(y.t[:, g * 256:(g + 1) * 256].rearrange("p (h d) -> p h d", h=4),
                                                          po.t[:, :, 0:64],
                                                          dn.t[:, :].unsqueeze(2).broadcast_to([128, 4, 64]), ALU.mult),
                         reads=[po.r, dn.r], writes=[y.r])
                    if g == 1:
                        finish_block(bi, qi, y)

            def finish_block(bi, qi, y):
                for pair in range(4):
                    S.op("pe", lambda e: e.transpose(ptr.t[:, pair, :], y.t[:, pair * 128:(pair + 1) * 128], ident.t[:]),
                         reads=[y.r, ident.r], writes=[ptr.r])
                yo = ytT[bi % 2]
                S.op("act", lambda e: e.copy(yo.t[:], ptr.t[:]), reads=[ptr.r], writes=[yo.r])
                S.dma(self.yT[512:1024, qi * 128:(qi + 1) * 128].rearrange("(c p) t -> p c t", p=128), yo.t[:],
                      reads=[yo.r], writes=[self.yT.r(("att", qi))], q="pool")

            LOOK = 2
            pend = []
            for i, stp in enumerate(steps):
                pend.append((stp, emit_S(stp)))
                if len(pend) > LOOK:
                    emit_PV(*pend.pop(0))
            while pend:
                emit_PV(*pend.pop(0))

    def phase_outproj(self, l, need_ctx):
        S, nc, w = self.S, self.nc, self.w
        allk = lambda d: [d.r(k) for k in d.res]
        GT = 512
        with ExitStack() as st:
            wob = self.sb(st, [128, 8, D], BF16, name="wob")
            stg = [self.sb(st, [128, 1024], name="stgo") for _ in range(2)]
            wsrc = w["w_out"][l].rearrange("(k p) n -> p k n", p=128)
            for k in range(8):
                self.cast_weight(wob.t[:, k, :], wob.r, wsrc[:, k, :], stg, k, D)
            Ga = self.sb(st, [128, D])
            yt = [self.sb(st, [128, 8, GT], BF16, name="yto") for _ in range(2)]
            xt = [self.sb(st, [128, D], name="xto") for _ in range(3)]
            junk = self.sb(st, [128, D], name="junko")
            po = [self.ps(st, [128, 512], name="poo") for _ in range(4)]
            groups = [(g * GT, 4) for g in range(L // GT)]
            if need_ctx:
                groups.append((L, 2))
            cur_which = None
            ix = 0; ipo = 0
            yres = allk(self.yT)
            for gi, (t0, ntile) in enumerate(groups):
                n = ntile * 128
                which = 0 if t0 < L else 1
                if which != cur_which:
                    S.dma(Ga.t[:], self.modD[l, which:which + 1, 5 * D:6 * D].broadcast_to([128, D]),
                          reads=[self.modD.r(l)], writes=[Ga.r])
                    cur_which = which
                y = yt[gi % 2]
                S.dma(y.t[:, :, 0:n], self.yT[:, t0:t0 + n].rearrange("(k p) t -> p k t", p=128), reads=yres, writes=[y.r])
                for i in range(ntile):
                    r0 = t0 + i * 128
                    x = xt[ix % 3]; ix += 1
                    S.dma(x.t[:], self.xs[r0:r0 + 128, :], reads=[self.xs.r(r0 // 128)], writes=[x.r], q="pool")
                    for dh in range(2):
                        pp = po[ipo % 4]; ipo += 1
                        for k in range(8):
                            S.op("pe", lambda e: e.matmul(pp.t[:], lhsT=y.t[:, k, i * 128:(i + 1) * 128],
                                                          rhs=wob.t[:, k, dh * 512:(dh + 1) * 512], start=(k == 0), stop=(k == 7)),
                                 reads=[y.r, wob.r], writes=[pp.r])
                        S.op("dve", lambda e: e.tensor_tensor(junk.t[:, dh * 512:(dh + 1) * 512], pp.t[:],
                                                              Ga.t[:, dh * 512:(dh + 1) * 512], ALU.mult),
                             reads=[pp.r, Ga.r], writes=[junk.r])
                        S.op("pool", lambda e: e.tensor_tensor(x.t[:, dh * 512:(dh + 1) * 512], x.t[:, dh * 512:(dh + 1) * 512],
                                                               junk.t[:, dh * 512:(dh + 1) * 512], ALU.add),
                             reads=[junk.r, x.r], writes=[x.r])
                    S.dma(self.xs[r0:r0 + 128, :], x.t[:], reads=[x.r], writes=[self.xs.r(r0 // 128)], q="pool")

    def phase_hyena(self, l, need_ctx):
        S, nc, w = self.S, self.nc, self.w
        allk = lambda d: [d.r(k) for k in d.res]
        PI = math.pi
        NB = 12
        NG = 4
        with ExitStack() as st:
            banks = [self.ps(st, [128, 512], name="hb") for _ in range(8)]
            bk = [0]
            def bank():
                b = banks[bk[0] % 8]; bk[0] += 1
                return b
            def ld(name, shape):
                t = self.sb(st, shape, name=name)
                S.dma(t.t[:], w[name].h.ap(), writes=[t.r])
                return t
            WA = ld("f_WA", [128, 130]); TWr = ld("f_TWr", [64, 65]); TWi = ld("f_TWi", [64, 65])
            C64 = ld("f_C64", [64, 64]); S64 = ld("f_S64", [64, 64]); nS64 = ld("f_nS64", [64, 64])
            CS1 = ld("f_CS1", [64, 128]); CS2 = ld("f_CS2", [64, 128])
            TWir = ld("f_TWir", [65, 64]); TWii = ld("f_TWii", [65, 64])
            Gr = ld("f_Gr", [65, 64]); nGi = ld("f_nGi", [65, 64])
            kext = [self.sb(st, [128, 512], name="kext") for _ in range(2)]
            usb = [self.sb(st, [128, T], name="usb") for _ in range(2)]
            x0sb = [self.sb(st, [128, T], name="x0sb") for _ in range(2)]
            skipc = self.sb(st, [128, 2])
            for cc in range(2):
                self.vec_col(skipc.t[:, cc:cc + 1], w["hy_skip"][l, cc * 128:(cc + 1) * 128], skipc.r)

            with ExitStack() as stm:
                fw0 = self.sb(stm, [33, 64]); S.dma(fw0.t[:], w["hy_fw0"][l], writes=[fw0.r])
                fwi = [self.sb(stm, [64, 64]) for _ in range(2)]
                for j in range(2):
                    S.dma(fwi[j].t[:], w["hy_fw_in"][l, j], writes=[fwi[j].r])
                fwl = self.sb(stm, [64, 512]); S.dma(fwl.t[:], w["hy_fw_last"][l], writes=[fwl.r])
                freq = self.sb(stm, [64, 1]); self.vec_col(freq.t[:], w["hy_freq"][l], freq.r)
                fb = self.sb(stm, [64, 3])
                self.vec_col(fb.t[:, 0:1], w["hy_fb0"][l], fb.r)
                self.vec_col(fb.t[:, 1:2], w["hy_fb_in"][l, 0], fb.r)
                self.vec_col(fb.t[:, 2:3], w["hy_fb_in"][l, 1], fb.r)
                S.op("dve", lambda e: e.tensor_scalar_mul(fb.t[:], fb.t[:], freq.t[:, 0:1]),
                     reads=[fb.r, freq.r], writes=[fb.r])
                zt = [self.sb(stm, [33, 512], name="zt") for _ in range(NG)]
                hA = [self.sb(stm, [64, 512], name="hA") for _ in range(NG)]
                hB = [self.sb(stm, [64, 512], name="hB") for _ in range(NG)]
                kq = [self.sb(stm, [64, 512], mybir.dt.int32, name="kq") for _ in range(NG)]
                dect = [self.sb(stm, [128, 512], name="dect") for _ in range(4)]
                kout = [self.sb(stm, [128, 512], name="kout") for _ in range(4)]
                glist = [("lat", g) for g in range(NFFT // 512)] + ([("ctx", 0)] if need_ctx else [])
                wts = [fw0, fwi[0], fwi[1]]
                io_ = 0
                for c0 in range(0, len(glist), NG):
                    grp = glist[c0:c0 + NG]
                    for gi, (kind, g) in enumerate(grp):
                        zsrc = w["hy_z"][:, g * 512:(g + 1) * 512] if kind == "lat" else w["hy_zc"].h.ap()
                        S.dma(zt[gi].t[:], zsrc, writes=[zt[gi].r])
                    hcur = [None] * len(grp)
                    for j in range(3):
                        pms = []
                        for gi in range(len(grp)):
                            pm = bank()
                            rhs = zt[gi].t[:] if j == 0 else hcur[gi].t[:]
                            rres = zt[gi].r if j == 0 else hcur[gi].r
                            S.op("pe", lambda e: e.matmul(pm.t[0:64, :], lhsT=wts[j].t[:], rhs=rhs, start=True, stop=True),
                                 reads=[wts[j].r, rres], writes=[pm.r])
                            pms.append(pm)
                        for gi in range(len(grp)):
                            hn = (hA if j % 2 == 0 else hB)[gi]
                            pm = pms[gi]
                            S.op("act", lambda e: e.activation(hn.t[:], pm.t[0:64, :], AF.Identity, bias=fb.t[:, j:j + 1], scale=freq.t[:, 0:1]),
                                 reads=[pm.r, freq.r, fb.r], writes=[hn.r])
                            S.op("dve", lambda e: e.tensor_scalar_mul(kq[gi].t[:], hn.t[:], 1.0 / (2.0 * PI)),
                                 reads=[hn.r], writes=[kq[gi].r])
                            S.op("dve", lambda e: e.scalar_tensor_tensor(hn.t[:], kq[gi].t[:], -2.0 * PI, hn.t[:], ALU.mult, ALU.add),
                                 reads=[hn.r, kq[gi].r], writes=[hn.r])
                            S.op("dve", lambda e: e.tensor_scalar(hn.t[:], hn.t[:], -PI, PI, ALU.max, ALU.min),
                                 reads=[hn.r], writes=[hn.r])
                            S.op("act", lambda e: e.activation(hn.t[:], hn.t[:], AF.Sin), reads=[hn.r], writes=[hn.r])
                            hcur[gi] = hn
                    for gi, (kind, g) in enumerate(grp):
                        h = hcur[gi]
                        for cch in range(2):
                            dt_ = dect[io_ % 4]; ko = kout[io_ % 4]; io_ += 1
                            if kind == "lat":
                                wsel = 0 if g < 8 else 1
                                pk = bank()
                                S.op("pe", lambda e: e.matmul(pk.t[:], lhsT=fwl.t[:, wsel * 256 + cch * 128: wsel * 256 + (cch + 1) * 128],
                                                              rhs=h.t[:], start=True, stop=True), reads=[fwl.r, h.r], writes=[pk.r])
                                S.dma(dt_.t[:], w["hy_dec"][cch * 128:(cch + 1) * 128, g * 512:(g + 1) * 512], writes=[dt_.r], q="pool")
                                S.op("dve", lambda e: e.tensor_tensor(ko.t[:], pk.t[:], dt_.t[:], ALU.mult), reads=[pk.r, dt_.r], writes=[ko.r])
                                S.dma(self.kfT[cch * 128:(cch + 1) * 128, g * 512:(g + 1) * 512], ko.t[:], reads=[ko.r],
                                      writes=[self.kfT.r((cch, g))], q="pool")
                            else:
                                S.dma(dt_.t[:], w["hy_decc"][cch * 128:(cch + 1) * 128, :], writes=[dt_.r], q="pool")
                                for wsel, (a_, b_) in ((1, (0, 255)), (0, (255, 511))):
                                    pk = bank()
                                    S.op("pe", lambda e: e.matmul(pk.t[:], lhsT=fwl.t[:, wsel * 256 + cch * 128: wsel * 256 + (cch + 1) * 128],
                                                                  rhs=h.t[:], start=True, stop=True), reads=[fwl.r, h.r], writes=[pk.r])
                                    S.op("dve", lambda e: e.tensor_tensor(kext[cch].t[:, a_:b_], pk.t[:, a_:b_], dt_.t[:, a_:b_], ALU.mult),
                                         reads=[pk.r, dt_.r], writes=[kext[cch].r])
                S.barrier()

            pres = allk(self.projT)
            with ExitStack() as st2:
                raws = [self.sb(st2, [128, T], name="raw") for _ in range(2)]
                x1c = self.sb(st2, [128, T], name="x1c")
                vc = self.sb(st2, [128, T], name="vc")
                cws = [self.sb(st2, [128, 3]) for _ in range(2)]; cbs = [self.sb(st2, [128, 1]) for _ in range(2)]
                ir = 0
                for cc in range(2):
                    for part, dst in ((0, x0sb[cc]), (1, x1c), (2, vc)):
                        raw = raws[ir % 2]; cw = cws[ir % 2]; cb = cbs[ir % 2]; ir += 1
                        ch0 = part * 256 + cc * 128
                        S.dma(raw.t[:], self.projT[512 + ch0:512 + ch0 + 128, :], reads=pres, writes=[raw.r])
                        S.dma(cw.t[:], w["hy_conv_w"][l, :, ch0:ch0 + 128].rearrange("k p -> p k"), writes=[cw.r],
                              allow_slow_non_contiguous=True)
                        self.vec_col(cb.t[:], w["hy_conv_b"][l, ch0:ch0 + 128], cb.r)
                        for (s0, s1) in ((0, L), (L, T)):
                            S.op("act", lambda e: e.activation(dst.t[:, s0:s1], raw.t[:, s0:s1], AF.Identity, bias=cb.t[:], scale=cw.t[:, 1:2]),
                                 reads=[raw.r, cw.r, cb.r], writes=[dst.r])
                            o_ = dst.t[:, s0 + 1:s1]; i_ = raw.t[:, s0:s1 - 1]
                            S.op("dve", lambda e: e.scalar_tensor_tensor(o_, i_, cw.t[:, 0:1], o_, ALU.mult, ALU.add),
                                 reads=[raw.r, cw.r, dst.r], writes=[dst.r])
                            o_ = dst.t[:, s0:s1 - 1]; i_ = raw.t[:, s0 + 1:s1]
                            S.op("dve", lambda e: e.scalar_tensor_tensor(o_, i_, cw.t[:, 2:3], o_, ALU.mult, ALU.add),
                                 reads=[raw.r, cw.r, dst.r], writes=[dst.r])
                    S.op("pool", lambda e: e.tensor_tensor(usb[cc].t[:], x1c.t[:], vc.t[:], ALU.mult), reads=[x1c.r, vc.r], writes=[usb[cc].r])
                    S.dma(self.uT[cc * 128:(cc + 1) * 128, :], usb[cc].t[:], reads=[usb[cc].r], writes=[self.uT.r(cc)])
                S.barrier()

            Bre = self.sb(st, [64, NB, 65], name="Bre"); Bim = self.sb(st, [64, NB, 65], name="Bim")
            tF = [[self.sb(st, [65, 400], name="tF") for _ in range(4)] for _ in range(2)]
            tI = [[self.sb(st, [65, 400], name="tI") for _ in range(4)] for _ in range(2)]
            tcn = {"F": 0, "I": 0}
            Ut = [self.sb(st, [128, NB, 64], name="Ut") for _ in range(2)]
            Kres = [self.sb(st, [64, NB, 65], name="Kre") for _ in range(2)]
            Kims = [self.sb(st, [64, NB, 65], name="Kim") for _ in range(2)]
            Yres = [self.sb(st, [64, NB, 65], name="Yre") for _ in range(2)]
            Yims = [self.sb(st, [64, NB, 65], name="Yim") for _ in range(2)]
            Bpre = self.sb(st, [65, NB, 64], name="Bpre"); Bpim = self.sb(st, [65, NB, 64], name="Bpim")
            ycs = [self.sb(st, [64, NB, 64], name="ycs") for _ in range(2)]
            xo = [self.sb(st, [64, 512], name="xo") for _ in range(4)]

            def cmul(kind, shape_view, ar, ai, br, bi, outr, outi, rres, wres_r, wres_i):
                pool_ = tF if kind == "F" else tI
                tt = pool_[tcn[kind] % 2]; tcn[kind] += 1
                t1, t2, t3, t4 = [shape_view(t) for t in tt]
                S.op("dve", lambda e: e.tensor_tensor(t1, ar, br, ALU.mult), reads=rres, writes=[tt[0].r])
                S.op("dve", lambda e: e.tensor_tensor(t2, ai, bi, ALU.mult), reads=rres, writes=[tt[1].r])
                S.op("dve", lambda e: e.tensor_tensor(t3, ar, bi, ALU.mult), reads=rres, writes=[tt[2].r])
                S.op("dve", lambda e: e.tensor_tensor(t4, ai, br, ALU.mult), reads=rres, writes=[tt[3].r])
                S.op("pool", lambda e: e.tensor_tensor(outr, t1, t2, ALU.subtract), reads=[tt[0].r, tt[1].r], writes=[wres_r])
                S.op("pool", lambda e: e.tensor_tensor(outi, t3, t4, ALU.add), reads=[tt[2].r, tt[3].r], writes=[wres_i])

            def fwd(U, K, nb):
                for sub in range(0, nb, 3):
                    ns = min(3, nb - sub)
                    pa = bank()
                    for j in range(ns):
                        S.op("pe", lambda e: e.matmul(pa.t[0:64, j * 130:(j + 1) * 130], lhsT=U.t[0:K, sub + j, :], rhs=WA.t[0:K, :],
                                                      start=True, stop=True), reads=[U.r, WA.r], writes=[pa.r])
                    pv_ = pa.t[0:64, 0:ns * 130].rearrange("p (c f) -> p c f", c=ns)
                    tw = lambda t: t.t[:, :].unsqueeze(1).broadcast_to([64, ns, 65])
                    sv = lambda t: t.t[0:64, 0:ns * 65].rearrange("p (c f) -> p c f", c=ns)
                    cmul("F", sv, pv_[:, :, 0:65], pv_[:, :, 65:130], tw(TWr), tw(TWi),
                         Bre.t[:, sub:sub + ns, :], Bim.t[:, sub:sub + ns, :], [pa.r, TWr.r, TWi.r], Bre.r, Bim.r)
                outs = []
                for grp in range(0, nb, 6):
                    ng = min(6, nb - grp)
                    ncol = ng * 65
                    pxr = bank(); pxi = bank()
                    br_ = Bre.t[:, grp:grp + ng, :].rearrange("p c f -> p (c f)")
                    bi_ = Bim.t[:, grp:grp + ng, :].rearrange("p c f -> p (c f)")
                    S.op("pe", lambda e: e.matmul(pxr.t[0:64, 0:ncol], lhsT=C64.t[:], rhs=br_, start=True, stop=False), reads=[C64.r, Bre.r], writes=[pxr.r])
                    S.op("pe", lambda e: e.matmul(pxr.t[0:64, 0:ncol], lhsT=S64.t[:], rhs=bi_, start=False, stop=True), reads=[S64.r, Bim.r], writes=[pxr.r])
                    S.op("pe", lambda e: e.matmul(pxi.t[0:64, 0:ncol], lhsT=C64.t[:], rhs=bi_, start=True, stop=False), reads=[C64.r, Bim.r], writes=[pxi.r])
                    S.op("pe", lambda e: e.matmul(pxi.t[0:64, 0:ncol], lhsT=nS64.t[:], rhs=br_, start=False, stop=True), reads=[nS64.r, Bre.r], writes=[pxi.r])
                    outs.append((grp, ng, pxr, pxi))
                return outs

            batches = [(c0, min(NB, 256 - c0)) for c0 in range(0, 256, NB)]
            kres = allk(self.kfT)
            ixo = 0
            for bi_, (c0, nb) in enumerate(batches):
                U = Ut[bi_ % 2]
                S.dma(U.t[:, 0:nb, :], self.kfT[c0:c0 + nb, :].rearrange("c (a b) -> a c b", b=64), reads=kres, writes=[U.r])
                for grp, ng, pxr, pxi in fwd(U, 128, nb):
                    ncol = ng * 65
                    for which, px in ((0, pxr), (1, pxi)):
                        o = xo[ixo % 4]; ixo += 1
                        S.op("act", lambda e: e.copy(o.t[:, 0:ncol], px.t[0:64, 0:ncol]), reads=[px.r], writes=[o.r])
                        S.dma(self.KfD[which, :, c0 + grp:c0 + grp + ng, :], o.t[:, 0:ncol].rearrange("p (c f) -> p c f", c=ng),
                              reads=[o.r], writes=[self.KfD.r((which, c0 + grp))], q="pool")
            S.barrier()
            ures = allk(self.uT)
            kfres = allk(self.KfD)

            def fwd_u(bi_):
                c0, nb = batches[bi_]
                U = Ut[bi_ % 2]; Kre = Kres[bi_ % 2]; Kim = Kims[bi_ % 2]; Yre = Yres[bi_ % 2]; Yim = Yims[bi_ % 2]
                S.dma(U.t[0:64, 0:nb, :], self.uT[c0:c0 + nb, 0:L].rearrange("c (a b) -> a c b", b=64), reads=ures, writes=[U.r])
                S.dma(Kre.t[:, 0:nb, :], self.KfD[0, :, c0:c0 + nb, :], reads=kfres, writes=[Kre.r], q="pool")
                S.dma(Kim.t[:, 0:nb, :], self.KfD[1, :, c0:c0 + nb, :], reads=kfres, writes=[Kim.r], q="pool")
                for grp, ng, pxr, pxi in fwd(U, 64, nb):
                    ncol = ng * 65
                    sv = lambda t: t.t[0:64, 0:ncol]
                    fl = lambda t: t.t[:, grp:grp + ng, :].rearrange("p c f -> p (c f)")
                    cmul("F", sv, pxr.t[0:64, 0:ncol], pxi.t[0:64, 0:ncol], fl(Kre), fl(Kim), fl(Yre), fl(Yim),
                         [pxr.r, pxi.r, Kre.r, Kim.r], Yre.r, Yim.r)

            def inv_u(bi_):
                c0, nb = batches[bi_]
                Yre = Yres[bi_ % 2]; Yim = Yims[bi_ % 2]
                for sub in range(0, nb, 4):
                    ns = min(4, nb - sub)
                    pd = bank()
                    for j in range(ns):
                        S.op("pe", lambda e: e.matmul(pd.t[0:65, j * 128:(j + 1) * 128], lhsT=Yre.t[:, sub + j, :], rhs=CS1.t[:],
                                                      start=True, stop=False), reads=[Yre.r, CS1.r], writes=[pd.r])
                        S.op("pe", lambda e: e.matmul(pd.t[0:65, j * 128:(j + 1) * 128], lhsT=Yim.t[:, sub + j, :], rhs=CS2.t[:],
                                                      start=False, stop=True), reads=[Yim.r, CS2.r], writes=[pd.r])
                    pv_ = pd.t[0:65, 0:ns * 128].rearrange("p (c r n) -> p c r n", c=ns, r=2)
                    tw = lambda t: t.t[:, :].unsqueeze(1).broadcast_to([65, ns, 64])
                    sv = lambda t: t.t[0:65, 0:ns * 64].rearrange("p (c n) -> p c n", c=ns)
                    cmul("I", sv, pv_[:, :, 0, :], pv_[:, :, 1, :], tw(TWir), tw(TWii),
                         Bpre.t[:, sub:sub + ns, :], Bpim.t[:, sub:sub + ns, :], [pd.r, TWir.r, TWii.r], Bpre.r, Bpim.r)
                yo = ycs[bi_ % 2]
                for grp in range(0, nb, 8):
                    ng = min(8, nb - grp)
                    ncol = ng * 64
                    py = bank()
                    S.op("pe", lambda e: e.matmul(py.t[0:64, 0:ncol], lhsT=Gr.t[:], rhs=Bpre.t[:, grp:grp + ng, :].rearrange("p c n -> p (c n)"),
                                                  start=True, stop=False), reads=[Gr.r, Bpre.r], writes=[py.r])
                    S.op("pe", lambda e: e.matmul(py.t[0:64, 0:ncol], lhsT=nGi.t[:], rhs=Bpim.t[:, grp:grp + ng, :].rearrange("p c n -> p (c n)"),
                                                  start=False, stop=True), reads=[nGi.r, Bpim.r], writes=[py.r])
                    S.op("act", lambda e: e.copy(yo.t[:, grp:grp + ng, :].rearrange("p c n -> p (c n)"), py.t[0:64, 0:ncol]),
                         reads=[py.r], writes=[yo.r])
                S.dma(self.ycT[c0:c0 + nb, :].rearrange("c (a b) -> a c b", b=64), yo.t[:, 0:nb, :], reads=[yo.r],
                      writes=[self.ycT.r(c0)], q="pool")

            fwd_u(0)
            for bi_ in range(len(batches)):
                if bi_ + 1 < len(batches):
                    fwd_u(bi_ + 1)
                inv_u(bi_)
            S.barrier()
            ycres = allk(self.ycT)
            ych = self.sb(st, [128, T], name="ych")
            ybf = self.sb(st, [128, T], BF16, name="ybfh")
            for cc in range(2):
                S.dma(ych.t[:, 0:L], self.ycT[cc * 128:(cc + 1) * 128, :], reads=ycres, writes=[ych.r])
                if need_ctx:
                    acc = [self.sb(st, [128, C], name="acc") for _ in range(2)]
                    for s_ in range(C):
                        a = acc[s_ % 2]
                        ks = kext[cc].t[:, 255 - s_:511 - s_]
                        us = usb[cc].t[:, L + s_:L + s_ + 1]
                        if s_ < 2:
                            S.op("dve", lambda e: e.tensor_scalar_mul(a.t[:], ks, us), reads=[kext[cc].r, usb[cc].r], writes=[a.r])
                        else:
                            S.op("dve", lambda e: e.scalar_tensor_tensor(a.t[:], ks, us, a.t[:], ALU.mult, ALU.add),
                                 reads=[kext[cc].r, usb[cc].r, a.r], writes=[a.r])
                    S.op("pool", lambda e: e.tensor_tensor(ych.t[:, L:T], acc[0].t[:], acc[1].t[:], ALU.add),
                         reads=[acc[0].r, acc[1].r], writes=[ych.r])
                else:
                    S.op("pool", lambda e: e.memset(ych.t[:, L:T], 0.0), writes=[ych.r])
                S.op("dve", lambda e: e.scalar_tensor_tensor(ych.t[:], usb[cc].t[:], skipc.t[:, cc:cc + 1], ych.t[:], ALU.mult, ALU.add),
                     reads=[usb[cc].r, skipc.r, ych.r], writes=[ych.r])
                S.op("pool", lambda e: e.tensor_tensor(ybf.t[:], ych.t[:], x0sb[cc].t[:], ALU.mult), reads=[ych.r, x0sb[cc].r], writes=[ybf.r])
                S.dma(self.yT[256 + cc * 128:256 + (cc + 1) * 128, :], ybf.t[:], reads=[ybf.r], writes=[self.yT.r(("hy", cc))])

    def phase_final(self):
        S, w = self.S, self.w
        with ExitStack() as st:
            G = self.sb(st, [128, D])
            S.dma(G.t[:], w["final_g"].h.ap().rearrange("(o n) -> o n", o=1).broadcast_to([128, D]), writes=[G.r])
            xt = [self.sb(st, [128, D]) for _ in range(3)]
            junk = self.sb(st, [128, D])
            ss = [self.sb(st, [128, 1]) for _ in range(3)]
            epst = self.sb(st, [128, 1])
            S.op("pool", lambda e: e.memset(epst.t[:], EPS), writes=[epst.r])
            for i in range(L // 128):
                x = xt[i % 3]; s = ss[i % 3]
                S.dma(x.t[:], self.xs[i * 128:(i + 1) * 128, :], reads=[self.xs.r(i)], writes=[x.r])
                S.op("act", lambda e: e.activation(junk.t[:], x.t[:], AF.Square, accum_out=s.t[:]), reads=[x.r], writes=[junk.r, s.r])
                S.op("act", lambda e: e.activation(s.t[:], s.t[:], AF.Sqrt, bias=epst.t[:], scale=1.0 / D), reads=[s.r, epst.r], writes=[s.r])
                S.op("dve", lambda e: e.reciprocal(s.t[:], s.t[:]), reads=[s.r], writes=[s.r])
                S.op("dve", lambda e: e.scalar_tensor_tensor(x.t[:], x.t[:], s.t[:, 0:1], G.t[:], ALU.mult, ALU.mult),
                     reads=[x.r, s.r, G.r], writes=[x.r])
                S.dma(self.out[i * 128:(i + 1) * 128, :], x.t[:], reads=[x.r], writes=[self.out.r(i)], q="pool")

    def build(self, consts):
        self.declare(consts)
        with self.es as es:
            self.S = Sched(self.nc, es)
            upto = self.upto
            self.phase_mod()
            self.S.barrier()
            done = False
            for l in range(self.nlayers):
                need_ctx = l < DEPTH - 1
                steps = [("ffn1", lambda: self.phase_ffn(l, 0, first=(l == 0))),
                         ("inproj", lambda: self.phase_inproj(l)),
                         ("lru", lambda: self.phase_lru(l)),
                         ("attn", lambda: self.phase_attn(l, need_ctx)),
                         ("hyena", lambda: self.phase_hyena(l, need_ctx)),
                         ("outproj", lambda: self.phase_outproj(l, need_ctx)),
                         ("ffn2", lambda: self.phase_ffn(l, 1, first=False, last_layer_ctx_skip=not need_ctx))]
                for nm, fn in steps:
                    if l == self.nlayers - 1 and upto in ("attn", "hyena") and nm in ("lru", "attn", "hyena") and nm != upto:
                        continue
                    fn()
                    self.S.barrier()
                    if l == self.nlayers - 1 and upto == nm:
                        done = True
                        break
                if done:
                    break
            if not done:
                self.phase_final()
            self.S.finish()
        return self.nc


_CST = None


def _get_consts():
    global _CST
    if _CST is None:
        _CST = _consts()
    return _CST


def make_in_maps(inputs, ncores=8):
    cst = _get_consts()
    maps = []
    shared = {k: np.ascontiguousarray(np.asarray(v, dtype=np.float32)) for k, v in inputs.items()
              if k not in ("x", "c", "ctx")}
    for b in range(ncores):
        m = dict(shared)
        m["x"] = np.ascontiguousarray(inputs["x"][b], dtype=np.float32)
        m["ctx"] = np.ascontiguousarray(inputs["ctx"][b], dtype=np.float32)
        m["c"] = np.ascontiguousarray(inputs["c"][b], dtype=np.float32)
        for k, v in cst.items():
            m["k_" + k] = v
        maps.append(m)
    return maps


def kernel(**inputs):
    cst = _get_consts()
    bld = Builder()
    nc = bld.build(cst)
    maps = make_in_maps(inputs, 8)
    res = run_bass_kernel_spmd(nc, maps, core_ids=list(range(8)))
    return np.stack([np.asarray(r["out"], dtype=np.float32) for r in res.results], axis=0)
```

```python
import math
import numpy as np
from contextlib import ExitStack
import concourse.bass as bass
import concourse.mybir as mybir
from concourse.bass_utils import run_bass_kernel_spmd

F32 = mybir.dt.float32
BF16 = mybir.dt.bfloat16
AF = mybir.ActivationFunctionType
ALU = mybir.AluOpType
AX = mybir.AxisListType

D = 1024
L = 4096
C = 256
T = L + C
NT = T // 128
DFF = 2816
NF = DFF // 128
DEPTH = 2
NMOD = 9
DIN = 2048
EPS = 1e-6
NFFT = 8192


class Res:
    __slots__ = ("w", "r")

    def __init__(self):
        self.w = None
        self.r = {}


class Lane:
    __slots__ = ("sem", "n")

    def __init__(self, sem):
        self.sem = sem
        self.n = 0


class Sched:
    def __init__(self, nc, es, lanes_sp=12, lanes_pool=6, lanes_act=2):
        self.nc = nc
        self.eng = {"pe": nc.tensor, "act": nc.scalar, "dve": nc.vector,
                    "pool": nc.gpsimd, "sp": nc.sync}
        self.sem = {}
        self.cnt = {}
        for k in ("pe", "act", "dve", "pool"):
            self.sem[k] = es.enter_context(nc.semaphore("s_" + k))
            self.cnt[k] = 0
        self.lanes = {}
        self.lane_rr = {}
        for q, n in (("sp", lanes_sp), ("pool", lanes_pool), ("act", lanes_act)):
            self.lanes[q] = [Lane(es.enter_context(nc.semaphore("d_%s%d" % (q, i)))) for i in range(n)]
            self.lane_rr[q] = 0
        self.seen = {k: {} for k in self.eng}
        self.pend = {k: False for k in self.eng}
        self.nwaits = 0
        self.nins = 0

    def _wait(self, E, deps):
        need = {}
        for tok in deps:
            kind, key, val = tok
            if kind == "e" and key == E:
                if E == "pe":
                    continue
                if self.cnt[E] - val >= 2:
                    continue
            k = (kind, key)
            if need.get(k, 0) < val:
                need[k] = val
        seen = self.seen[E]
        for k, val in need.items():
            if seen.get(k, 0) >= val:
                continue
            seen[k] = val
            sem = self.sem[k[1]] if k[0] == "e" else k[1].sem
            self.eng[E].wait_ge(sem, val)
            self.nwaits += 1

    @staticmethod
    def _deps(reads, writes):
        deps = set()
        for r in reads:
            if r.w is not None:
                deps.add(r.w)
        for w in writes:
            if w.w is not None:
                deps.add(w.w)
            deps.update(w.r.values())
        return deps

    @staticmethod
    def _commit(tok, skey, reads, writes):
        for r in reads:
            r.r[skey] = tok
        for w in writes:
            w.w = tok
            w.r = {}

    def op(self, E, fn, reads=(), writes=(), inc=True):
        self._wait(E, self._deps(reads, writes))
        ins = fn(self.eng[E])
        if inc or E != "pe":
            self.cnt[E] += 1
            ins.then_inc(self.sem[E], 1)
            tok = ("e", E, self.cnt[E])
            self.pend[E] = False
        else:
            tok = ("e", E, self.cnt[E] + 1)
            self.pend[E] = True
        self._commit(tok, E, reads, writes)
        self.nins += 1
        return ins

    def dma(self, out, in_, reads=(), writes=(), q="sp", **kw):
        lanes = self.lanes[q]
        lane = lanes[self.lane_rr[q] % len(lanes)]
        self.lane_rr[q] += 1
        deps = self._deps(reads, writes)
        if lane.n:
            deps.add(("d", lane, 16 * lane.n))
        self._wait(q, deps)
        ins = self.eng[q].dma_start(out=out, in_=in_, **kw)
        ins.then_inc(lane.sem, 16)
        lane.n += 1
        self._commit(("d", lane, 16 * lane.n), lane, reads, writes)
        self.nins += 1
        return ins

    def barrier(self):
        assert not self.pend["pe"], "PE accumulation group left without its incrementing instruction"
        deps = set()
        for k in self.cnt:
            if self.cnt[k]:
                deps.add(("e", k, self.cnt[k]))
        for q in self.lanes:
            for ln in self.lanes[q]:
                if ln.n:
                    deps.add(("d", ln, 16 * ln.n))
        for E in ("pe", "act", "dve", "pool", "sp"):
            self._wait(E, deps)

    def finish(self):
        assert not self.pend["pe"]
        deps = set()
        for k in self.cnt:
            if self.cnt[k]:
                deps.add(("e", k, self.cnt[k]))
        for q in self.lanes:
            for ln in self.lanes[q]:
                if ln.n:
                    deps.add(("d", ln, 16 * ln.n))
        self._wait("sp", deps)


class Tl:
    def __init__(self, t, nslots=1):
        self.t = t
        self.rs = [Res() for _ in range(nslots)]

    @property
    def r(self):
        return self.rs[0]

    def __getitem__(self, k):
        return self.t[k]


class Dr:
    def __init__(self, h):
        self.h = h
        self.res = {}

    def r(self, key=0):
        if key not in self.res:
            self.res[key] = Res()
        return self.res[key]

    def __getitem__(self, k):
        return self.h[k]


def _consts():
    cst = {}
    cst["ident"] = np.eye(128, dtype=np.float32)
    inv = 10000.0 ** (-np.arange(16, dtype=np.float64) / 16)
    t = np.arange(L)
    ang = np.concatenate([(t // 64)[:, None] * inv, (t % 64)[:, None] * inv], axis=1)
    cosf = np.ones((128, T), np.float64)
    sinf = np.zeros((128, T), np.float64)
    for p in range(128):
        pp = (p % 64) % 32
        cosf[p, :L] = np.cos(ang[:, pp])
        sinf[p, :L] = np.sin(ang[:, pp])
    cst["ropeC"] = cosf.astype(np.float32)
    cst["ropeS"] = sinf.astype(np.float32)
    j = np.arange(128)[:, None]
    q = np.arange(128)[None, :]
    cst["mprev"] = np.tile((q <= j).astype(np.float32), (1, 4))
    cst["mnext"] = np.tile((j <= q).astype(np.float32), (1, 4))

    def zfeat(Lh):
        tt = np.linspace(0.0, 1.0, Lh)[:, None]
        w = 2.0 * math.pi * np.arange(Lh)[:, None] / Lh
        f = np.linspace(1e-4, 15, 16)[None, :]
        z = np.concatenate([tt, np.cos(f * w), -np.sin(f * w)], axis=-1)
        max_decay = math.log(1e-2) / 0.3
        min_decay = math.log(1e-2) / 1.5
        deltas = np.abs(np.linspace(min_decay, max_decay, 256))
        dec = np.exp(-tt * deltas[None, :])
        return z, dec
    z, dec = zfeat(L)
    zf = np.zeros((NFFT, 33)); df = np.zeros((NFFT, 256))
    zf[:L] = z; df[:L] = dec
    zf[L + 1:] = z[1:][::-1]; df[L + 1:] = dec[1:][::-1]
    cst["hy_z"] = np.ascontiguousarray(zf.T).astype(np.float32)
    cst["hy_dec"] = np.ascontiguousarray(df.T).astype(np.float32)
    zc, decc = zfeat(C)
    ze = np.zeros((512, 33)); de = np.zeros((512, 256))
    for m in range(511):
        lag = abs(m - 255)
        ze[m] = zc[lag]; de[m] = decc[lag]
    cst["hy_zc"] = np.ascontiguousarray(ze.T).astype(np.float32)
    cst["hy_decc"] = np.ascontiguousarray(de.T).astype(np.float32)
    n1 = np.arange(128)[:, None]; f1 = np.arange(65)[None, :]
    a = 2 * math.pi * n1 * f1 / 128
    cst["f_WA"] = np.concatenate([np.cos(a), -np.sin(a)], axis=1).astype(np.float32)
    n2 = np.arange(64)[:, None]
    a = 2 * math.pi * n2 * f1 / NFFT
    cst["f_TWr"] = np.cos(a).astype(np.float32)
    cst["f_TWi"] = (-np.sin(a)).astype(np.float32)
    f2 = np.arange(64)[None, :]
    a = 2 * math.pi * n2 * f2 / 64
    cst["f_C64"] = np.cos(a).astype(np.float32)
    cst["f_S64"] = np.sin(a).astype(np.float32)
    cst["f_nS64"] = (-np.sin(a)).astype(np.float32)
    cst["f_CS1"] = np.concatenate([np.cos(a), np.sin(a)], axis=1).astype(np.float32)
    cst["f_CS2"] = np.concatenate([-np.sin(a), np.cos(a)], axis=1).astype(np.float32)
    f1c = np.arange(65)[:, None]; n2r = np.arange(64)[None, :]
    a = 2 * math.pi * f1c * n2r / NFFT
    cst["f_TWir"] = np.cos(a).astype(np.float32)
    cst["f_TWii"] = np.sin(a).astype(np.float32)
    g = np.full((65, 1), 2.0); g[0] = 1.0; g[64] = 1.0
    n1r = np.arange(64)[None, :]
    a = 2 * math.pi * f1c * n1r / 128
    cst["f_Gr"] = (g * np.cos(a) / NFFT).astype(np.float32)
    cst["f_nGi"] = (-g * np.sin(a) / NFFT).astype(np.float32)
    return cst


_CONST_SHAPES = None


class Builder:
    def __init__(self, debug=None, nlayers=DEPTH, upto=None):
        self.debug = debug or []
        self.nlayers = nlayers
        self.upto = upto
        self.nc = bass.Bass("TRN2", target_bir_lowering=False)
        self.es = ExitStack()
        self.S = None
        self.n_sb = 0

    def din(self, name, shape, dt=F32):
        return Dr(self.nc.dram_tensor(name, list(shape), dt, kind="ExternalInput"))

    def dscr(self, name, shape, dt=F32):
        kind = "ExternalOutput" if name in self.debug else "Internal"
        return Dr(self.nc.dram_tensor(name, list(shape), dt, kind=kind))

    def sb(self, st, shape, dt=F32, nslots=1, name=None):
        self.n_sb += 1
        t = st.enter_context(self.nc.sbuf_tensor("%s_%d" % (name or "t", self.n_sb), list(shape), dt))
        return Tl(t, nslots)

    def ps(self, st, shape, dt=F32, name=None):
        self.n_sb += 1
        t = st.enter_context(self.nc.psum_tensor("%s_%d" % (name or "p", self.n_sb), list(shape), dt))
        return Tl(t, 1)

    def dump(self, name, ap, shape, reads, dt=F32):
        if name not in self.debug:
            return
        d = self.nc.dram_tensor(name, list(shape), dt, kind="ExternalOutput")
        self.S.dma(d.ap(), ap, reads=reads)

    def declare(self, consts):
        w = {}
        w["x"] = self.din("x", [L, D])
        w["ctx"] = self.din("ctx", [C, D])
        w["c"] = self.din("c", [D])
        w["c_ctx"] = self.din("c_ctx", [D])
        shp = dict(w_mod=[DEPTH, D, NMOD * D], b_mod=[DEPTH, NMOD * D], norm_g=[DEPTH, 3, D],
                   ffn_w1=[DEPTH, 2, D, 2 * DFF], ffn_w2=[DEPTH, 2, DFF, D], w_in=[DEPTH, D, DIN],
                   w_out=[DEPTH, D, D], lru_conv_w=[DEPTH, 4, 256], lru_conv_b=[DEPTH, 256],
                   lru_wa=[DEPTH, 2, 4, 64, 64], lru_ba=[DEPTH, 2, 256], lru_wx=[DEPTH, 2, 4, 64, 64],
                   lru_bx=[DEPTH, 2, 256], lru_lam=[DEPTH, 2, 256], hy_conv_w=[DEPTH, 3, 768],
                   hy_conv_b=[DEPTH, 768], hy_fw0=[DEPTH, 33, 64], hy_fb0=[DEPTH, 64],
                   hy_fw_in=[DEPTH, 2, 64, 64], hy_fb_in=[DEPTH, 2, 64], hy_freq=[DEPTH, 64],
                   hy_fw_last=[DEPTH, 64, 512], hy_skip=[DEPTH, 256], attn_sink=[DEPTH, 8], final_g=[D])
        for k, s in shp.items():
            w[k] = self.din(k, s)
        for k, v in consts.items():
            w[k] = self.din("k_" + k, v.shape)
        self.w = w
        self.out = Dr(self.nc.dram_tensor("out", [L, D], F32, kind="ExternalOutput"))
        self.xs = self.dscr("xs", [T, D])
        self.modD = self.dscr("modD", [DEPTH, 2, NMOD * D])
        self.projT = self.dscr("projT", [1280, T])
        self.qT = self.dscr("qT", [4, 128, T], BF16)
        self.kT = self.dscr("kT", [128, T], BF16)
        self.vS = self.dscr("vS", [T, 128], BF16)
        self.yT = self.dscr("yT", [D, T], BF16)
        self.uT = self.dscr("uT", [256, T])
        self.x0T = self.dscr("x0T", [256, T])
        self.ycT = self.dscr("ycT", [256, L])
        self.kfT = self.dscr("kfT", [256, NFFT])
        self.KfD = self.dscr("KfD", [2, 64, 256, 65])

    def col_load(self, st, dst, src_ap_1d, n, q="sp"):
        S = self.S
        S.dma(dst.t[:, 0:n], src_ap_1d.rearrange("(k p) -> p k", p=128), writes=[dst.r], q=q,
              allow_slow_non_contiguous=True)

    def phase_mod(self):
        S, nc, w = self.S, self.nc, self.w
        with ExitStack() as st:
            sT = self.sb(st, [128, 8, 2])
            cin = self.sb(st, [128, 16])
            S.dma(cin.t[:, 0:8], w["c"].h.ap().rearrange("(k p) -> p k", p=128), writes=[cin.r],
                  allow_slow_non_contiguous=True)
            S.dma(cin.t[:, 8:16], w["c_ctx"].h.ap().rearrange("(k p) -> p k", p=128), writes=[cin.r],
                  allow_slow_non_contiguous=True)
            S.op("act", lambda e: e.activation(sT.t[:, :, 0], cin.t[:, 0:8], AF.Silu), reads=[cin.r], writes=[sT.r])
            S.op("act", lambda e: e.activation(sT.t[:, :, 1], cin.t[:, 8:16], AF.Silu), reads=[cin.r], writes=[sT.r])
            wt = [self.sb(st, [128, 8, 512]) for _ in range(2)]
            pm = [self.ps(st, [128, 512]) for _ in range(2)]
            bt = self.sb(st, [2, NMOD * D])
            mo = self.sb(st, [2, NMOD * D])
            it = 0
            for l in range(self.nlayers):
                S.dma(bt.t[:], w["b_mod"][l:l + 1, :].broadcast_to([2, NMOD * D]), writes=[bt.r])
                wsrc = w["w_mod"][l].rearrange("(k p) n -> p k n", p=128)
                for cg in range(NMOD * D // 512):
                    wb = wt[it % 2]; pp = pm[it % 2]; it += 1
                    S.dma(wb.t[:], wsrc[:, :, cg * 512:(cg + 1) * 512], writes=[wb.r],
                          q=("sp" if cg % 2 == 0 else "pool"))
                    for k in range(8):
                        S.op("pe", lambda e: e.matmul(pp.t[0:2, :], lhsT=sT.t[:, k, :], rhs=wb.t[:, k, :],
                                                      start=(k == 0), stop=(k == 7)),
                             reads=[sT.r, wb.r], writes=[pp.r], inc=(k == 7))
                    S.op("dve", lambda e: e.tensor_tensor(mo.t[:, cg * 512:(cg + 1) * 512], pp.t[0:2, :],
                                                          bt.t[:, cg * 512:(cg + 1) * 512], ALU.add),
                         reads=[pp.r, bt.r], writes=[mo.r])
                S.dma(self.modD[l], mo.t[:], reads=[mo.r], writes=[self.modD.r(l)])

    def load_mod_tiles(self, l, which, i_shift, i_scale, i_gate, i_norm, gate_mul, G, Sh, Ga, tmp):
        S, w = self.S, self.w
        md = self.modD
        def bc(i):
            return md[l, which:which + 1, i * D:(i + 1) * D].broadcast_to([128, D])
        S.dma(Sh.t[:], bc(i_shift), reads=[md.r(l)], writes=[Sh.r])
        S.dma(tmp.t[:], bc(i_scale), reads=[md.r(l)], writes=[tmp.r])
        S.dma(G.t[:], w["norm_g"][l, i_norm:i_norm + 1, :].broadcast_to([128, D]), writes=[G.r])
        S.op("dve", lambda e: e.scalar_tensor_tensor(G.t[:], tmp.t[:], 1.0, G.t[:], ALU.add, ALU.mult),
             reads=[tmp.r, G.r], writes=[G.r])
        if Ga is not None:
            S.dma(Ga.t[:], bc(i_gate), reads=[md.r(l)], writes=[Ga.r])
            if gate_mul != 1.0:
                S.op("pool", lambda e: e.tensor_scalar_mul(Ga.t[:], Ga.t[:], gate_mul), reads=[Ga.r], writes=[Ga.r])

    def norm_group(self, xt, ntile, G, Sh, hb, ss, rstd, junk, epst):
        S = self.S
        for i in range(ntile):
            S.op("act", lambda e: e.activation(junk.t[:], xt.t[:, i, :], AF.Square, accum_out=ss.t[:, i:i + 1]),
                 reads=[xt.rs[i]], writes=[junk.r, ss.r])
        S.op("act", lambda e: e.activation(rstd.t[:, 0:ntile], ss.t[:, 0:ntile], AF.Sqrt, bias=epst.t[:], scale=1.0 / D),
             reads=[ss.r, epst.r], writes=[rstd.r])
        S.op("dve", lambda e: e.reciprocal(rstd.t[:, 0:ntile], rstd.t[:, 0:ntile]), reads=[rstd.r], writes=[rstd.r])
        for i in range(ntile):
            S.op("dve", lambda e: e.scalar_tensor_tensor(junk.t[:], xt.t[:, i, :], rstd.t[:, i:i + 1], G.t[:],
                                                         ALU.mult, ALU.mult),
                 reads=[xt.rs[i], rstd.r, G.r], writes=[junk.r])
            S.op("pool", lambda e: e.tensor_tensor(hb.t[:, i, :], junk.t[:], Sh.t[:], ALU.add),
                 reads=[junk.r, Sh.r], writes=[hb.rs[i]])

    def transpose_group(self, hb, ntile, hT, ptr, ident):
        S = self.S
        for i in range(ntile):
            p = ptr[i % len(ptr)]
            for k in range(8):
                S.op("pe", lambda e: e.transpose(p.t[:, k, :], hb.t[:, i, k * 128:(k + 1) * 128], ident.t[:]),
                     reads=[hb.rs[i], ident.r], writes=[p.r], inc=(k == 7))
            eng = "act" if i % 2 == 0 else "dve"
            if eng == "act":
                S.op("act", lambda e: e.copy(hT.t[:, :, i * 128:(i + 1) * 128], p.t[:]), reads=[p.r], writes=[hT.r])
            else:
                S.op("dve", lambda e: e.tensor_copy(hT.t[:, :, i * 128:(i + 1) * 128], p.t[:]), reads=[p.r], writes=[hT.r])

    def load_ident(self, st):
        S = self.S
        idf = self.sb(st, [128, 128])
        ident = self.sb(st, [128, 128], BF16)
        S.dma(idf.t[:], self.w["ident"].h.ap(), writes=[idf.r])
        S.op("dve", lambda e: e.tensor_copy(ident.t[:], idf.t[:]), reads=[idf.r], writes=[ident.r])
        return ident

    def cast_weight(self, dst_ap, dst_res, src_ap, stg, idx, ncols, scale=None):
        S = self.S
        sl = stg[idx % len(stg)]
        S.dma(sl.t[:, 0:ncols], src_ap, writes=[sl.r], q=("sp" if idx % 2 == 0 else "pool"))
        eng = ("dve", "act", "pool")[idx % 3]
        if eng == "act":
            S.op("act", lambda e: e.copy(dst_ap, sl.t[:, 0:ncols]), reads=[sl.r], writes=[dst_res])
        else:
            S.op(eng, lambda e: e.tensor_copy(dst_ap, sl.t[:, 0:ncols]), reads=[sl.r], writes=[dst_res])

    def ffn_weight_loader(self, l, j, w1b, w2b, stg, act_only=False):
        w = self.w
        S = self.S
        ci = 0
        w1src = w["ffn_w1"][l, j].rearrange("(k p) n -> p k n", p=128)
        w2src = w["ffn_w2"][l, j].rearrange("(f p) n -> p f n", p=128)
        items = []
        for k in range(8):
            for c0 in range(0, 2 * DFF, 1024):
                nco = min(1024, 2 * DFF - c0)
                items.append((w1b.t[:, k, c0:c0 + nco], w1b.r, w1src[:, k, c0:c0 + nco], nco))
        for f in range(NF):
            items.append((w2b.t[:, f, :], w2b.r, w2src[:, f, :], D))
        for (dst, res, src, nco) in items:
            if act_only:
                sl = stg[ci % len(stg)]
                S.dma(sl.t[:, 0:nco], src, writes=[sl.r], q="act")
                S.op("act", lambda e: e.copy(dst, sl.t[:, 0:nco]), reads=[sl.r], writes=[res])
            else:
                self.cast_weight(dst, res, src, stg, ci, nco)
            ci += 1
            yield

    def phase_ffn(self, l, j, first, last_layer_ctx_skip=False, weights=None):
        S, nc, w = self.S, self.nc, self.w
        GT = 256
        i_shift, i_scale, i_gate, i_norm = (0, 1, 2, 0) if j == 0 else (6, 7, 8, 2)
        with ExitStack() as st:
            if weights is None:
                w1b = self.sb(st, [128, 8, 2 * DFF], BF16, name="w1b")
                w2b = self.sb(st, [128, NF, D], BF16, name="w2b")
                stg = [self.sb(st, [128, 1024], name="stg") for _ in range(2)]
                for _ in self.ffn_weight_loader(l, j, w1b, w2b, stg):
                    pass
            else:
                w1b, w2b = weights
            ident = self.load_ident(st)
            G = self.sb(st, [128, D]); Sh = self.sb(st, [128, D]); Ga = self.sb(st, [128, D])
            xts = [self.sb(st, [128, 2, D], nslots=2, name="xt") for _ in range(2)]
            hbs = [self.sb(st, [128, 2, D], BF16, nslots=2, name="hb") for _ in range(2)]
            junk = self.sb(st, [128, D], name="junk")
            hTs = [self.sb(st, [128, 8, GT], BF16, name="hT") for _ in range(2)]
            gT = self.sb(st, [128, NF, GT], BF16, nslots=NF, name="gT")
            sil = [self.sb(st, [128, GT], name="sil") for _ in range(2)]
            sss = [self.sb(st, [128, 2]) for _ in range(2)]; rstds = [self.sb(st, [128, 2]) for _ in range(2)]
            epst = self.sb(st, [128, 1])
            S.op("pool", lambda e: e.memset(epst.t[:], EPS), writes=[epst.r])
            ptr = [self.ps(st, [128, 8, 128], BF16, name="ptr") for _ in range(1)]
            pab = [self.ps(st, [128, 2, GT], name="pab") for _ in range(3)]
            po = [self.ps(st, [128, 512], name="po") for _ in range(4)]
            ngroups = T // GT
            if last_layer_ctx_skip:
                ngroups = L // GT
            cur_which = [None]

            def load_x(g):
                xt = xts[g % 2]; t0 = g * GT
                for i in range(2):
                    r0 = t0 + i * 128
                    if first:
                        src = w["x"][r0:r0 + 128, :] if r0 < L else w["ctx"][r0 - L:r0 - L + 128, :]
                        S.dma(xt.t[:, i, :], src, writes=[xt.rs[i]])
                    else:
                        S.dma(xt.t[:, i, :], self.xs[r0:r0 + 128, :], reads=[self.xs.r(r0 // 128)], writes=[xt.rs[i]])

            def norm(g):
                which = 0 if g * GT < L else 1
                if which != cur_which[0]:
                    self.load_mod_tiles(l, which, i_shift, i_scale, i_gate, i_norm, 0.5, G, Sh, Ga, junk)
                    cur_which[0] = which
                self.norm_group(xts[g % 2], 2, G, Sh, hbs[g % 2], sss[g % 2], rstds[g % 2], junk, epst)

            def transp(g):
                self.transpose_group(hbs[g % 2], 2, hTs[g % 2], ptr, ident)

            def stage1(g):
                hT = hTs[g % 2]
                for f in range(NF):
                    pb = pab[f % 3]
                    for half in range(2):
                        col = half * DFF + f * 128
                        for k in range(8):
                            S.op("pe", lambda e: e.matmul(pb.t[:, half, :], lhsT=w1b.t[:, k, col:col + 128],
                                                          rhs=hT.t[:, k, :], start=(k == 0), stop=(k == 7)),
                                 reads=[w1b.r, hT.r], writes=[pb.r], inc=(k == 7))
                    sl = sil[f % 2]
                    S.op("act", lambda e: e.activation(sl.t[:], pb.t[:, 0, :], AF.Silu), reads=[pb.r], writes=[sl.r])
                    S.op("dve", lambda e: e.tensor_tensor(gT.t[:, f, :], sl.t[:], pb.t[:, 1, :], ALU.mult),
                         reads=[sl.r, pb.r], writes=[gT.rs[f]])

            def stage2(g):
                xt = xts[g % 2]; t0 = g * GT
                for i in range(2):
                    for dh in range(2):
                        pp = po[(i * 2 + dh) % 4]
                        for f in range(NF):
                            S.op("pe", lambda e: e.matmul(pp.t[:], lhsT=gT.t[:, f, i * 128:(i + 1) * 128],
                                                          rhs=w2b.t[:, f, dh * 512:(dh + 1) * 512],
                                                          start=(f == 0), stop=(f == NF - 1)),
                                 reads=[gT.rs[f], w2b.r], writes=[pp.r], inc=(f == NF - 1))
                        S.op("dve", lambda e: e.tensor_tensor(junk.t[:, dh * 512:(dh + 1) * 512], pp.t[:],
                                                              Ga.t[:, dh * 512:(dh + 1) * 512], ALU.mult),
                             reads=[pp.r, Ga.r], writes=[junk.r])
                        S.op("pool", lambda e: e.tensor_tensor(xt.t[:, i, dh * 512:(dh + 1) * 512],
                                                               xt.t[:, i, dh * 512:(dh + 1) * 512],
                                                               junk.t[:, dh * 512:(dh + 1) * 512], ALU.add),
                             reads=[junk.r, xt.rs[i]], writes=[xt.rs[i]])
                    r0 = t0 + i * 128
                    S.dma(self.xs[r0:r0 + 128, :], xt.t[:, i, :], reads=[xt.rs[i]], writes=[self.xs.r(r0 // 128)], q="pool")

            def is_switch(g):
                return g < ngroups and (0 if g * GT < L else 1) != cur_which[0] and cur_which[0] is not None
            load_x(0); norm(0); transp(0)
            for g in range(ngroups):
                nxt = g + 1 < ngroups
                sw = nxt and is_switch(g + 1)
                if nxt:
                    load_x(g + 1)
                    if not sw:
                        norm(g + 1)
                stage1(g)
                if nxt and not sw:
                    transp(g + 1)
                stage2(g)
                if sw:
                    norm(g + 1); transp(g + 1)

    def phase_inproj(self, l):
        S, nc, w = self.S, self.nc, self.w
        GT = 512
        NCOL = 2688
        with ExitStack() as st:
            wb = self.sb(st, [128, 8, NCOL], BF16, name="winb")
            stg = [self.sb(st, [128, DIN], name="stgi") for _ in range(2)]
            ident = self.load_ident(st)
            wsrc = w["w_in"][l].rearrange("(k p) n -> p k n", p=128)
            engs = ("dve", "pool")
            ne = 0
            for k in range(8):
                sl = stg[k % 2]
                S.dma(sl.t[:], wsrc[:, k, :], writes=[sl.r], q=("sp" if k % 2 == 0 else "pool"))
                def cp(dst, src, neg=False):
                    nonlocal ne
                    E = engs[ne % 2]; ne += 1
                    if neg:
                        S.op(E, lambda e: e.tensor_scalar_mul(dst, src, -1.0), reads=[sl.r], writes=[wb.r])
                    else:
                        S.op(E, lambda e: e.tensor_copy(dst, src), reads=[sl.r], writes=[wb.r])
                S.op("act", lambda e: e.copy(wb.t[:, k, 0:1280], sl.t[:, 0:1280]), reads=[sl.r], writes=[wb.r])
                cp(wb.t[:, k, 1792:2048], sl.t[:, 1792:2048])
                qs = sl.t[:, 1280:1792].rearrange("p (half ch d) -> p ch half d", half=2, ch=4)
                cp(wb.t[:, k, 1280:1792].rearrange("p (ch half d) -> p ch half d", ch=4, half=2), qs)
                qs2 = sl.t[:, 1280:1792].rearrange("p (half ch two d) -> p ch half two d", half=2, ch=4, two=2)
                qd2 = wb.t[:, k, 2048:2560].rearrange("p (ch half two d) -> p ch half two d", ch=4, half=2, two=2)
                cp(qd2[:, :, :, 0, :], qs2[:, :, :, 1, :], neg=True)
                cp(qd2[:, :, :, 1, :], qs2[:, :, :, 0, :])
                ks2 = sl.t[:, 1792:1920].rearrange("p (g two d) -> p g two d", g=2, two=2)
                kd2 = wb.t[:, k, 2560:2688].rearrange("p (g two d) -> p g two d", g=2, two=2)
                cp(kd2[:, :, 0, :], ks2[:, :, 1, :], neg=True)
                cp(kd2[:, :, 1, :], ks2[:, :, 0, :])
            G = self.sb(st, [128, D]); Sh = self.sb(st, [128, D])
            xts = [self.sb(st, [128, 4, D], nslots=4, name="xti") for _ in range(2)]
            hbs = [self.sb(st, [128, 4, D], BF16, nslots=4, name="hbi") for _ in range(2)]
            junk = self.sb(st, [128, D], name="junki")
            hTs = [self.sb(st, [128, 8, GT], BF16, name="hTi") for _ in range(2)]
            sss = [self.sb(st, [128, 4]) for _ in range(2)]; rstds = [self.sb(st, [128, 4]) for _ in range(2)]
            epst = self.sb(st, [128, 1])
            S.op("pool", lambda e: e.memset(epst.t[:], EPS), writes=[epst.r])
            rcs = [self.sb(st, [128, GT]) for _ in range(2)]; rss = [self.sb(st, [128, GT]) for _ in range(2)]
            ost = [self.sb(st, [128, GT], name="ost") for _ in range(3)]
            obf = [self.sb(st, [128, GT], BF16, name="obf") for _ in range(2)]
            t1s = [self.sb(st, [128, GT]) for _ in range(2)]; t2s = [self.sb(st, [128, GT]) for _ in range(2)]
            vbf = self.sb(st, [128, 4, 128], BF16)
            ptr = [self.ps(st, [128, 8, 128], BF16, name="ptri") for _ in range(2)]
            pp = [self.ps(st, [128, GT], name="ppi") for _ in range(5)]
            pv = self.ps(st, [128, 4, 128], name="pvi")
            groups = [(g * GT, 4) for g in range(L // GT)] + [(L, 2)]
            cur_which = [None]
            ipc = [0]

            def load_x(gi):
                t0, ntile = groups[gi]
                n = ntile * 128
                xt = xts[gi % 2]
                for i in range(ntile):
                    r0 = t0 + i * 128
                    S.dma(xt.t[:, i, :], self.xs[r0:r0 + 128, :], reads=[self.xs.r(r0 // 128)], writes=[xt.rs[i]])
                S.dma(rcs[gi % 2].t[:, 0:n], w["ropeC"][:, t0:t0 + n], writes=[rcs[gi % 2].r], q="pool")
                S.dma(rss[gi % 2].t[:, 0:n], w["ropeS"][:, t0:t0 + n], writes=[rss[gi % 2].r], q="pool")

            def norm(gi):
                t0, ntile = groups[gi]
                which = 0 if t0 < L else 1
                if which != cur_which[0]:
                    self.load_mod_tiles(l, which, 3, 4, None, 1, 1.0, G, Sh, None, junk)
                    cur_which[0] = which
                self.norm_group(xts[gi % 2], ntile, G, Sh, hbs[gi % 2], sss[gi % 2], rstds[gi % 2], junk, epst)

            def transp(gi):
                t0, ntile = groups[gi]
                self.transpose_group(hbs[gi % 2], ntile, hTs[gi % 2], ptr, ident)

            def mms(gi):
                t0, ntile = groups[gi]
                n = ntile * 128
                hT = hTs[gi % 2]; rc = rcs[gi % 2]; rs_ = rss[gi % 2]
                def mm(pt, c0):
                    for k in range(8):
                        S.op("pe", lambda e: e.matmul(pt.t[:, 0:n], lhsT=wb.t[:, k, c0:c0 + 128], rhs=hT.t[:, k, 0:n],
                                                      start=(k == 0), stop=(k == 7)),
                             reads=[wb.r, hT.r], writes=[pt.r], inc=(k == 7))
                for oc in range(10):
                    pt = pp[ipc[0] % 5]; ipc[0] += 1
                    mm(pt, oc * 128)
                    o = ost[oc % 3]
                    S.op("act", lambda e: e.copy(o.t[:, 0:n], pt.t[:, 0:n]), reads=[pt.r], writes=[o.r])
                    S.dma(self.projT[oc * 128:(oc + 1) * 128, t0:t0 + n], o.t[:, 0:n], reads=[o.r],
                          writes=[self.projT.r((oc, t0))], q="pool")
                for ch in range(5):
                    pa = pp[ipc[0] % 5]; ipc[0] += 1
                    pb_ = pp[ipc[0] % 5]; ipc[0] += 1
                    mm(pa, 1280 + ch * 128)
                    mm(pb_, 2048 + ch * 128)
                    t1 = t1s[ch % 2]; t2 = t2s[ch % 2]
                    S.op("dve", lambda e: e.tensor_tensor(t1.t[:, 0:n], pa.t[:, 0:n], rc.t[:, 0:n], ALU.mult),
                         reads=[pa.r, rc.r], writes=[t1.r])
                    S.op("dve", lambda e: e.tensor_tensor(t2.t[:, 0:n], pb_.t[:, 0:n], rs_.t[:, 0:n], ALU.mult),
                         reads=[pb_.r, rs_.r], writes=[t2.r])
                    ob = obf[ch % 2]
                    S.op("pool", lambda e: e.tensor_tensor(ob.t[:, 0:n], t1.t[:, 0:n], t2.t[:, 0:n], ALU.add),
                         reads=[t1.r, t2.r], writes=[ob.r])
                    if ch < 4:
                        S.dma(self.qT[ch, :, t0:t0 + n], ob.t[:, 0:n], reads=[ob.r], writes=[self.qT.r((ch, t0))], q="pool")
                    else:
                        S.dma(self.kT[:, t0:t0 + n], ob.t[:, 0:n], reads=[ob.r], writes=[self.kT.r(t0)], q="pool")
                for i in range(ntile):
                    for k in range(8):
                        S.op("pe", lambda e: e.matmul(pv.t[:, i, :], lhsT=hT.t[:, k, i * 128:(i + 1) * 128],
                                                      rhs=wb.t[:, k, 1920:2048], start=(k == 0), stop=(k == 7)),
                             reads=[wb.r, hT.r], writes=[pv.r], inc=(k == 7))
                S.op("act", lambda e: e.copy(vbf.t[:, 0:ntile, :], pv.t[:, 0:ntile, :]), reads=[pv.r], writes=[vbf.r])
                S.dma(self.vS[t0:t0 + n, :].rearrange("(i p) d -> p i d", p=128), vbf.t[:, 0:ntile, :], reads=[vbf.r],
                      writes=[self.vS.r(t0)], q="pool")

            load_x(0); norm(0); transp(0)
            for gi in range(len(groups)):
                if gi + 1 < len(groups):
                    load_x(gi + 1); norm(gi + 1)
                mms(gi)
                if gi + 1 < len(groups):
                    transp(gi + 1)

    def vec_col(self, dst_ap, src_1d, res):
        self.S.dma(dst_ap, src_1d.rearrange("(p o) -> p o", o=1), writes=[res])

    def phase_lru(self, l):
        S, nc, w = self.S, self.nc, self.w
        with ExitStack() as st:
            big = lambda nm: self.sb(st, [128, T], name=nm)
            xl, u, gl, A, B, Cc, Df, Eb = [big(n) for n in ("xl", "u", "gl", "A", "B", "Cc", "Df", "Eb")]
            ybf = self.sb(st, [128, T], BF16, name="ybf")
            one = self.sb(st, [128, 1])
            S.op("pool", lambda e: e.memset(one.t[:], 1.0), writes=[one.r])
            pg = [self.ps(st, [128, 512], name="pg") for _ in range(4)]
            ip = 0
            for cc in range(2):
                c0 = cc * 128
                S.dma(xl.t[:], self.projT[c0:c0 + 128, :], reads=[self.projT.r(k) for k in self.projT.res], writes=[xl.r])
                S.dma(gl.t[:], self.projT[256 + c0:256 + c0 + 128, :], reads=[self.projT.r(k) for k in self.projT.res], writes=[gl.r], q="pool")
                cw = self.sb(st, [128, 4]); cb = self.sb(st, [128, 1])
                S.dma(cw.t[:], w["lru_conv_w"][l, :, c0:c0 + 128].rearrange("k p -> p k"), writes=[cw.r], allow_slow_non_contiguous=True)
                self.vec_col(cb.t[:], w["lru_conv_b"][l, c0:c0 + 128], cb.r)
                for (s0, s1) in ((0, L), (L, T)):
                    S.op("dve", lambda e: e.tensor_scalar(u.t[:, s0:s1], xl.t[:, s0:s1], cw.t[:, 2:3], cb.t[:, 0:1], ALU.mult, ALU.add),
                         reads=[xl.r, cw.r, cb.r], writes=[u.r])
                    for k, off in ((0, -2), (1, -1), (3, 1)):
                        if off < 0:
                            o_ = u.t[:, s0 - off:s1]; i_ = xl.t[:, s0:s1 + off]
                        else:
                            o_ = u.t[:, s0:s1 - off]; i_ = xl.t[:, s0 + off:s1]
                        S.op("dve", lambda e: e.scalar_tensor_tensor(o_, i_, cw.t[:, k:k + 1], o_, ALU.mult, ALU.add),
                             reads=[xl.r, cw.r, u.r], writes=[u.r])
                for d in range(2):
                    wa = self.sb(st, [128, 128]); wx = self.sb(st, [128, 128])
                    for (wt_, nm) in ((wa, "lru_wa"), (wx, "lru_wx")):
                        S.op("pool", lambda e: e.memset(wt_.t[:], 0.0), writes=[wt_.r])
                        S.dma(wt_.t[0:64, 0:64], w[nm][l, d, 2 * cc], writes=[wt_.r])
                        S.dma(wt_.t[64:128, 64:128], w[nm][l, d, 2 * cc + 1], writes=[wt_.r])
                    ba = self.sb(st, [128, 1]); bx = self.sb(st, [128, 1]); lam = self.sb(st, [128, 1])
                    self.vec_col(ba.t[:], w["lru_ba"][l, d, c0:c0 + 128], ba.r)
                    self.vec_col(bx.t[:], w["lru_bx"][l, d, c0:c0 + 128], bx.r)
                    self.vec_col(lam.t[:], w["lru_lam"][l, d, c0:c0 + 128], lam.r)
                    sp = self.sb(st, [128, 2])
                    S.op("act", lambda e: e.activation(lam.t[:], lam.t[:], AF.Exp, scale=-1.0), reads=[lam.r], writes=[lam.r])
                    S.op("act", lambda e: e.activation(lam.t[:], lam.t[:], AF.Ln, bias=one.t[:], scale=1.0), reads=[lam.r, one.r], writes=[lam.r])
                    S.op("dve", lambda e: e.tensor_scalar_mul(sp.t[:, 0:1], lam.t[:], -8.0), reads=[lam.r], writes=[sp.r])
                    S.op("dve", lambda e: e.tensor_scalar_mul(sp.t[:, 1:2], lam.t[:], -16.0), reads=[lam.r], writes=[sp.r])
                    for g0 in range(0, T, 512):
                        n = min(512, T - g0)
                        for (wt_, bt_, dst) in ((wa, ba, A), (wx, bx, B)):
                            pt = pg[ip % 4]; ip += 1
                            S.op("pe", lambda e: e.matmul(pt.t[:, 0:n], lhsT=wt_.t[:], rhs=u.t[:, g0:g0 + n], start=True, stop=True),
                                 reads=[wt_.r, u.r], writes=[pt.r])
                            S.op("act", lambda e: e.activation(dst.t[:, g0:g0 + n], pt.t[:, 0:n], AF.Sigmoid, bias=bt_.t[:], scale=1.0),
                                 reads=[pt.r, bt_.r], writes=[dst.r])
                    S.op("act", lambda e: e.activation(Cc.t[:], A.t[:], AF.Exp, scale=sp.t[:, 0:1]), reads=[A.r, sp.r], writes=[Cc.r])
                    S.op("act", lambda e: e.activation(A.t[:], A.t[:], AF.Exp, scale=sp.t[:, 1:2]), reads=[A.r, sp.r], writes=[A.r])
                    S.op("act", lambda e: e.activation(A.t[:], A.t[:], AF.Sqrt, bias=one.t[:], scale=-1.0), reads=[A.r, one.r], writes=[A.r])
                    S.op("pool", lambda e: e.tensor_tensor(B.t[:], B.t[:], u.t[:], ALU.mult), reads=[B.r, u.r], writes=[B.r])
                    S.op("pool", lambda e: e.tensor_tensor(A.t[:], A.t[:], B.t[:], ALU.mult), reads=[A.r, B.r], writes=[A.r])
                    if d == 0:
                        S.op("dve", lambda e: e.tensor_tensor_scan(Df.t[:, L:T], Cc.t[:, L:T], A.t[:, L:T], 0.0, ALU.mult, ALU.add),
                             reads=[Cc.r, A.r], writes=[Df.r])
                        S.op("dve", lambda e: e.tensor_tensor_scan(Df.t[:, 0:L], Cc.t[:, 0:L], A.t[:, 0:L], Df.t[:, T - 1:T], ALU.mult, ALU.add),
                             reads=[Cc.r, A.r, Df.r], writes=[Df.r])
                    else:
                        rv = lambda tl, a, b: tl.t[:, a:b][:, ::-1]
                        S.op("dve", lambda e: e.tensor_tensor_scan(rv(Eb, L, T), rv(Cc, L, T), rv(A, L, T), 0.0, ALU.mult, ALU.add),
                             reads=[Cc.r, A.r], writes=[Eb.r])
                        S.op("dve", lambda e: e.tensor_tensor_scan(rv(Eb, 0, L), rv(Cc, 0, L), rv(A, 0, L), Eb.t[:, L:L + 1], ALU.mult, ALU.add),
                             reads=[Cc.r, A.r, Eb.r], writes=[Eb.r])
                S.op("act", lambda e: e.activation(gl.t[:], gl.t[:], AF.Gelu), reads=[gl.r], writes=[gl.r])
                S.op("pool", lambda e: e.tensor_tensor(Df.t[:], Df.t[:], Eb.t[:], ALU.add), reads=[Df.r, Eb.r], writes=[Df.r])
                S.op("dve", lambda e: e.tensor_tensor(ybf.t[:], Df.t[:], gl.t[:], ALU.mult), reads=[Df.r, gl.r], writes=[ybf.r])
                S.dma(self.yT[c0:c0 + 128, :], ybf.t[:], reads=[ybf.r], writes=[self.yT.r(("lru", cc))])

    def phase_attn(self, l, need_ctx):
        S, nc, w = self.S, self.nc, self.w
        allk = lambda d: [d.r(k) for k in d.res]
        with ExitStack() as st:
            kTs = self.sb(st, [128, T], BF16, name="kTs")
            qTs = self.sb(st, [128, 4, T], BF16, name="qTs")
            vtmp = self.sb(st, [128, NT, 128], BF16, name="vtmp")
            vA = self.sb(st, [128, NT, 2, 65], BF16, name="vA")
            S.dma(kTs.t[:], self.kT.h.ap(), reads=allk(self.kT), writes=[kTs.r])
            for ch in range(4):
                S.dma(qTs.t[:, ch, :], self.qT[ch], reads=allk(self.qT), writes=[qTs.r], q=("sp" if ch % 2 == 0 else "pool"))
            S.dma(vtmp.t[:], self.vS.h.ap().rearrange("(i p) d -> p i d", p=128), reads=allk(self.vS), writes=[vtmp.r])
            S.op("pool", lambda e: e.memset(vA.t[:], 1.0), writes=[vA.r])
            S.op("dve", lambda e: e.tensor_copy(vA.t[:, :, :, 0:64], vtmp.t[:].rearrange("p i (g d) -> p i g d", g=2)),
                 reads=[vtmp.r, vA.r], writes=[vA.r])
            ident = self.load_ident(st)
            mf = self.sb(st, [128, 512])
            mprev = self.sb(st, [128, 512], BF16); mnext = self.sb(st, [128, 512], BF16)
            for (mt, nm) in ((mprev, "mprev"), (mnext, "mnext")):
                S.dma(mf.t[:], w[nm].h.ap(), writes=[mf.r])
                S.op("dve", lambda e: e.tensor_copy(mt.t[:], mf.t[:]), reads=[mf.r], writes=[mt.r])
            sk = self.sb(st, [128, 8])
            S.dma(sk.t[:], w["attn_sink"][l:l + 1, :].broadcast_to([128, 8]), writes=[sk.r])
            S.op("act", lambda e: e.activation(sk.t[:], sk.t[:], AF.Exp), reads=[sk.r], writes=[sk.r])
            ps_s = [self.ps(st, [128, 512], name="ps_s") for _ in range(3)]
            ps_o = [self.ps(st, [128, 512], name="ps_o") for _ in range(4)]
            ptr = self.ps(st, [128, 4, 128], BF16, name="ptra")
            pbuf = [self.sb(st, [128, 512], BF16, name="pbuf") for _ in range(3)]
            yt = [self.sb(st, [128, 512], BF16, name="yt") for _ in range(2)]
            ytT = [self.sb(st, [128, 4, 128], BF16, name="ytT") for _ in range(2)]
            den = [self.sb(st, [128, 4], name="den") for _ in range(2)]
            blocks = []
            for i in range(L // 128):
                ch = []
                if i > 0:
                    ch.append((i - 1, mprev))
                ch.append((i, None))
                if i < L // 128 - 1:
                    ch.append((i + 1, mnext))
                ch += [(32, None), (33, None)]
                blocks.append((i, ch))
            if need_ctx:
                for i in (32, 33):
                    blocks.append((i, [(32, None), (33, None)]))
            rot = 0; io = 0
            steps = []
            for bi, (qi, chunks) in enumerate(blocks):
                for g in range(2):
                    for ci, (kc, mk) in enumerate(chunks):
                        steps.append((bi, qi, g, ci, kc, mk, len(chunks)))
            state = {}

            def emit_S(stp):
                nonlocal rot
                bi, qi, g, ci, kc, mk, nch = stp
                ps = ps_s[rot % 3]; pT = pbuf[rot % 3]; rot += 1
                pr = slice(g * 64, (g + 1) * 64)
                S.op("pe", lambda e: e.matmul(ps.t[:].rearrange("p (c q) -> p c q", c=4),
                                              lhsT=kTs.t[pr, kc * 128:(kc + 1) * 128],
                                              rhs=qTs.t[pr, :, qi * 128:(qi + 1) * 128], start=True, stop=True),
                     reads=[kTs.r, qTs.r], writes=[ps.r])
                S.op("act", lambda e: e.activation(pT.t[:], ps.t[:], AF.Exp, scale=0.125), reads=[ps.r], writes=[pT.r])
                if mk is not None:
                    S.op("dve", lambda e: e.tensor_tensor(pT.t[:], pT.t[:], mk.t[:], ALU.mult), reads=[pT.r, mk.r], writes=[pT.r])
                return pT

            def emit_PV(stp, pT):
                nonlocal io
                bi, qi, g, ci, kc, mk, nch = stp
                if ci == 0:
                    pof = ps_o[io % 4]; io += 1
                    po = Tl(pof.t[:, 0:260].rearrange("p (h d) -> p h d", h=4))
                    po.rs = pof.rs
                    state["po"] = po
                po = state["po"]
                for hh in range(4):
                    S.op("pe", lambda e: e.matmul(po.t[:, hh, :], lhsT=pT.t[:, hh * 128:(hh + 1) * 128],
                                                  rhs=vA.t[:, kc, g, :], start=(ci == 0 and hh == 0),
                                                  stop=(ci == nch - 1 and hh == 3)),
                         reads=[pT.r, vA.r], writes=[po.r], inc=(hh == 3))
                if ci == nch - 1:
                    y = yt[bi % 2]
                    dn = den[g]
                    S.op("dve", lambda e: e.tensor_tensor(dn.t[:], po.t[:, :, 64], sk.t[:, g * 4:(g + 1) * 4], ALU.add),
                         reads=[po.r, sk.r], writes=[dn.r])
                    S.op("dve", lambda e: e.reciprocal(dn.t[:], dn.t[:]), reads=[dn.r], writes=[dn.r])
                    S.op("dve", lambda e: e.tensor_tensor(y.t[:, g * 256:(g + 1) * 256].rearrange("p (h d) -> p h d", h=4),
                                                          po.t[:, :, 0:64],
                                                          dn.t[:, :].unsqueeze(2).broadcast_to([128, 4, 64]), ALU.mult),
                         reads=[po.r, dn.r], writes=[y.r])
                    if g == 1:
                        finish_block(bi, qi, y)

            def finish_block(bi, qi, y):
                for pair in range(4):
                    S.op("pe", lambda e: e.transpose(ptr.t[:, pair, :], y.t[:, pair * 128:(pair + 1) * 128], ident.t[:]),
                         reads=[y.r, ident.r], writes=[ptr.r], inc=(pair == 3))
                yo = ytT[bi % 2]
                S.op("act", lambda e: e.copy(yo.t[:], ptr.t[:]), reads=[ptr.r], writes=[yo.r])
                S.dma(self.yT[512:1024, qi * 128:(qi + 1) * 128].rearrange("(c p) t -> p c t", p=128), yo.t[:],
                      reads=[yo.r], writes=[self.yT.r(("att", qi))], q="pool")

            LOOK = 2
            pend = []
            for i, stp in enumerate(steps):
                pend.append((stp, emit_S(stp)))
                if len(pend) > LOOK:
                    emit_PV(*pend.pop(0))
            while pend:
                emit_PV(*pend.pop(0))

    def phase_outproj(self, l, need_ctx, loader=None):
        S, nc, w = self.S, self.nc, self.w
        allk = lambda d: [d.r(k) for k in d.res]
        GT = 512
        with ExitStack() as st:
            wob = self.sb(st, [128, 8, D], BF16, name="wob")
            stg = [self.sb(st, [128, 1024], name="stgo") for _ in range(2)]
            wsrc = w["w_out"][l].rearrange("(k p) n -> p k n", p=128)
            for k in range(8):
                self.cast_weight(wob.t[:, k, :], wob.r, wsrc[:, k, :], stg, k, D)
            Ga = self.sb(st, [128, D])
            yt = [self.sb(st, [128, 8, GT], BF16, name="yto") for _ in range(2)]
            xt = [self.sb(st, [128, D], name="xto") for _ in range(3)]
            junk = self.sb(st, [128, D], name="junko")
            po = [self.ps(st, [128, 512], name="poo") for _ in range(4)]
            groups = [(g * GT, 4) for g in range(L // GT)]
            if need_ctx:
                groups.append((L, 2))
            cur_which = None
            ix = 0; ipo = 0
            yres = allk(self.yT)
            for gi, (t0, ntile) in enumerate(groups):
                n = ntile * 128
                which = 0 if t0 < L else 1
                if which != cur_which:
                    S.dma(Ga.t[:], self.modD[l, which:which + 1, 5 * D:6 * D].broadcast_to([128, D]),
                          reads=[self.modD.r(l)], writes=[Ga.r])
                    cur_which = which
                y = yt[gi % 2]
                S.dma(y.t[:, :, 0:n], self.yT[:, t0:t0 + n].rearrange("(k p) t -> p k t", p=128), reads=yres, writes=[y.r])
                for i in range(ntile):
                    r0 = t0 + i * 128
                    x = xt[ix % 3]; ix += 1
                    if loader is not None:
                        for _ in range(3):
                            next(loader, None)
                    S.dma(x.t[:], self.xs[r0:r0 + 128, :], reads=[self.xs.r(r0 // 128)], writes=[x.r], q="pool")
                    for dh in range(2):
                        pp = po[ipo % 4]; ipo += 1
                        for k in range(8):
                            S.op("pe", lambda e: e.matmul(pp.t[:], lhsT=y.t[:, k, i * 128:(i + 1) * 128],
                                                          rhs=wob.t[:, k, dh * 512:(dh + 1) * 512], start=(k == 0), stop=(k == 7)),
                                 reads=[y.r, wob.r], writes=[pp.r], inc=(k == 7))
                        S.op("dve", lambda e: e.tensor_tensor(junk.t[:, dh * 512:(dh + 1) * 512], pp.t[:],
                                                              Ga.t[:, dh * 512:(dh + 1) * 512], ALU.mult),
                             reads=[pp.r, Ga.r], writes=[junk.r])
                        S.op("pool", lambda e: e.tensor_tensor(x.t[:, dh * 512:(dh + 1) * 512], x.t[:, dh * 512:(dh + 1) * 512],
                                                               junk.t[:, dh * 512:(dh + 1) * 512], ALU.add),
                             reads=[junk.r, x.r], writes=[x.r])
                    S.dma(self.xs[r0:r0 + 128, :], x.t[:], reads=[x.r], writes=[self.xs.r(r0 // 128)], q="pool")

    def phase_hyena(self, l, need_ctx):
        S, nc, w = self.S, self.nc, self.w
        allk = lambda d: [d.r(k) for k in d.res]
        PI = math.pi
        NB = 12
        NG = 4
        with ExitStack() as st:
            banks = [self.ps(st, [128, 512], name="hb") for _ in range(8)]
            bk = [0]
            def bank():
                b = banks[bk[0] % 8]; bk[0] += 1
                return b
            def ld(name, shape):
                t = self.sb(st, shape, name=name)
                S.dma(t.t[:], w[name].h.ap(), writes=[t.r])
                return t
            WA = ld("f_WA", [128, 130]); TWr = ld("f_TWr", [64, 65]); TWi = ld("f_TWi", [64, 65])
            C64 = ld("f_C64", [64, 64]); S64 = ld("f_S64", [64, 64]); nS64 = ld("f_nS64", [64, 64])
            CS1 = ld("f_CS1", [64, 128]); CS2 = ld("f_CS2", [64, 128])
            TWir = ld("f_TWir", [65, 64]); TWii = ld("f_TWii", [65, 64])
            Gr = ld("f_Gr", [65, 64]); nGi = ld("f_nGi", [65, 64])
            kext = [self.sb(st, [128, 512], name="kext") for _ in range(2)]
            usb = [self.sb(st, [128, T], name="usb") for _ in range(2)]
            x0sb = [self.sb(st, [128, T], name="x0sb") for _ in range(2)]
            skipc = self.sb(st, [128, 2])
            for cc in range(2):
                self.vec_col(skipc.t[:, cc:cc + 1], w["hy_skip"][l, cc * 128:(cc + 1) * 128], skipc.r)

            with ExitStack() as stm:
                fw0 = self.sb(stm, [33, 64]); S.dma(fw0.t[:], w["hy_fw0"][l], writes=[fw0.r])
                fwi = [self.sb(stm, [64, 64]) for _ in range(2)]
                for j in range(2):
                    S.dma(fwi[j].t[:], w["hy_fw_in"][l, j], writes=[fwi[j].r])
                fwl = self.sb(stm, [64, 512]); S.dma(fwl.t[:], w["hy_fw_last"][l], writes=[fwl.r])
                freq = self.sb(stm, [64, 1]); self.vec_col(freq.t[:], w["hy_freq"][l], freq.r)
                fb = self.sb(stm, [64, 3])
                self.vec_col(fb.t[:, 0:1], w["hy_fb0"][l], fb.r)
                self.vec_col(fb.t[:, 1:2], w["hy_fb_in"][l, 0], fb.r)
                self.vec_col(fb.t[:, 2:3], w["hy_fb_in"][l, 1], fb.r)
                S.op("dve", lambda e: e.tensor_scalar_mul(fb.t[:], fb.t[:], freq.t[:, 0:1]),
                     reads=[fb.r, freq.r], writes=[fb.r])
                zt = [self.sb(stm, [33, 512], name="zt") for _ in range(NG)]
                hA = [self.sb(stm, [64, 512], name="hA") for _ in range(NG)]
                hB = [self.sb(stm, [64, 512], name="hB") for _ in range(NG)]
                kq = [self.sb(stm, [64, 512], mybir.dt.int32, name="kq") for _ in range(NG)]
                dect = [self.sb(stm, [128, 512], name="dect") for _ in range(4)]
                kout = [self.sb(stm, [128, 512], name="kout") for _ in range(4)]
                glist = [("lat", g) for g in range(NFFT // 512)] + ([("ctx", 0)] if need_ctx else [])
                wts = [fw0, fwi[0], fwi[1]]
                io_ = 0
                for c0 in range(0, len(glist), NG):
                    grp = glist[c0:c0 + NG]
                    for gi, (kind, g) in enumerate(grp):
                        zsrc = w["hy_z"][:, g * 512:(g + 1) * 512] if kind == "lat" else w["hy_zc"].h.ap()
                        S.dma(zt[gi].t[:], zsrc, writes=[zt[gi].r])
                    hcur = [None] * len(grp)
                    for j in range(3):
                        pms = []
                        for gi in range(len(grp)):
                            pm = bank()
                            rhs = zt[gi].t[:] if j == 0 else hcur[gi].t[:]
                            rres = zt[gi].r if j == 0 else hcur[gi].r
                            S.op("pe", lambda e: e.matmul(pm.t[0:64, :], lhsT=wts[j].t[:], rhs=rhs, start=True, stop=True),
                                 reads=[wts[j].r, rres], writes=[pm.r])
                            pms.append(pm)
                        for gi in range(len(grp)):
                            hn = (hA if j % 2 == 0 else hB)[gi]
                            pm = pms[gi]
                            S.op("act", lambda e: e.activation(hn.t[:], pm.t[0:64, :], AF.Identity, bias=fb.t[:, j:j + 1], scale=freq.t[:, 0:1]),
                                 reads=[pm.r, freq.r, fb.r], writes=[hn.r])
                            S.op("dve", lambda e: e.tensor_scalar_mul(kq[gi].t[:], hn.t[:], 1.0 / (2.0 * PI)),
                                 reads=[hn.r], writes=[kq[gi].r])
                            S.op("dve", lambda e: e.scalar_tensor_tensor(hn.t[:], kq[gi].t[:], -2.0 * PI, hn.t[:], ALU.mult, ALU.add),
                                 reads=[hn.r, kq[gi].r], writes=[hn.r])
                            S.op("dve", lambda e: e.tensor_scalar(hn.t[:], hn.t[:], -PI, PI, ALU.max, ALU.min),
                                 reads=[hn.r], writes=[hn.r])
                            S.op("act", lambda e: e.activation(hn.t[:], hn.t[:], AF.Sin), reads=[hn.r], writes=[hn.r])
                            hcur[gi] = hn
                    for gi, (kind, g) in enumerate(grp):
                        h = hcur[gi]
                        for cch in range(2):
                            dt_ = dect[io_ % 4]; ko = kout[io_ % 4]; io_ += 1
                            if kind == "lat":
                                wsel = 0 if g < 8 else 1
                                pk = bank()
                                S.op("pe", lambda e: e.matmul(pk.t[:], lhsT=fwl.t[:, wsel * 256 + cch * 128: wsel * 256 + (cch + 1) * 128],
                                                              rhs=h.t[:], start=True, stop=True), reads=[fwl.r, h.r], writes=[pk.r])
                                S.dma(dt_.t[:], w["hy_dec"][cch * 128:(cch + 1) * 128, g * 512:(g + 1) * 512], writes=[dt_.r], q="pool")
                                S.op("dve", lambda e: e.tensor_tensor(ko.t[:], pk.t[:], dt_.t[:], ALU.mult), reads=[pk.r, dt_.r], writes=[ko.r])
                                S.dma(self.kfT[cch * 128:(cch + 1) * 128, g * 512:(g + 1) * 512], ko.t[:], reads=[ko.r],
                                      writes=[self.kfT.r((cch, g))], q="pool")
                            else:
                                S.dma(dt_.t[:], w["hy_decc"][cch * 128:(cch + 1) * 128, :], writes=[dt_.r], q="pool")
                                for wsel, (a_, b_) in ((1, (0, 255)), (0, (255, 511))):
                                    pk = bank()
                                    S.op("pe", lambda e: e.matmul(pk.t[:], lhsT=fwl.t[:, wsel * 256 + cch * 128: wsel * 256 + (cch + 1) * 128],
                                                                  rhs=h.t[:], start=True, stop=True), reads=[fwl.r, h.r], writes=[pk.r])
                                    S.op("dve", lambda e: e.tensor_tensor(kext[cch].t[:, a_:b_], pk.t[:, a_:b_], dt_.t[:, a_:b_], ALU.mult),
                                         reads=[pk.r, dt_.r], writes=[kext[cch].r])
                S.barrier()

            pres = allk(self.projT)
            with ExitStack() as st2:
                raws = [self.sb(st2, [128, T], name="raw") for _ in range(2)]
                x1c = self.sb(st2, [128, T], name="x1c")
                vc = self.sb(st2, [128, T], name="vc")
                cws = [self.sb(st2, [128, 3]) for _ in range(2)]; cbs = [self.sb(st2, [128, 1]) for _ in range(2)]
                ir = 0
                for cc in range(2):
                    for part, dst in ((0, x0sb[cc]), (1, x1c), (2, vc)):
                        raw = raws[ir % 2]; cw = cws[ir % 2]; cb = cbs[ir % 2]; ir += 1
                        ch0 = part * 256 + cc * 128
                        S.dma(raw.t[:], self.projT[512 + ch0:512 + ch0 + 128, :], reads=pres, writes=[raw.r])
                        S.dma(cw.t[:], w["hy_conv_w"][l, :, ch0:ch0 + 128].rearrange("k p -> p k"), writes=[cw.r],
                              allow_slow_non_contiguous=True)
                        self.vec_col(cb.t[:], w["hy_conv_b"][l, ch0:ch0 + 128], cb.r)
                        for (s0, s1) in ((0, L), (L, T)):
                            S.op("act", lambda e: e.activation(dst.t[:, s0:s1], raw.t[:, s0:s1], AF.Identity, bias=cb.t[:], scale=cw.t[:, 1:2]),
                                 reads=[raw.r, cw.r, cb.r], writes=[dst.r])
                            o_ = dst.t[:, s0 + 1:s1]; i_ = raw.t[:, s0:s1 - 1]
                            S.op("dve", lambda e: e.scalar_tensor_tensor(o_, i_, cw.t[:, 0:1], o_, ALU.mult, ALU.add),
                                 reads=[raw.r, cw.r, dst.r], writes=[dst.r])
                            o_ = dst.t[:, s0:s1 - 1]; i_ = raw.t[:, s0 + 1:s1]
                            S.op("dve", lambda e: e.scalar_tensor_tensor(o_, i_, cw.t[:, 2:3], o_, ALU.mult, ALU.add),
                                 reads=[raw.r, cw.r, dst.r], writes=[dst.r])
                    S.op("pool", lambda e: e.tensor_tensor(usb[cc].t[:], x1c.t[:], vc.t[:], ALU.mult), reads=[x1c.r, vc.r], writes=[usb[cc].r])
                    S.dma(self.uT[cc * 128:(cc + 1) * 128, :], usb[cc].t[:], reads=[usb[cc].r], writes=[self.uT.r(cc)])
                S.barrier()

            Bre = self.sb(st, [64, NB, 65], name="Bre"); Bim = self.sb(st, [64, NB, 65], name="Bim")
            tF = [[self.sb(st, [65, 400], name="tF") for _ in range(4)] for _ in range(2)]
            tI = [[self.sb(st, [65, 400], name="tI") for _ in range(4)] for _ in range(2)]
            tcn = {"F": 0, "I": 0}
            Ut = [self.sb(st, [128, NB, 64], name="Ut") for _ in range(2)]
            Kres = [self.sb(st, [64, NB, 65], name="Kre") for _ in range(2)]
            Kims = [self.sb(st, [64, NB, 65], name="Kim") for _ in range(2)]
            Yres = [self.sb(st, [64, NB, 65], name="Yre") for _ in range(2)]
            Yims = [self.sb(st, [64, NB, 65], name="Yim") for _ in range(2)]
            Bpre = self.sb(st, [65, NB, 64], name="Bpre"); Bpim = self.sb(st, [65, NB, 64], name="Bpim")
            ycs = [self.sb(st, [64, NB, 64], name="ycs") for _ in range(2)]
            xo = [self.sb(st, [64, 512], name="xo") for _ in range(4)]

            def cmul(kind, shape_view, ar, ai, br, bi, outr, outi, rres, wres_r, wres_i):
                pool_ = tF if kind == "F" else tI
                tt = pool_[tcn[kind] % 2]; tcn[kind] += 1
                t1, t2, t3, t4 = [shape_view(t) for t in tt]
                S.op("dve", lambda e: e.tensor_tensor(t1, ar, br, ALU.mult), reads=rres, writes=[tt[0].r])
                S.op("dve", lambda e: e.tensor_tensor(t2, ai, bi, ALU.mult), reads=rres, writes=[tt[1].r])
                S.op("dve", lambda e: e.tensor_tensor(t3, ar, bi, ALU.mult), reads=rres, writes=[tt[2].r])
                S.op("dve", lambda e: e.tensor_tensor(t4, ai, br, ALU.mult), reads=rres, writes=[tt[3].r])
                S.op("pool", lambda e: e.tensor_tensor(outr, t1, t2, ALU.subtract), reads=[tt[0].r, tt[1].r], writes=[wres_r])
                S.op("pool", lambda e: e.tensor_tensor(outi, t3, t4, ALU.add), reads=[tt[2].r, tt[3].r], writes=[wres_i])

            def fwd(U, K, nb):
                for sub in range(0, nb, 3):
                    ns = min(3, nb - sub)
                    pa = bank()
                    for j in range(ns):
                        S.op("pe", lambda e: e.matmul(pa.t[0:64, j * 130:(j + 1) * 130], lhsT=U.t[0:K, sub + j, :], rhs=WA.t[0:K, :],
                                                      start=True, stop=True), reads=[U.r, WA.r], writes=[pa.r], inc=(j == ns - 1))
                    pv_ = pa.t[0:64, 0:ns * 130].rearrange("p (c f) -> p c f", c=ns)
                    tw = lambda t: t.t[:, :].unsqueeze(1).broadcast_to([64, ns, 65])
                    sv = lambda t: t.t[0:64, 0:ns * 65].rearrange("p (c f) -> p c f", c=ns)
                    cmul("F", sv, pv_[:, :, 0:65], pv_[:, :, 65:130], tw(TWr), tw(TWi),
                         Bre.t[:, sub:sub + ns, :], Bim.t[:, sub:sub + ns, :], [pa.r, TWr.r, TWi.r], Bre.r, Bim.r)
                outs = []
                for grp in range(0, nb, 6):
                    ng = min(6, nb - grp)
                    ncol = ng * 65
                    pxr = bank(); pxi = bank()
                    br_ = Bre.t[:, grp:grp + ng, :].rearrange("p c f -> p (c f)")
                    bi_ = Bim.t[:, grp:grp + ng, :].rearrange("p c f -> p (c f)")
                    S.op("pe", lambda e: e.matmul(pxr.t[0:64, 0:ncol], lhsT=C64.t[:], rhs=br_, start=True, stop=False), reads=[C64.r, Bre.r], writes=[pxr.r], inc=False)
                    S.op("pe", lambda e: e.matmul(pxr.t[0:64, 0:ncol], lhsT=S64.t[:], rhs=bi_, start=False, stop=True), reads=[S64.r, Bim.r], writes=[pxr.r])
                    S.op("pe", lambda e: e.matmul(pxi.t[0:64, 0:ncol], lhsT=C64.t[:], rhs=bi_, start=True, stop=False), reads=[C64.r, Bim.r], writes=[pxi.r], inc=False)
                    S.op("pe", lambda e: e.matmul(pxi.t[0:64, 0:ncol], lhsT=nS64.t[:], rhs=br_, start=False, stop=True), reads=[nS64.r, Bre.r], writes=[pxi.r])
                    outs.append((grp, ng, pxr, pxi))
                return outs

            batches = [(c0, min(NB, 256 - c0)) for c0 in range(0, 256, NB)]
            kres = allk(self.kfT)
            ixo = 0
            for bi_, (c0, nb) in enumerate(batches):
                U = Ut[bi_ % 2]
                S.dma(U.t[:, 0:nb, :], self.kfT[c0:c0 + nb, :].rearrange("c (a b) -> a c b", b=64), reads=kres, writes=[U.r])
                for grp, ng, pxr, pxi in fwd(U, 128, nb):
                    ncol = ng * 65
                    for which, px in ((0, pxr), (1, pxi)):
                        o = xo[ixo % 4]; ixo += 1
                        S.op("act", lambda e: e.copy(o.t[:, 0:ncol], px.t[0:64, 0:ncol]), reads=[px.r], writes=[o.r])
                        S.dma(self.KfD[which, :, c0 + grp:c0 + grp + ng, :], o.t[:, 0:ncol].rearrange("p (c f) -> p c f", c=ng),
                              reads=[o.r], writes=[self.KfD.r((which, c0 + grp))], q="pool")
            S.barrier()
            ures = allk(self.uT)
            kfres = allk(self.KfD)

            def fwd_u(bi_):
                c0, nb = batches[bi_]
                U = Ut[bi_ % 2]; Kre = Kres[bi_ % 2]; Kim = Kims[bi_ % 2]; Yre = Yres[bi_ % 2]; Yim = Yims[bi_ % 2]
                S.dma(U.t[0:64, 0:nb, :], self.uT[c0:c0 + nb, 0:L].rearrange("c (a b) -> a c b", b=64), reads=ures, writes=[U.r])
                S.dma(Kre.t[:, 0:nb, :], self.KfD[0, :, c0:c0 + nb, :], reads=kfres, writes=[Kre.r], q="pool")
                S.dma(Kim.t[:, 0:nb, :], self.KfD[1, :, c0:c0 + nb, :], reads=kfres, writes=[Kim.r], q="pool")
                for grp, ng, pxr, pxi in fwd(U, 64, nb):
                    ncol = ng * 65
                    sv = lambda t: t.t[0:64, 0:ncol]
                    fl = lambda t: t.t[:, grp:grp + ng, :].rearrange("p c f -> p (c f)")
                    cmul("F", sv, pxr.t[0:64, 0:ncol], pxi.t[0:64, 0:ncol], fl(Kre), fl(Kim), fl(Yre), fl(Yim),
                         [pxr.r, pxi.r, Kre.r, Kim.r], Yre.r, Yim.r)

            def inv_u(bi_):
                c0, nb = batches[bi_]
                Yre = Yres[bi_ % 2]; Yim = Yims[bi_ % 2]
                for sub in range(0, nb, 4):
                    ns = min(4, nb - sub)
                    pd = bank()
                    for j in range(ns):
                        S.op("pe", lambda e: e.matmul(pd.t[0:65, j * 128:(j + 1) * 128], lhsT=Yre.t[:, sub + j, :], rhs=CS1.t[:],
                                                      start=True, stop=False), reads=[Yre.r, CS1.r], writes=[pd.r], inc=False)
                        S.op("pe", lambda e: e.matmul(pd.t[0:65, j * 128:(j + 1) * 128], lhsT=Yim.t[:, sub + j, :], rhs=CS2.t[:],
                                                      start=False, stop=True), reads=[Yim.r, CS2.r], writes=[pd.r], inc=(j == ns - 1))
                    pv_ = pd.t[0:65, 0:ns * 128].rearrange("p (c r n) -> p c r n", c=ns, r=2)
                    tw = lambda t: t.t[:, :].unsqueeze(1).broadcast_to([65, ns, 64])
                    sv = lambda t: t.t[0:65, 0:ns * 64].rearrange("p (c n) -> p c n", c=ns)
                    cmul("I", sv, pv_[:, :, 0, :], pv_[:, :, 1, :], tw(TWir), tw(TWii),
                         Bpre.t[:, sub:sub + ns, :], Bpim.t[:, sub:sub + ns, :], [pd.r, TWir.r, TWii.r], Bpre.r, Bpim.r)
                yo = ycs[bi_ % 2]
                for grp in range(0, nb, 8):
                    ng = min(8, nb - grp)
                    ncol = ng * 64
                    py = bank()
                    S.op("pe", lambda e: e.matmul(py.t[0:64, 0:ncol], lhsT=Gr.t[:], rhs=Bpre.t[:, grp:grp + ng, :].rearrange("p c n -> p (c n)"),
                                                  start=True, stop=False), reads=[Gr.r, Bpre.r], writes=[py.r], inc=False)
                    S.op("pe", lambda e: e.matmul(py.t[0:64, 0:ncol], lhsT=nGi.t[:], rhs=Bpim.t[:, grp:grp + ng, :].rearrange("p c n -> p (c n)"),
                                                  start=False, stop=True), reads=[nGi.r, Bpim.r], writes=[py.r])
                    S.op("act", lambda e: e.copy(yo.t[:, grp:grp + ng, :].rearrange("p c n -> p (c n)"), py.t[0:64, 0:ncol]),
                         reads=[py.r], writes=[yo.r])
                S.dma(self.ycT[c0:c0 + nb, :].rearrange("c (a b) -> a c b", b=64), yo.t[:, 0:nb, :], reads=[yo.r],
                      writes=[self.ycT.r(c0)], q="pool")

            fwd_u(0)
            for bi_ in range(len(batches)):
                if bi_ + 1 < len(batches):
                    fwd_u(bi_ + 1)
                inv_u(bi_)
            S.barrier()
            ycres = allk(self.ycT)
            ych = self.sb(st, [128, T], name="ych")
            ybf = self.sb(st, [128, T], BF16, name="ybfh")
            for cc in range(2):
                S.dma(ych.t[:, 0:L], self.ycT[cc * 128:(cc + 1) * 128, :], reads=ycres, writes=[ych.r])
                if need_ctx:
                    acc = [self.sb(st, [128, C], name="acc") for _ in range(2)]
                    for s_ in range(C):
                        a = acc[s_ % 2]
                        ks = kext[cc].t[:, 255 - s_:511 - s_]
                        us = usb[cc].t[:, L + s_:L + s_ + 1]
                        if s_ < 2:
                            S.op("dve", lambda e: e.tensor_scalar_mul(a.t[:], ks, us), reads=[kext[cc].r, usb[cc].r], writes=[a.r])
                        else:
                            S.op("dve", lambda e: e.scalar_tensor_tensor(a.t[:], ks, us, a.t[:], ALU.mult, ALU.add),
                                 reads=[kext[cc].r, usb[cc].r, a.r], writes=[a.r])
                    S.op("pool", lambda e: e.tensor_tensor(ych.t[:, L:T], acc[0].t[:], acc[1].t[:], ALU.add),
                         reads=[acc[0].r, acc[1].r], writes=[ych.r])
                else:
                    S.op("pool", lambda e: e.memset(ych.t[:, L:T], 0.0), writes=[ych.r])
                S.op("dve", lambda e: e.scalar_tensor_tensor(ych.t[:], usb[cc].t[:], skipc.t[:, cc:cc + 1], ych.t[:], ALU.mult, ALU.add),
                     reads=[usb[cc].r, skipc.r, ych.r], writes=[ych.r])
                S.op("pool", lambda e: e.tensor_tensor(ybf.t[:], ych.t[:], x0sb[cc].t[:], ALU.mult), reads=[ych.r, x0sb[cc].r], writes=[ybf.r])
                S.dma(self.yT[256 + cc * 128:256 + (cc + 1) * 128, :], ybf.t[:], reads=[ybf.r], writes=[self.yT.r(("hy", cc))])

    def phase_final(self):
        S, w = self.S, self.w
        with ExitStack() as st:
            G = self.sb(st, [128, D])
            S.dma(G.t[:], w["final_g"].h.ap().rearrange("(o n) -> o n", o=1).broadcast_to([128, D]), writes=[G.r])
            xt = [self.sb(st, [128, D]) for _ in range(3)]
            junk = self.sb(st, [128, D])
            ss = [self.sb(st, [128, 1]) for _ in range(3)]
            epst = self.sb(st, [128, 1])
            S.op("pool", lambda e: e.memset(epst.t[:], EPS), writes=[epst.r])
            for i in range(L // 128):
                x = xt[i % 3]; s = ss[i % 3]
                S.dma(x.t[:], self.xs[i * 128:(i + 1) * 128, :], reads=[self.xs.r(i)], writes=[x.r])
                S.op("act", lambda e: e.activation(junk.t[:], x.t[:], AF.Square, accum_out=s.t[:]), reads=[x.r], writes=[junk.r, s.r])
                S.op("act", lambda e: e.activation(s.t[:], s.t[:], AF.Sqrt, bias=epst.t[:], scale=1.0 / D), reads=[s.r, epst.r], writes=[s.r])
                S.op("dve", lambda e: e.reciprocal(s.t[:], s.t[:]), reads=[s.r], writes=[s.r])
                S.op("dve", lambda e: e.scalar_tensor_tensor(x.t[:], x.t[:], s.t[:, 0:1], G.t[:], ALU.mult, ALU.mult),
                     reads=[x.r, s.r, G.r], writes=[x.r])
                S.dma(self.out[i * 128:(i + 1) * 128, :], x.t[:], reads=[x.r], writes=[self.out.r(i)], q="pool")

    def phase_outproj_ffn2(self, l, need_ctx):
        with ExitStack() as ow:
            w1b = self.sb(ow, [128, 8, 2 * DFF], BF16, name="w1b")
            w2b = self.sb(ow, [128, NF, D], BF16, name="w2b")
            with ExitStack() as o2:
                stg = [self.sb(o2, [128, 1024], name="stgp") for _ in range(2)]
                loader = self.ffn_weight_loader(l, 1, w1b, w2b, stg, act_only=True)
                self.phase_outproj(l, need_ctx, loader)
                for _ in loader:
                    pass
                self.S.barrier()
            self.phase_ffn(l, 1, first=False, last_layer_ctx_skip=not need_ctx, weights=(w1b, w2b))

    def build(self, consts):
        self.declare(consts)
        with self.es as es:
            self.S = Sched(self.nc, es)
            upto = self.upto
            self.phase_mod()
            self.S.barrier()
            done = False
            for l in range(self.nlayers):
                need_ctx = l < DEPTH - 1
                steps = [("ffn1", lambda: self.phase_ffn(l, 0, first=(l == 0))),
                         ("inproj", lambda: self.phase_inproj(l)),
                         ("lru", lambda: self.phase_lru(l)),
                         ("attn", lambda: self.phase_attn(l, need_ctx)),
                         ("hyena", lambda: self.phase_hyena(l, need_ctx)),
                         ("outffn2", lambda: self.phase_outproj_ffn2(l, need_ctx))]
                for nm, fn in steps:
                    if l == self.nlayers - 1 and upto in ("attn", "hyena") and nm in ("lru", "attn", "hyena") and nm != upto:
                        continue
                    fn()
                    self.S.barrier()
                    if l == self.nlayers - 1 and upto == nm:
                        done = True
                        break
                if done:
                    break
            if not done:
                self.phase_final()
            self.S.finish()
        return self.nc


_CST = None


def _get_consts():
    global _CST
    if _CST is None:
        _CST = _consts()
    return _CST


def make_in_maps(inputs, ncores=8):
    cst = _get_consts()
    maps = []
    shared = {k: np.ascontiguousarray(np.asarray(v, dtype=np.float32)) for k, v in inputs.items()
              if k not in ("x", "c", "ctx")}
    for b in range(ncores):
        m = dict(shared)
        m["x"] = np.ascontiguousarray(inputs["x"][b], dtype=np.float32)
        m["ctx"] = np.ascontiguousarray(inputs["ctx"][b], dtype=np.float32)
        m["c"] = np.ascontiguousarray(inputs["c"][b], dtype=np.float32)
        for k, v in cst.items():
            m["k_" + k] = v
        maps.append(m)
    return maps


def kernel(**inputs):
    cst = _get_consts()
    bld = Builder()
    nc = bld.build(cst)
    maps = make_in_maps(inputs, 8)
    res = run_bass_kernel_spmd(nc, maps, core_ids=list(range(8)))
    return np.stack([np.asarray(r["out"], dtype=np.float32) for r in res.results], axis=0)
```
